# Optimizing a Trainium2 kernel written in Bass

```python
import jax, jax.numpy as jnp
from jax import lax
import numpy as np

D_MODEL = 1024
BATCH = 8
SEQ = 2048
DEPTH = 1
DEC_BATCH = 4
DEC_SEQ = 8192
PAST_LEN = 128

D_MIX = D_MODEL
D_A = D_MIX // 2
N_HEADS_A = 4
HEAD_A = D_A // N_HEADS_A
CHUNK_A = 128
D_B = D_MIX - D_A
N_HEADS_B = 4
HEAD_V_B = D_B // N_HEADS_B
D_K_B = D_B // 2
HEAD_K_B = D_K_B // N_HEADS_B
GATE_RANK = 16
GATE_NORMALIZER = 16.0
CHUNK_B = 64
EPS = 1e-6
SPLITS = [D_A, D_A, D_A, D_K_B, D_K_B, D_B, D_B, GATE_RANK, GATE_RANK]
D_IN = sum(SPLITS)

kernel_name = "hybrid_gmlp_gla_bidir_encoder"


def rmsnorm(x, g):
    xf = x.astype(jnp.float32)
    xf = xf * lax.rsqrt(jnp.mean(xf * xf, axis=-1, keepdims=True) + EPS)
    return (xf * g.astype(jnp.float32)).astype(x.dtype)


def layernorm(x, g):
    xf = x.astype(jnp.float32)
    xf = xf - jnp.mean(xf, axis=-1, keepdims=True)
    xf = xf * lax.rsqrt(jnp.mean(xf * xf, axis=-1, keepdims=True) + EPS)
    return (xf * g.astype(jnp.float32)).astype(x.dtype)


def spatial_gating(u, v, w_sp, b_sp, g_v):
    bsz, s, _ = u.shape
    v = layernorm(v, g_v)
    vc = v.reshape(bsz, s // CHUNK_A, CHUNK_A, N_HEADS_A, HEAD_A)
    sv = jnp.einsum('hij,bcjhd->bcihd', w_sp, vc) + b_sp.T[None, None, :, :, None]
    return u * sv.reshape(bsz, s, D_A).astype(u.dtype)


def gla_direction(q, k, v, g, strict):
    bsz, s, h, dk = q.shape
    dv = v.shape[-1]
    n = s // CHUNK_B
    f32 = jnp.float32
    q = q.astype(f32).reshape(bsz, n, CHUNK_B, h, dk)
    k = k.astype(f32).reshape(bsz, n, CHUNK_B, h, dk)
    v = v.astype(f32).reshape(bsz, n, CHUNK_B, h, dv)
    g = g.astype(f32).reshape(bsz, n, CHUNK_B, h, dk)
    bcum = jnp.cumsum(g, axis=2)
    b_last = bcum[:, :, -1:]
    q_in = q * jnp.exp(bcum)
    k_in = k * jnp.exp(-bcum)
    scores = jnp.einsum('bnchd,bnjhd->bnhcj', q_in, k_in)
    idx = jnp.arange(CHUNK_B)
    mask = (idx[:, None] > idx[None, :]) if strict else (idx[:, None] >= idx[None, :])
    scores = jnp.where(mask, scores, 0.0)
    o_intra = jnp.einsum('bnhcj,bnjhe->bnche', scores, v)
    k_dec = k * jnp.exp(b_last - bcum)
    chunk_kv = jnp.einsum('bnjhd,bnjhe->bnhde', k_dec, v)
    decay = jnp.exp(b_last[:, :, 0])

    def step(state, inp):
        kv_c, dec_c = inp
        return dec_c[..., None] * state + kv_c, state

    init = jnp.zeros((bsz, h, dk, dv), f32)
    _, states = lax.scan(step, init, (jnp.moveaxis(chunk_kv, 1, 0), jnp.moveaxis(decay, 1, 0)))
    states = jnp.moveaxis(states, 0, 1)
    o_inter = jnp.einsum('bnchd,bnhde->bnche', q_in, states)
    return (o_intra + o_inter).reshape(bsz, s, h, dv)


def hybrid_layer(x, norm_pre, w_in, w_sp, b_sp, g_v_a, w_gk_fwd, b_gk_fwd,
                 w_gk_bwd, b_gk_bwd, g_norm_b, w_out, norm_post):
    bsz, s, _ = x.shape
    h = rmsnorm(x, norm_pre)
    p = h @ w_in
    offs = np.cumsum(SPLITS)[:-1].tolist()
    u_a, v_a, z_a, q_b, k_b, v_b, z_b, lr_f, lr_b = jnp.split(p, offs, axis=-1)

    out_a = spatial_gating(jax.nn.gelu(u_a), jax.nn.gelu(v_a), w_sp, b_sp, g_v_a)
    out_a = out_a * jax.nn.silu(z_a)

    q = q_b.reshape(bsz, s, N_HEADS_B, HEAD_K_B) * (HEAD_K_B ** -0.5)
    k = k_b.reshape(bsz, s, N_HEADS_B, HEAD_K_B)
    v = v_b.reshape(bsz, s, N_HEADS_B, HEAD_V_B)
    g_f = jax.nn.log_sigmoid((lr_f @ w_gk_fwd + b_gk_fwd).astype(jnp.float32)) / GATE_NORMALIZER
    g_b = jax.nn.log_sigmoid((lr_b @ w_gk_bwd + b_gk_bwd).astype(jnp.float32)) / GATE_NORMALIZER
    g_f = g_f.reshape(bsz, s, N_HEADS_B, HEAD_K_B)
    g_b = g_b.reshape(bsz, s, N_HEADS_B, HEAD_K_B)
    o_fwd = gla_direction(q, k, v, g_f, strict=False)
    flip = lambda t: jnp.flip(t, axis=1)
    o_bwd = flip(gla_direction(flip(q), flip(k), flip(v), flip(g_b), strict=True))
    o_b = rmsnorm((o_fwd + o_bwd).astype(x.dtype), g_norm_b)
    out_b = o_b.reshape(bsz, s, D_B) * jax.nn.silu(z_b)

    mixed = jnp.concatenate([out_a, out_b], axis=-1) @ w_out
    return x + rmsnorm(mixed, norm_post)


def setup_inputs(seed: int = 0) -> dict:
    key = jax.random.key(seed)
    ks = jax.random.split(key, 16)
    f32 = jnp.float32
    nrm = lambda k, shape, scale: jax.random.normal(k, shape, f32) * scale
    return {
        "x_prompt": nrm(ks[0], (BATCH, SEQ, D_MODEL), 1.0),
        "x_sample": nrm(ks[1], (DEC_BATCH, DEC_SEQ, D_MODEL), 1.0),
        "norm_pre": 1.0 + nrm(ks[2], (DEPTH, D_MODEL), 0.02),
        "w_in": nrm(ks[3], (DEPTH, D_MODEL, D_IN), D_MODEL ** -0.5),
        "w_sp": nrm(ks[4], (DEPTH, N_HEADS_A, CHUNK_A, CHUNK_A), 0.5 * CHUNK_A ** -0.5),
        "b_sp": 1.0 + nrm(ks[5], (DEPTH, N_HEADS_A, CHUNK_A), 0.02),
        "g_v_a": 1.0 + nrm(ks[6], (DEPTH, D_A), 0.02),
        "w_gk_fwd": nrm(ks[7], (DEPTH, GATE_RANK, D_K_B), GATE_RANK ** -0.5),
        "b_gk_fwd": nrm(ks[8], (DEPTH, D_K_B), 0.01),
        "w_gk_bwd": nrm(ks[9], (DEPTH, GATE_RANK, D_K_B), GATE_RANK ** -0.5),
        "b_gk_bwd": nrm(ks[10], (DEPTH, D_K_B), 0.01),
        "g_norm_b": 1.0 + nrm(ks[11], (DEPTH, HEAD_V_B), 0.02),
        "w_out": nrm(ks[12], (DEPTH, D_MIX, D_MODEL), D_MIX ** -0.5),
        "norm_post": 1.0 + nrm(ks[13], (DEPTH, D_MODEL), 0.02),
    }


def reference(x_prompt, x_sample, norm_pre, w_in, w_sp, b_sp, g_v_a, w_gk_fwd, b_gk_fwd,
              w_gk_bwd, b_gk_bwd, g_norm_b, w_out, norm_post):
    y_prompt = x_prompt
    y_sample = x_sample
    for l in range(DEPTH):
        params = (norm_pre[l], w_in[l], w_sp[l], b_sp[l], g_v_a[l], w_gk_fwd[l], b_gk_fwd[l],
                  w_gk_bwd[l], b_gk_bwd[l], g_norm_b[l], w_out[l], norm_post[l])
        y_prompt = hybrid_layer(y_prompt, *params)
        y_sample = hybrid_layer(y_sample, *params)
    return (y_prompt, y_sample)
```

```python
import math
from contextlib import ExitStack

import numpy as np
import concourse.bass as bass
import concourse.mybir as mybir
from concourse.bass_utils import run_bass_kernel_spmd

F32 = mybir.dt.float32
BF16 = mybir.dt.bfloat16
AF = mybir.ActivationFunctionType
ALU = mybir.AluOpType
AX = mybir.AxisListType

D = 1024
DIN = 3104
EPS = 1e-6
C1 = math.sqrt(2.0 / math.pi)
C2 = 0.044715

FULL_JOBS = [(64, 32, 0, 0), (16, 16, 8192, 4096)]


class Sched:
    ENGS = ("pe", "act", "dve", "pool", "sp")
    SYNC_LAT = 0.12
    ACT_SWITCH = 1.3
    PRIO = "rank"

    def __init__(self, nc):
        self.nc = nc
        self.all = []
        self.last_w = {}
        self.readers = {}
        self.total = 0

    def op(self, eng, fn, reads=(), writes=(), dma_key=None, cost=None, lat=None, tset=None, group=None):
        uid = len(self.all)
        self.total += 1
        deps = set()
        for b in reads:
            d = self.last_w.get(b)
            if d is not None:
                deps.add(d)
        for b in writes:
            d = self.last_w.get(b)
            if d is not None:
                deps.add(d)
            deps.update(self.readers.get(b, ()))
        deps.discard(uid)
        for b in reads:
            self.readers.setdefault(b, []).append(uid)
        for b in writes:
            self.last_w[b] = uid
            self.readers[b] = []
        if eng == "sp":
            assert dma_key is not None
        if dma_key is not None:
            cost = 0.1 if eng == "sp" else 1.0
            lat = 3.0
        if cost is None:
            cost, tset = self._estimate(eng, fn)
        if lat is None:
            lat = cost
        self.all.append(dict(eng=eng, fn=fn, deps=deps, dma_key=dma_key, cost=cost, lat=lat, tset=tset,
                             sig=False, cnt=None, group=group))

    class _Probe:
        def __getattr__(self, name):
            def rec(*a, **k):
                self.call = (name, a, k)
                return None
            return rec

    def _estimate(self, eng, fn):
        pr = Sched._Probe()
        fn(pr)
        name, a, k = pr.call
        out = k.get("out", a[0] if a else None)
        cols = 1
        for d in tuple(out.shape)[1:]:
            cols *= int(d)
        tset = None
        if eng == "pe":
            return max(0.066, cols / 2170.0), None
        if eng == "act":
            f = k.get("func")
            if f in (AF.Exp, AF.Ln):
                tset = 6
            elif f == AF.Silu:
                tset = 18
            elif f == AF.Gelu_apprx_tanh:
                tset = 11
            return 0.22 + cols * 0.00085, tset
        if eng == "dve":
            return 0.12 + cols * 0.00105, None
        if eng == "sp":
            return 0.1, None
        return 0.3 + cols * 0.002, None

    def schedule(self):
        import heapq
        ops = self.all
        n = len(ops)
        ndeps = [len(o["deps"]) for o in ops]
        users = [[] for _ in range(n)]
        for u, o in enumerate(ops):
            for d in o["deps"]:
                users[d].append(u)
        ready_t = [0.0] * n
        fin = [0.0] * n
        future = {e: [] for e in self.ENGS}
        avail = {e: [] for e in self.ENGS}
        free_t = {e: 0.0 for e in self.ENGS}
        order = {e: [] for e in self.ENGS}
        cur_set = [None]
        prio = list(range(n))
        if self.PRIO == "rank":
            rank = [0.0] * n
            for u in range(n - 1, -1, -1):
                m = 0.0
                for v in users[u]:
                    if rank[v] > m:
                        m = rank[v]
                rank[u] = ops[u]["cost"] + m
            top = max(rank)
            prio = [(top - rank[u]) for u in range(n)]
        for u, o in enumerate(ops):
            if ndeps[u] == 0:
                heapq.heappush(future[o["eng"]], (0.0, u))
        groups = {}
        for u, o in enumerate(ops):
            if o["group"] is not None:
                groups.setdefault(o["group"], []).append(u)
        scheduled = [False] * n
        done = 0

        def commit(u, e, t):
            o = ops[u]
            c = o["cost"]
            if e == "act" and o["tset"] is not None and cur_set[0] != o["tset"]:
                c += self.ACT_SWITCH
                cur_set[0] = o["tset"]
            start = max(t, free_t[e], ready_t[u])
            free_t[e] = start + c
            fin[u] = start + (o["lat"] if o["dma_key"] is not None else c)
            order[e].append(u)
            scheduled[u] = True
            for v in users[u]:
                ov = ops[v]
                rt = fin[u] + (0.0 if ov["eng"] == e else self.SYNC_LAT)
                if rt > ready_t[v]:
                    ready_t[v] = rt
                ndeps[v] -= 1
                if ndeps[v] == 0:
                    heapq.heappush(future[ov["eng"]], (ready_t[v], v))

        while done < n:
            best = None
            for e in self.ENGS:
                fu, av = future[e], avail[e]
                while fu and (scheduled[fu[0][1]] or fu[0][0] <= free_t[e]):
                    v_ = heapq.heappop(fu)[1]
                    if not scheduled[v_]:
                        heapq.heappush(av, (prio[v_], v_))
                while av and scheduled[av[0][1]]:
                    heapq.heappop(av)
                if av:
                    cand = (free_t[e], av[0][1], e)
                elif fu:
                    cand = (fu[0][0], fu[0][1], e)
                else:
                    continue
                if best is None or cand < best:
                    best = cand
            t, u, e = best
            if avail[e] and avail[e][0][1] == u:
                heapq.heappop(avail[e])
            else:
                heapq.heappop(future[e])
            commit(u, e, t)
            done += 1
            g_ = ops[u]["group"]
            if g_ is not None:
                for v in groups[g_]:
                    if not scheduled[v]:
                        assert ndeps[v] == 0 and ops[v]["eng"] == e, "group members must share engine and inputs"
                        commit(v, e, free_t[e])
                        done += 1
        self.order = order
        self.est_us = max(free_t.values())

    def emit(self, stack):
        nc = self.nc
        self.schedule()
        ops = self.all
        pos = {}
        for e in self.ENGS:
            for i, u in enumerate(self.order[e]):
                pos[u] = i
        for u, o in enumerate(ops):
            sd = {}
            for d in o["deps"]:
                de = ops[d]["eng"]
                if de == "pe" and o["eng"] == "pe":
                    continue
                key = ("dma", ops[d]["dma_key"]) if ops[d]["dma_key"] is not None else (de, None)
                if key not in sd or pos[sd[key]] < pos[d]:
                    sd[key] = d
            o["sdeps"] = sd
            for d in sd.values():
                ops[d]["sig"] = True
        sems = {}
        for e in ("pe", "act", "dve", "pool"):
            sems[e] = stack.enter_context(nc.semaphore("s_" + e))
            c = 0
            for u in self.order[e]:
                if ops[u]["sig"] and ops[u]["dma_key"] is None:
                    c += 1
                    ops[u]["cnt"] = c
        dsem, dcount, dkey_eng = {}, {}, {}
        for e in self.ENGS:
            for u in self.order[e]:
                o = ops[u]
                k = o["dma_key"]
                if k is None:
                    continue
                assert dkey_eng.setdefault(k, e) == e, "a DMA key must stay on one queue"
                if k not in dsem:
                    dsem[k] = stack.enter_context(nc.semaphore("d_" + str(k)))
                    dcount[k] = 0
                dcount[k] += 16
                o["cnt"] = dcount[k]
                o["sem"] = dsem[k]
        block = stack.enter_context(nc.Block())
        engmap = {"pe": block.tensor, "act": block.scalar, "dve": block.vector,
                  "pool": block.gpsimd, "sp": block.sync}
        final_waits = [(dsem[k], dcount[k]) for k in dsem]
        for e in self.ENGS:
            lst = self.order[e]

            def body(eng, lst=lst, e=e):
                waited = {}
                for u in lst:
                    o = ops[u]
                    for (key, d) in o["sdeps"].items():
                        src = ops[d]
                        if waited.get(key, 0) >= src["cnt"]:
                            continue
                        waited[key] = src["cnt"]
                        eng.wait_ge(src["sem"] if key[0] == "dma" else sems[key[0]], src["cnt"])
                    ins = o["fn"](eng)
                    if o["dma_key"] is not None:
                        ins.then_inc(o["sem"], 16)
                    elif o["sig"]:
                        ins.then_inc(sems[e], 1)
                if e == "sp":
                    for (s_, c_) in final_waits:
                        eng.wait_ge(s_, c_)
            engmap[e](body)


def build_nc(jobs):
    n_x_rows = max(j[2] + j[0] * 128 for j in jobs)
    n_o_rows = max(j[3] + j[1] * 128 for j in jobs)
    max_own = max(j[1] for j in jobs)
    nc = bass.Bass("TRN2", target_bir_lowering=False)
    x_d = nc.dram_tensor("x", [n_x_rows, D], F32, kind="ExternalInput").ap()
    y_d = nc.dram_tensor("y", [n_o_rows, D], F32, kind="ExternalOutput").ap()
    win_d = nc.dram_tensor("w_in", [D, DIN], F32, kind="ExternalInput").ap()
    wout_d = nc.dram_tensor("w_out", [D, D], F32, kind="ExternalInput").ap()
    gpre_d = nc.dram_tensor("gpre", [128, 8], F32, kind="ExternalInput").ap()
    wspT_d = nc.dram_tensor("wspT", [128, 4 * 128], F32, kind="ExternalInput").ap()
    bsp_d = nc.dram_tensor("bsp", [128, 4], F32, kind="ExternalInput").ap()
    gv_d = nc.dram_tensor("gv", [512], F32, kind="ExternalInput").ap()
    wgk_d = nc.dram_tensor("wgk", [33, 512], F32, kind="ExternalInput").ap()
    gnb_d = nc.dram_tensor("gnb", [512], F32, kind="ExternalInput").ap()
    gpost_d = nc.dram_tensor("gpost", [D], F32, kind="ExternalInput").ap()

    with ExitStack() as st:
        def T(name, shape, dt=F32):
            return st.enter_context(nc.sbuf_tensor(name, shape, dt))

        S = Sched(nc)
        ident = T("ident", [128, 128], BF16)
        maskF = T("maskF", [128, 512], BF16)
        maskB = T("maskB", [128, 512], BF16)
        triF = T("triF", [128, 128], BF16)
        triB = T("triB", [128, 128], BF16)
        negcol = T("negcol", [128, 1], BF16)
        Wb = T("Wb", [128, 8, DIN], BF16)
        Wo = T("Wo", [128, 8, D], BF16)
        yt = T("yt", [128, D], F32)
        NB = 2
        UV = [T("UV%d" % i, [128, D], F32) for i in range(NB)]
        ZZ = [T("ZZ%d" % i, [128, D], F32) for i in range(NB)]
        gpre = T("gpre_s", [128, 8], F32)
        wspT = T("wspT_s", [128, 512], BF16)
        bsp = T("bsp_s", [128, 4], F32)
        gvb = T("gvb", [128, 512], F32)
        wgk = T("wgk_s", [33, 512], BF16)
        gnb = T("gnb_s", [128, 512], F32)
        gpost = T("gpost_s", [128, D], F32)
        lrT = T("lrT", [33, 128], BF16)

        def P(e, f, r=(), w=(), c=None, ts=None, g=None):
            S.op(e, f, reads=r, writes=w, cost=c, tset=ts, group=g)

        P("pool", lambda e: e.memset(ident[:], 1.0), w=["ident"])
        P("pool", lambda e: e.affine_select(out=ident[:], in_=ident[:], pattern=[[-1, 128]], compare_op=ALU.is_equal,
                                            fill=0.0, base=0, channel_multiplier=1), r=["ident"], w=["ident"])
        P("pool", lambda e: e.memset(maskF[:], 1.0), w=["maskF"])
        P("pool", lambda e: e.affine_select(out=maskF[:].rearrange("p (h c) -> p h c", h=4), in_=maskF[:].rearrange("p (h c) -> p h c", h=4),
                                            pattern=[[0, 4], [1, 128]], compare_op=ALU.is_ge, fill=0.0, base=0,
                                            channel_multiplier=-1), r=["maskF"], w=["maskF"])
        P("pool", lambda e: e.memset(maskB[:], 1.0), w=["maskB"])
        P("pool", lambda e: e.affine_select(out=maskB[:].rearrange("p (h c) -> p h c", h=4), in_=maskB[:].rearrange("p (h c) -> p h c", h=4),
                                            pattern=[[0, 4], [-1, 128]], compare_op=ALU.is_gt, fill=0.0, base=0,
                                            channel_multiplier=1), r=["maskB"], w=["maskB"])
        P("pool", lambda e: e.memset(triF[:], -1.0 / 16), w=["triF"])
        P("pool", lambda e: e.affine_select(out=triF[:], in_=triF[:], pattern=[[1, 128]], compare_op=ALU.is_ge, fill=0.0,
                                            base=0, channel_multiplier=-1), r=["triF"], w=["triF"])
        P("pool", lambda e: e.memset(triB[:], -1.0 / 16), w=["triB"])
        P("pool", lambda e: e.affine_select(out=triB[:], in_=triB[:], pattern=[[-1, 128]], compare_op=ALU.is_ge, fill=0.0,
                                            base=0, channel_multiplier=1), r=["triB"], w=["triB"])
        P("pool", lambda e: e.memset(negcol[:], -1.0 / 16), w=["negcol"])
        P("pool", lambda e: e.memset(lrT[32:33, :], 1.0), w=["lrT_one"])

        S.op("sp", lambda e: e.dma_start(out=gpre[:], in_=gpre_d[:, :]), writes=["gpre"], dma_key="gpre")
        NC = 25
        vbc = T("vbc", [128, NC, 512], BF16)
        stg = [(UV[0], ["U0", "V0"], "stg0"), (ZZ[0], ["ZA0", "ZB0"], "stg1"),
               (UV[1], ["U1", "V1"], "stg2"), (ZZ[1], ["ZA1", "ZB1"], "stg3")]
        for j in range(4):
            view = vbc[:, 4 * j:4 * j + 4, :].bitcast(F32).rearrange("p a c -> p (a c)")
            stg.append((view, ["vbc%d" % (4 * j + i_) for i_ in range(4)], "stg%d" % (4 + j)))
        NSTG = [len(stg)]
        stg_n = [0]

        def wcol_ids(c0, c1):
            ids = []
            for (a, b_, nm) in ((0, 1792, "WbA"), (1792, 2592, "WbB"), (2592, 3104, "WbC")):
                if c0 < b_ and c1 > a:
                    ids.append(nm)
            return ids

        def load_win_piece(k, c0, c1):
            buf, ids, key = stg[stg_n[0] % NSTG[0]]
            use_dve = (stg_n[0] % 2 == 0)
            stg_n[0] += 1
            S.op("sp", lambda e: e.dma_start(out=buf[:, 0:c1 - c0], in_=win_d[k * 128:(k + 1) * 128, c0:c1]), writes=ids, dma_key=key)
            if use_dve:
                P("dve", lambda e: e.tensor_scalar(out=Wb[:, k, c0:c1], in0=buf[:, 0:c1 - c0], scalar1=gpre[:, k:k + 1], scalar2=None, op0=ALU.mult),
                  r=ids + ["gpre"], w=wcol_ids(c0, c1))
            else:
                P("act", lambda e: e.activation(out=Wb[:, k, c0:c1], in_=buf[:, 0:c1 - c0], func=AF.Identity, scale=gpre[:, k:k + 1]),
                  r=ids + ["gpre"], w=wcol_ids(c0, c1))

        def load_wout_piece(k):
            buf, ids, key = stg[stg_n[0] % NSTG[0]]
            use_dve = (stg_n[0] % 2 == 0)
            stg_n[0] += 1
            S.op("sp", lambda e: e.dma_start(out=buf[:, :], in_=wout_d[k * 128:(k + 1) * 128, :]), writes=ids, dma_key=key)
            if use_dve:
                P("dve", lambda e: e.tensor_copy(out=Wo[:, k, :], in_=buf[:, :]), r=ids, w=["Wo"])
            else:
                P("act", lambda e: e.activation(out=Wo[:, k, :], in_=buf[:, :], func=AF.Copy), r=ids, w=["Wo"])

        for k in range(8):
            load_win_piece(k, 1792, 2592)
        NSTG[0] = 4
        deferred = []
        for (c0, c1) in ((0, 1024), (1024, 1792), (2592, 3104)):
            for k in range(8):
                deferred.append(lambda k=k, c0=c0, c1=c1: load_win_piece(k, c0, c1))
        for k in range(8):
            deferred.append(lambda k=k: load_wout_piece(k))
        S.op("sp", lambda e: e.dma_start(out=yt[0:33, 0:512], in_=wgk_d[:, :]), writes=["yt0", "yt1"], dma_key="ytst")
        P("dve", lambda e: e.tensor_copy(out=wgk[:], in_=yt[0:33, 0:512]), r=["yt0", "yt1"], w=["wgk"])
        S.op("sp", lambda e: e.dma_start(out=yt[:, 0:512], in_=wspT_d[:, :]), writes=["yt0", "yt1"], dma_key="ytst")
        P("dve", lambda e: e.tensor_copy(out=wspT[:], in_=yt[:, 0:512]), r=["yt0", "yt1"], w=["wspT"])
        S.op("sp", lambda e: e.dma_start(out=bsp[:], in_=bsp_d[:, :]), writes=["bsp"], dma_key="bsp")
        S.op("sp", lambda e: e.dma_start(out=gvb[:], in_=gv_d.partition_broadcast(128)), writes=["gvb"], dma_key="gvb")
        S.op("sp", lambda e: e.dma_start(out=gnb[:], in_=gnb_d.partition_broadcast(128)), writes=["gnb"], dma_key="gnb")
        S.op("sp", lambda e: e.dma_start(out=gpost[:], in_=gpost_d.partition_broadcast(128)), writes=["gpost"], dma_key="gpost")

        NXS = 2
        xs = [T("xs%d" % i, [128, D], F32) for i in range(NXS)]
        xbf = [T("xbf%d" % i, [128, D], BF16) for i in range(2)]
        xT = [T("xT%d" % i, [128, 8, 128], BF16) for i in range(2)]
        junk = T("junk", [128, D], BF16)
        junky = T("junky", [128, D], BF16)
        sm = [T("sm%d" % i, [128, 32], F32) for i in range(4)]

        def slots(name, shape, dt=F32, n=NB):
            return [T("%s%d" % (name, i), shape, dt) for i in range(n)]
        U_ = [UV[i][:, 0:512] for i in range(NB)]
        V_ = [UV[i][:, 512:1024] for i in range(NB)]
        ZA = [ZZ[i][:, 0:512] for i in range(NB)]
        ZB = [ZZ[i][:, 512:1024] for i in range(NB)]
        def vbsrc(job, n, b):
            if 1 <= n < job[1] - 1 and n - 1 < NC:
                return (lambda lo, hi: vbc[:, n - 1, lo:hi]), "vbc%d" % (n - 1), True
            return (lambda lo, hi: vb[b][:, lo:hi]), "vb%d" % b, False
        TS = T("TS", [128, 512], F32)
        ON = T("ON", [128, 512], F32)
        qk = slots("qk", [128, 512])
        scr = T("scr", [128, 512], F32)
        scr2 = T("scr2", [128, 512], F32)
        Ep = T("Ep", [128, 512], F32)
        Em = T("Em", [128, 512], F32)
        vlnb = slots("vlnb", [128, 512], BF16)
        vb = slots("vb", [128, 512], BF16)
        lrs = slots("lrs", [128, 32], BF16)
        spb = slots("spb", [128, 512], BF16)
        qin = slots("qin", [128, 1024], BF16)
        kin = slots("kin", [128, 512], BF16)
        qT = slots("qT", [128, 8, 128], BF16)
        kT = slots("kT", [128, 4, 128], BF16)
        scf = slots("scf", [128, 512], BF16)
        scb = slots("scb", [128, 512], BF16)
        cat = T("cat", [128, D], BF16)
        catT = T("catT", [128, 8, 128], BF16)
        dec = [T("dec%d" % i, [128, 2], F32) for i in range(3)]
        Tst = T("Tst", [128, 512], F32)
        Sf = [T("Sf%d" % i, [128, 512], BF16) for i in range(2)]
        Sb = T("Sb", [128, max_own, 256], BF16)

        NPB = 6
        pb = [st.enter_context(nc.psum_tensor("pb%d" % i, [128, 512], F32)) for i in range(NPB)]
        pt = [st.enter_context(nc.psum_tensor("pt%d" % i, [128, 1024], BF16)) for i in range(2)]
        cnt = dict(bank=0, pth=0, tile=0, dec=0)
        for i in range(NB):
            P("pool", lambda e, i=i: e.memset(qin[i][:], 0.0), w=["qin%d_0" % i, "qin%d_1" % i])

        def bank():
            i = cnt["bank"] % NPB
            cnt["bank"] += 1
            return pb[i], "pb%d" % i

        def ptbank():
            i = cnt["pth"] % 2
            cnt["pth"] += 1
            return pt[i], "pt%d" % i

        def bankA():
            i = cnt["bank"] % 2
            cnt["bank"] += 1
            return pb[i], "pb%d" % i

        def bankB():
            i = 2 + cnt["bankb"] % 2
            cnt["bankb"] += 1
            return pb[i], "pb%d" % i
        cnt["bankb"] = 0

        def load_x(row0, g, orow=None):
            s3 = g % NXS
            S.op("sp", lambda e: e.dma_start(out=xs[s3][:], in_=x_d[row0:row0 + 128, :]), writes=["xs%d" % s3],
                 dma_key="xs%d" % s3)
            if orow is not None:
                S.op("sp", lambda e: e.dma_start(out=y_d[orow:orow + 128, :], in_=x_d[row0:row0 + 128, :]),
                     writes=["yrow%d" % orow, "ycpk%d" % (g % 4)], dma_key="ycp%d" % (g % 4))

        rstdc = T("rstdc", [128, 64], F32)
        lrc = T("lrc", [128, max_own, 32], BF16)

        def lrsrc(job, n, b):
            if 1 <= n < job[1]:
                return lrc[:, n, :], "lrc%d" % n, True
            return lrs[b][:], "lrs%d" % b, False

        def front1(g):
            s3, s2, ss = g % NXS, g % 2, g % 4
            kind, job, n, _ = tiles[g]
            own = (1 <= n < job[1])
            if kind == "p2" and own:
                rs, rsid = rstdc[:, n:n + 1], "rstdc%d" % n
            else:
                rs, rsid = (rstdc[:, n:n + 1], "rstdc%d" % n) if own else (sm[ss][:, 2:3], "sm%dc" % ss)
                P("act", lambda e: e.activation(out=junk[:], in_=xs[s3][:], func=AF.Square, accum_out=sm[ss][:, 0:1]),
                  r=["xs%d" % s3], w=["sm%da" % ss, "junk"])
                P("act", lambda e: e.activation(out=sm[ss][:, 1:2], in_=sm[ss][:, 0:1], func=AF.Ln, scale=1.0 / D, bias=EPS),
                  r=["sm%da" % ss], w=["sm%db" % ss])
                P("act", lambda e: e.activation(out=rs, in_=sm[ss][:, 1:2], func=AF.Exp, scale=-0.5),
                  r=["sm%db" % ss], w=[rsid])
            P("dve", lambda e: e.tensor_scalar(out=xbf[s2][:], in0=xs[s3][:], scalar1=rs, scalar2=None, op0=ALU.mult),
              r=["xs%d" % s3, rsid], w=["xbf%d" % s2])

        def front2(g):
            s2 = g % 2
            ph, pid = ptbank()
            for k in range(8):
                P("pe", lambda e, ph=ph, k=k: e.transpose(out=ph[:, k * 128:(k + 1) * 128],
                                                          in_=xbf[s2][:, k * 128:(k + 1) * 128], identity=ident[:]),
                  r=["xbf%d" % s2, "ident"], w=[pid])
            if tiles[g][0] == "p1":
                P("act", lambda e, ph=ph: e.activation(out=xT[s2][:], in_=ph[:, :].rearrange("p (k t) -> p k t", k=8), func=AF.Copy),
                  r=[pid], w=["xT%d" % s2])
            else:
                P("dve", lambda e, ph=ph: e.tensor_copy(out=xT[s2][:], in_=ph[:, :].rearrange("p (k t) -> p k t", k=8)),
                  r=[pid], w=["xT%d" % s2])

        def genF(g):
            for _ in range(3):
                yield
            front1(g)
            yield
            yield
            front2(g)
            yield

        def inproj(s2, c0, c1):
            bk, bid = bankA()
            wids = wcol_ids(c0, c1)
            for k in range(8):
                P("pe", lambda e, k=k, bk=bk: e.matmul(bk[:, 0:c1 - c0], lhsT=xT[s2][:, k, :], rhs=Wb[:, k, c0:c1],
                                                       start=(k == 0), stop=(k == 7)),
                  r=["xT%d" % s2] + wids, w=[bid])
            return bk, bid

        def gate_chain(b, ncols, c_lo, lr_ap, lr_id):
            ph, pid = ptbank()
            P("pe", lambda e: e.transpose(out=ph[0:32, 0:128], in_=lr_ap, identity=ident[:]),
              r=[lr_id, "ident"], w=[pid])
            P("dve", lambda e: e.tensor_copy(out=lrT[0:32, :], in_=ph[0:32, 0:128]), r=[pid], w=["lrT"])
            bk, bid = bankB()
            P("pe", lambda e: e.matmul(bk[:, 0:ncols], lhsT=lrT[0:33, :], rhs=wgk[0:33, c_lo:c_lo + ncols], start=True, stop=True),
              r=["lrT", "lrT_one", "wgk"], w=[bid])
            P("act", lambda e: e.activation(out=scr2[:, 0:ncols], in_=bk[:, 0:ncols], func=AF.Exp, scale=-1.0),
              r=[bid], w=["scr2"])
            P("act", lambda e: e.activation(out=spb[b][:, 0:ncols], in_=scr2[:, 0:ncols], func=AF.Ln, bias=1.0),
              r=["scr2"], w=["spb%d" % b])

        def p1_A(job, n, first, g):
            b, s2 = g % NB, g % 2
            bk, bid = inproj(s2, 1792, 2080)
            lr_ap, lr_id, _ = lrsrc(job, n, b)
            P("act", lambda e: e.activation(out=lr_ap, in_=bk[:, 256:288], func=AF.Copy), r=[bid], w=[lr_id])
            P("act", lambda e: e.activation(out=qk[b][:, 256:512], in_=bk[:, 0:256], func=AF.Copy), r=[bid], w=["qk%d" % b])
            yield
            bk2, bid2 = inproj(s2, 2080, 2592)
            vget, vid, _ = vbsrc(job, n, b)
            P("dve", lambda e: e.tensor_copy(out=vget(0, 512), in_=bk2[:, :]), r=[bid2], w=[vid])
            yield

        def p1_B(job, n, first, g):
            ntot, nown, xr0, _ = job
            b = g % NB
            lr_ap, lr_id, _ = lrsrc(job, n, b)
            gate_chain(b, 256, 256, lr_ap, lr_id)
            yield
            bc, bcid = bankB()
            P("pe", lambda e: e.matmul(bc[:, 0:256], lhsT=triB[:], rhs=spb[b][:, 0:256], start=True, stop=True),
              r=["triB", "spb%d" % b], w=[bcid])
            bl, blid = bankB()
            for pr in range(2):
                P("pe", lambda e, pr=pr: e.matmul(bl[:, pr:pr + 1], lhsT=spb[b][:, pr * 128:(pr + 1) * 128], rhs=negcol[:, 0:1],
                                                  start=True, stop=True), r=["negcol", "spb%d" % b], w=[blid])
            P("act", lambda e: e.activation(out=Em[:, 0:256], in_=bc[:, 0:256], func=AF.Exp, scale=-1.0), r=[bcid], w=["Em"])
            dcur = cnt["dec"] % 3
            dprev = (cnt["dec"] - 1) % 3
            cnt["dec"] += 1
            P("act", lambda e: e.activation(out=dec[dcur][:, 0:2], in_=bl[:, 0:2], func=AF.Exp), r=[blid], w=["dec%d" % dcur])
            P("dve", lambda e: e.tensor_tensor(out=kin[b][:, 0:256], in0=Em[:, 0:256], in1=qk[b][:, 256:512], op=ALU.mult),
              r=["Em", "qk%d" % b], w=["kin%d" % b])
            yield
            kv, kvid = bankB()
            vget, vid, _ = vbsrc(job, n, b)
            for pr in range(2):
                P("pe", lambda e, pr=pr: e.matmul(kv[:, pr * 256:(pr + 1) * 256], lhsT=kin[b][:, pr * 128:(pr + 1) * 128],
                                                  rhs=vget(pr * 256, (pr + 1) * 256), start=True, stop=True),
                  r=["kin%d" % b, vid], w=[kvid])
            if first:
                P("dve", lambda e: e.tensor_copy(out=Tst[:], in_=kv[:, :]), r=[kvid], w=["Tst0", "Tst1"])
            else:
                for pr in range(2):
                    P("dve", lambda e, pr=pr: e.scalar_tensor_tensor(out=Tst[:, pr * 256:(pr + 1) * 256], in0=Tst[:, pr * 256:(pr + 1) * 256],
                                                                     scalar=dec[dprev][:, pr:pr + 1], in1=kv[:, pr * 256:(pr + 1) * 256],
                                                                     op0=ALU.mult, op1=ALU.add),
                      r=[kvid, "Tst%d" % pr, "dec%d" % dprev], w=["Tst%d" % pr])
            if 1 <= n <= nown:
                for hh in range(2):
                    rs = slice(hh * 64, (hh + 1) * 64)
                    P("dve", lambda e, hh=hh, rs=rs: e.tensor_tensor(
                        out=Sb[rs, n - 1, :].rearrange("p (r c) -> p r c", r=2),
                        in0=Tst[rs, :].rearrange("p (r x) -> p r x", r=2)[:, :, hh * 128:(hh + 1) * 128],
                        in1=dec[dcur][rs, 0:2].unsqueeze(2).to_broadcast([64, 2, 128]), op=ALU.mult),
                      r=["Tst0", "Tst1", "dec%d" % dcur], w=["Sb%d_0%d" % (n - 1, hh), "Sb%d_1%d" % (n - 1, hh)])
            yield

        def p2_A(job, n, first, g):
            b, s2, ss = g % NB, g % 2, g % 4
            if not lrsrc(job, n, b)[2]:
                bk, bid = inproj(s2, 2048, 2080)
                P("act", lambda e, bk=bk: e.activation(out=lrs[b][:], in_=bk[:, 0:32], func=AF.Copy), r=[bid], w=["lrs%d" % b])
                yield
            bk, bid = inproj(s2, 1536, 2048)
            P("act", lambda e, bk=bk: e.activation(out=qk[b][:], in_=bk[:, :], func=AF.Copy), r=[bid], w=["qk%d" % b])
            yield
            if not vbsrc(job, n, b)[2]:
                bk, bid = inproj(s2, 2080, 2592)
                P("dve", lambda e, bk=bk: e.tensor_copy(out=vb[b][:], in_=bk[:, :]), r=[bid], w=["vb%d" % b])
                yield
            bk, bid = inproj(s2, 1024, 1536)
            P("act", lambda e, bk=bk: e.activation(out=ZA[b][:], in_=bk[:, :], func=AF.Copy), r=[bid], w=["ZA%d" % b])
            yield
            bk, bid = inproj(s2, 2592, 3104)
            P("dve", lambda e, bk=bk: e.tensor_copy(out=ZB[b][:], in_=bk[:, :]), r=[bid], w=["ZB%d" % b])
            yield
            bk, bid = inproj(s2, 0, 512)
            P("act", lambda e, bk=bk: e.activation(out=U_[b][:], in_=bk[:, :], func=AF.Copy), r=[bid], w=["U%d" % b])
            yield
            bk, bid = inproj(s2, 512, 1024)
            P("dve", lambda e, bk=bk: e.tensor_copy(out=V_[b][:], in_=bk[:, :]), r=[bid], w=["V%d" % b])
            yield
            yield
            yield

            zin, uin = ["ZA%d" % b, "ZB%d" % b], ["U%d" % b, "V%d" % b]
            P("act", lambda e: e.activation(out=ZA[b][:], in_=ZA[b][:], func=AF.Silu), r=zin, w=["ZA%d" % b], g="sil%d" % g)
            P("act", lambda e: e.activation(out=ZB[b][:], in_=ZB[b][:], func=AF.Silu), r=zin, w=["ZB%d" % b], g="sil%d" % g)
            P("act", lambda e: e.activation(out=U_[b][:], in_=U_[b][:], func=AF.Gelu_apprx_tanh), r=uin, w=["U%d" % b], g="gel%d" % g)
            P("act", lambda e: e.activation(out=V_[b][:], in_=V_[b][:], func=AF.Gelu_apprx_tanh), r=uin, w=["V%d" % b], g="gel%d" % g)
            P("dve", lambda e: e.bn_stats(out=sm[ss][:, 8:14], in_=V_[b][:]), r=["V%d" % b], w=["sm%dd" % ss])
            P("dve", lambda e: e.bn_aggr(out=sm[ss][:, 14:16], in_=sm[ss][:, 8:14]), r=["sm%dd" % ss], w=["sm%de" % ss])
            P("act", lambda e: e.activation(out=sm[ss][:, 16:17], in_=sm[ss][:, 15:16], func=AF.Ln, bias=EPS), r=["sm%de" % ss], w=["sm%df" % ss])
            P("act", lambda e: e.activation(out=sm[ss][:, 17:18], in_=sm[ss][:, 16:17], func=AF.Exp, scale=-0.5), r=["sm%df" % ss], w=["sm%dg" % ss])
            P("dve", lambda e: e.tensor_scalar(out=V_[b][:], in0=V_[b][:], scalar1=sm[ss][:, 14:15], scalar2=sm[ss][:, 17:18],
                                               op0=ALU.subtract, op1=ALU.mult), r=["V%d" % b, "sm%de" % ss, "sm%dg" % ss], w=["V%d" % b])
            P("dve", lambda e: e.tensor_tensor(out=vlnb[b][:], in0=V_[b][:], in1=gvb[:], op=ALU.mult),
              r=["V%d" % b, "gvb"], w=["vlnb%d" % b])
            P("dve", lambda e: e.tensor_tensor(out=U_[b][:], in0=U_[b][:], in1=ZA[b][:], op=ALU.mult),
              r=["U%d" % b, "ZA%d" % b], w=["U%d" % b])
            P("dve", lambda e: e.tensor_tensor(out=ZB[b][:], in0=ZB[b][:], in1=gnb[:], op=ALU.mult),
              r=["ZB%d" % b, "gnb"], w=["ZB%d" % b])
            yield

        def p2_B(job, n, first, g):
            ntot, nown, xr0, or0 = job
            b, s3, ss = g % NB, g % NXS, g % 4
            last_in_seq = (n == ntot - 1)
            lr_ap, lr_id, _ = lrsrc(job, n, b)
            gate_chain(b, 512, 0, lr_ap, lr_id)
            yield
            bc, bcid = bankB()
            P("pe", lambda e: e.matmul(bc[:, 0:256], lhsT=triF[:], rhs=spb[b][:, 0:256], start=True, stop=True),
              r=["triF", "spb%d" % b], w=[bcid])
            P("pe", lambda e: e.matmul(bc[:, 256:512], lhsT=triB[:], rhs=spb[b][:, 256:512], start=True, stop=True),
              r=["triB", "spb%d" % b], w=[bcid])
            bl, blid = bankB()
            for pr in range(2):
                P("pe", lambda e, pr=pr: e.matmul(bl[:, pr:pr + 1], lhsT=spb[b][:, pr * 128:(pr + 1) * 128], rhs=negcol[:, 0:1],
                                                  start=True, stop=True), r=["negcol", "spb%d" % b], w=[blid])
            P("act", lambda e: e.activation(out=Ep[:], in_=bc[:, :], func=AF.Exp, bias=math.log(0.125)), r=[bcid], w=["Ep"])
            P("act", lambda e: e.activation(out=Em[:], in_=bc[:, :], func=AF.Exp, scale=-1.0), r=[bcid], w=["Em"])
            dcur = cnt["dec"] % 3
            dprev = (cnt["dec"] - 1) % 3
            cnt["dec"] += 1
            P("act", lambda e: e.activation(out=dec[dcur][:, 0:2], in_=bl[:, 0:2], func=AF.Exp), r=[blid], w=["dec%d" % dcur])
            for hh in range(2):
                P("dve", lambda e, hh=hh: e.tensor_tensor(
                    out=qin[b][:].rearrange("p (t r h c) -> p t r h c", t=2, r=2, h=2)[:, :, :, hh, hh * 64:(hh + 1) * 64],
                    in0=Ep[:].rearrange("p (t r h d) -> p t r h d", t=2, r=2, h=2)[:, :, :, hh, :],
                    in1=qk[b][:, 0:256].rearrange("p (r h d) -> p r h d", r=2, h=2)[:, :, hh, :].unsqueeze(1).to_broadcast([128, 2, 2, 64]),
                    op=ALU.mult), r=["Ep", "qk%d" % b], w=["qin%d_%d" % (b, hh)])
            P("dve", lambda e: e.tensor_tensor(out=kin[b][:].rearrange("p (t d) -> p t d", t=2), in0=Em[:].rearrange("p (t d) -> p t d", t=2),
                                               in1=qk[b][:, 256:512].unsqueeze(1).to_broadcast([128, 2, 256]), op=ALU.mult),
              r=["Em", "qk%d" % b], w=["kin%d" % b])
            sv, svid = bankB()
            for h in range(4):
                P("pe", lambda e, h=h: e.matmul(sv[:, h * 128:(h + 1) * 128], lhsT=wspT[:, h * 128:(h + 1) * 128],
                                                rhs=vlnb[b][:, h * 128:(h + 1) * 128], start=True, stop=True),
                  r=["wspT", "vlnb%d" % b], w=[svid])
            P("dve", lambda e: e.tensor_tensor(out=TS[:].rearrange("p (h c) -> p h c", h=4), in0=sv[:, :].rearrange("p (h c) -> p h c", h=4),
                                               in1=bsp[:, 0:4].unsqueeze(2).to_broadcast([128, 4, 128]), op=ALU.add),
              r=[svid, "bsp"], w=["TS"])
            P("dve", lambda e: e.tensor_tensor(out=cat[:, 0:512], in0=TS[:], in1=U_[b][:], op=ALU.mult),
              r=["TS", "U%d" % b], w=["cata"])
            yield
            ph, pid = ptbank()
            for k in range(8):
                P("pe", lambda e, ph=ph, k=k: e.transpose(out=ph[:, k * 128:(k + 1) * 128], in_=qin[b][:, k * 128:(k + 1) * 128],
                                                          identity=ident[:]), r=["qin%d_0" % b, "qin%d_1" % b, "ident"], w=[pid])
            P("act", lambda e, ph=ph: e.activation(out=qT[b][:], in_=ph[:, :].rearrange("p (k t) -> p k t", k=8), func=AF.Copy),
              r=[pid], w=["qT%d" % b])
            ph2, pid2 = ptbank()
            for k in range(4):
                P("pe", lambda e, k=k: e.transpose(out=ph2[:, k * 128:(k + 1) * 128], in_=kin[b][:, k * 128:(k + 1) * 128],
                                                   identity=ident[:]), r=["kin%d" % b, "ident"], w=[pid2])
            P("dve", lambda e: e.tensor_copy(out=kT[b][:], in_=ph2[:, 0:512].rearrange("p (k t) -> p k t", k=4)), r=[pid2], w=["kT%d" % b])
            yield
            for (t, dstb, dstid, mk, mkid) in ((0, scf[b], "scf%d" % b, maskF, "maskF"), (1, scb[b], "scb%d" % b, maskB, "maskB")):
                sc, scid = bankB()
                for h in range(4):
                    P("pe", lambda e, h=h, sc=sc, t=t: e.matmul(sc[:, h * 128:(h + 1) * 128], lhsT=kT[b][:, t * 2 + h // 2, :],
                                                                rhs=qT[b][:, t * 4 + h, :], start=True, stop=True),
                      r=["qT%d" % b, "kT%d" % b], w=[scid])
                P("dve", lambda e, sc=sc, dstb=dstb, mk=mk: e.tensor_tensor(out=dstb[:], in0=sc[:, :], in1=mk[:], op=ALU.mult),
                  r=[scid, mkid], w=[dstid])
            kv, kvid = bankB()
            vget, vid, _ = vbsrc(job, n, b)
            for pr in range(2):
                P("pe", lambda e, pr=pr: e.matmul(kv[:, pr * 256:(pr + 1) * 256], lhsT=kin[b][:, pr * 128:(pr + 1) * 128],
                                                  rhs=vget(pr * 256, (pr + 1) * 256), start=True, stop=True),
                  r=["kin%d" % b, vid], w=[kvid])
            yield
            sfc = n % 2
            ob, obid = bankB()
            for h in range(4):
                pr, hh = h // 2, h % 2
                oc = ob[:, h * 128:(h + 1) * 128]
                steps = ["scf", "scb"] + (["sf"] if n > 0 else []) + ([] if last_in_seq else ["sb"])
                for i, kind in enumerate(steps):
                    f_, l_ = (i == 0), (i == len(steps) - 1)
                    if kind == "scf":
                        P("pe", lambda e, oc=oc, h=h, f_=f_, l_=l_: e.matmul(oc, lhsT=scf[b][:, h * 128:(h + 1) * 128],
                                                                             rhs=vget(h * 128, (h + 1) * 128), start=f_, stop=l_),
                          r=["scf%d" % b, vid], w=[obid])
                    elif kind == "scb":
                        P("pe", lambda e, oc=oc, h=h, f_=f_, l_=l_: e.matmul(oc, lhsT=scb[b][:, h * 128:(h + 1) * 128],
                                                                             rhs=vget(h * 128, (h + 1) * 128), start=f_, stop=l_),
                          r=["scb%d" % b, vid], w=[obid])
                    elif kind == "sf":
                        P("pe", lambda e, oc=oc, pr=pr, hh=hh, f_=f_, l_=l_: e.matmul(
                            oc, lhsT=qT[b][:, 2 * pr + hh, :],
                            rhs=Sf[sfc][:, pr * 256 + hh * 128:pr * 256 + (hh + 1) * 128], start=f_, stop=l_),
                          r=["qT%d" % b, "Sf%d" % sfc], w=[obid])
                    else:
                        P("pe", lambda e, oc=oc, pr=pr, hh=hh, f_=f_, l_=l_: e.matmul(
                            oc, lhsT=qT[b][:, 4 + 2 * pr + hh, :],
                            rhs=Sb[:, n, pr * 128:(pr + 1) * 128], start=f_, stop=l_),
                          r=["qT%d" % b, "Sb%d_%d0" % (n, pr), "Sb%d_%d1" % (n, pr)], w=[obid])
            if n == 0:
                P("dve", lambda e: e.tensor_copy(out=Tst[:], in_=kv[:, :]), r=[kvid], w=["Tst0", "Tst1"])
            else:
                for pr in range(2):
                    P("dve", lambda e, pr=pr: e.scalar_tensor_tensor(out=Tst[:, pr * 256:(pr + 1) * 256], in0=Tst[:, pr * 256:(pr + 1) * 256],
                                                                     scalar=dec[dprev][:, pr:pr + 1], in1=kv[:, pr * 256:(pr + 1) * 256],
                                                                     op0=ALU.mult, op1=ALU.add),
                      r=[kvid, "Tst%d" % pr, "dec%d" % dprev], w=["Tst%d" % pr])
            if n + 1 < nown:
                P("dve", lambda e: e.tensor_tensor(out=Sf[1 - sfc][:].rearrange("p (r c) -> p r c", r=2),
                                                   in0=Tst[:].rearrange("p (r c) -> p r c", r=2),
                                                   in1=dec[dcur][:, 0:2].unsqueeze(2).to_broadcast([128, 2, 256]), op=ALU.mult),
                  r=["Tst0", "Tst1", "dec%d" % dcur], w=["Sf%d" % (1 - sfc)])
            P("act", lambda e: e.activation(out=scr[:], in_=ob[:, :], func=AF.Square), r=[obid], w=["scr"])
            P("dve", lambda e: e.reduce_sum(out=sm[ss][:, 20:24], in_=scr[:].rearrange("p (h c) -> p h c", h=4), axis=AX.X),
              r=["scr"], w=["sm%dh" % ss])
            P("act", lambda e: e.activation(out=sm[ss][:, 24:28], in_=sm[ss][:, 20:24], func=AF.Ln, scale=1.0 / 128, bias=EPS),
              r=["sm%dh" % ss], w=["sm%di" % ss])
            P("act", lambda e: e.activation(out=sm[ss][:, 28:32], in_=sm[ss][:, 24:28], func=AF.Exp, scale=-0.5),
              r=["sm%di" % ss], w=["sm%dj" % ss])
            P("dve", lambda e: e.tensor_tensor(out=ON[:].rearrange("p (h c) -> p h c", h=4), in0=ob[:, :].rearrange("p (h c) -> p h c", h=4),
                                               in1=sm[ss][:, 28:32].unsqueeze(2).to_broadcast([128, 4, 128]), op=ALU.mult),
              r=[obid, "sm%dj" % ss], w=["ON"])
            P("dve", lambda e: e.tensor_tensor(out=cat[:, 512:1024], in0=ON[:], in1=ZB[b][:], op=ALU.mult),
              r=["ON", "ZB%d" % b], w=["catb"])
            yield
            yield
            ph, pid = ptbank()
            for k in range(8):
                P("pe", lambda e, ph=ph, k=k: e.transpose(out=ph[:, k * 128:(k + 1) * 128], in_=cat[:, k * 128:(k + 1) * 128],
                                                          identity=ident[:]), r=["cata", "catb", "ident"], w=[pid])
            P("dve", lambda e, ph=ph: e.tensor_copy(out=catT[:], in_=ph[:, :].rearrange("p (k t) -> p k t", k=8)), r=[pid], w=["catT"])
            yield
            ybk = []
            for c in range(2):
                bk, bid = pb[4 + c], "pb%d" % (4 + c)
                for k in range(8):
                    P("pe", lambda e, k=k, bk=bk, c=c: e.matmul(bk[:, :], lhsT=catT[:, k, :], rhs=Wo[:, k, c * 512:(c + 1) * 512],
                                                                start=(k == 0), stop=(k == 7)),
                      r=["catT", "Wo"], w=[bid])
                ybk.append((bk, bid))
            pend[g] = (ybk, ss, s3, or0 + n * 128)
            yield

        pend = {}

        def p2_C(g):
            ybk, ss, s3, orow = pend.pop(g)
            for c, (bk, bid) in enumerate(ybk):
                P("act", lambda e, bk=bk, c=c: e.activation(out=junky[:, c * 512:(c + 1) * 512], in_=bk[:, :], func=AF.Square,
                                                            accum_out=sm[ss][:, 3 + c:4 + c]), r=[bid], w=["sm%dk%d" % (ss, c), "junky%d" % c])
            yield
            P("dve", lambda e: e.tensor_tensor(out=sm[ss][:, 5:6], in0=sm[ss][:, 3:4], in1=sm[ss][:, 4:5], op=ALU.add),
              r=["sm%dk0" % ss, "sm%dk1" % ss], w=["sm%dl" % ss])
            P("act", lambda e: e.activation(out=sm[ss][:, 6:7], in_=sm[ss][:, 5:6], func=AF.Ln, scale=1.0 / D, bias=EPS),
              r=["sm%dl" % ss], w=["sm%dm" % ss])
            P("act", lambda e: e.activation(out=sm[ss][:, 7:8], in_=sm[ss][:, 6:7], func=AF.Exp, scale=-0.5),
              r=["sm%dm" % ss], w=["sm%dn" % ss])
            yield
            for c, (bk, bid) in enumerate(ybk):
                P("dve", lambda e, bk=bk, c=c: e.scalar_tensor_tensor(out=yt[:, c * 512:(c + 1) * 512], in0=bk[:, :], scalar=sm[ss][:, 7:8],
                                                                      in1=gpost[:, c * 512:(c + 1) * 512], op0=ALU.mult, op1=ALU.mult),
                  r=[bid, "sm%dn" % ss, "gpost"], w=["yt%d" % c])
            S.op("pool", lambda e: e.dma_start(out=y_d[orow:orow + 128, :], in_=yt[:], accum_op=ALU.add),
                 reads=["yt0", "yt1", "yrow%d" % orow], writes=["yrow%d" % orow], dma_key="yst")
            yield

        tiles = []
        for job in jobs:
            ntot, nown, xr0, _ = job
            for i, n in enumerate(range(ntot - 1, 0, -1)):
                tiles.append(("p1", job, n, i == 0))
            for n in range(nown):
                tiles.append(("p2", job, n, False))
        first_p2 = next(i for i, t in enumerate(tiles) if t[0] == "p2")
        AHEAD = 2

        def xrow(t):
            return t[1][2] + t[2] * 128

        def genA(i):
            kind, job, n, first = tiles[i]
            return (p1_A if kind == "p1" else p2_A)(job, n, first, i)

        def genB(i):
            kind, job, n, first = tiles[i]
            return (p1_B if kind == "p1" else p2_B)(job, n, first, i)

        def run_interleaved(gens):
            gens = [g_ for g_ in gens if g_ is not None]
            while gens:
                for g_ in list(gens):
                    try:
                        next(g_)
                    except StopIteration:
                        gens.remove(g_)

        def orow_of(t):
            return (t[1][3] + t[2] * 128) if t[0] == "p2" else None

        for i in range(min(AHEAD, len(tiles))):
            load_x(xrow(tiles[i]), i, orow_of(tiles[i]))
        for i in range(min(2, len(tiles))):
            front1(i)
            front2(i)
        run_interleaved([genA(0)])
        def genC(i):
            return p2_C(i) if (i >= 0 and i in pend) else None

        for i in range(len(tiles) + 1):
            if i + AHEAD < len(tiles):
                load_x(xrow(tiles[i + AHEAD]), i + AHEAD, orow_of(tiles[i + AHEAD]))
            if i + 1 == first_p2:
                while deferred:
                    deferred.pop(0)()
            elif deferred and i < len(tiles) and tiles[i][0] == "p1":
                deferred.pop(0)()
            run_interleaved([genC(i - 1),
                             genB(i) if i < len(tiles) else None,
                             genA(i + 1) if i + 1 < len(tiles) else None,
                             genF(i + 2) if i + 2 < len(tiles) else None])
        S.emit(st)
        nc._mk_total_ops = S.total
    return nc


def _layout(flip, xsamp, xpr, norm_pre, w_in, w_sp, b_sp, g_v_a, w_gk_fwd, b_gk_fwd, w_gk_bwd, b_gk_bwd,
            g_norm_b, w_out, norm_post):
    w_in_c = w_in[0]
    lr_f, lr_b = w_in_c[:, 3072:3088], w_in_c[:, 3088:3104]
    wsp, bsp_ = w_sp[0], b_sp[0]
    gf, bf, gb, bb = w_gk_fwd[0], b_gk_fwd[0], w_gk_bwd[0], b_gk_bwd[0]
    if flip:
        xsamp = xsamp[::-1]
        xpr = xpr[::-1]
        lr_f, lr_b = lr_b, lr_f
        wsp = wsp[:, ::-1, ::-1]
        bsp_ = bsp_[:, ::-1]
        gf, bf, gb, bb = gb, bb, gf, bf
    w_in_c = np.concatenate([w_in_c[:, :2048], lr_f, lr_b, w_in_c[:, 2048:3072]], axis=1)
    wgk = np.zeros((33, 512), np.float32)
    wgk[0:16, 0:256] = gf
    wgk[16:32, 256:512] = gb
    wgk[32, 0:256] = bf
    wgk[32, 256:512] = bb
    return {
        "x": np.ascontiguousarray(np.concatenate([xsamp, xpr], axis=0), dtype=np.float32),
        "w_in": np.ascontiguousarray(w_in_c, dtype=np.float32),
        "w_out": np.ascontiguousarray(w_out[0], dtype=np.float32),
        "gpre": np.ascontiguousarray(norm_pre[0].reshape(8, 128).T, dtype=np.float32),
        "wspT": np.ascontiguousarray(np.transpose(wsp, (2, 0, 1)).reshape(128, 512), dtype=np.float32),
        "bsp": np.ascontiguousarray(bsp_.T, dtype=np.float32),
        "gv": np.ascontiguousarray(g_v_a[0], dtype=np.float32),
        "wgk": wgk,
        "gnb": np.ascontiguousarray(np.tile(g_norm_b[0], 4), dtype=np.float32),
        "gpost": np.ascontiguousarray(norm_post[0], dtype=np.float32),
    }


def kernel(**inputs):
    inputs = {k: np.asarray(v) for k, v in inputs.items()}
    nc = build_nc(FULL_JOBS)
    xp, xsm = inputs.pop("x_prompt"), inputs.pop("x_sample")
    in_maps = [_layout(c % 2 == 1, xsm[c // 2], xp[c], **inputs) for c in range(8)]
    res = run_bass_kernel_spmd(nc, in_maps, core_ids=list(range(8)))
    y_prompt = np.empty((8, 2048, D), np.float32)
    y_sample = np.empty((4, 8192, D), np.float32)
    for c in range(8):
        y = np.asarray(res.results[c]["y"])
        ys, yp = y[:4096], y[4096:6144]
        if c % 2 == 1:
            y_sample[c // 2, 4096:] = ys[::-1]
            y_prompt[c] = yp[::-1]
        else:
            y_sample[c // 2, :4096] = ys
            y_prompt[c] = yp
    return (y_prompt, y_sample)
```

```python
import math
from contextlib import ExitStack

import numpy as np
import concourse.bass as bass
import concourse.mybir as mybir
from concourse.bass_utils import run_bass_kernel_spmd

F32 = mybir.dt.float32
BF16 = mybir.dt.bfloat16
AF = mybir.ActivationFunctionType
ALU = mybir.AluOpType
AX = mybir.AxisListType

D = 1024
DIN = 3104
EPS = 1e-6
C1 = math.sqrt(2.0 / math.pi)
C2 = 0.044715

FULL_JOBS = [(64, 32, 0, 0), (16, 16, 8192, 4096)]


class Sched:
    ENGS = ("pe", "act", "dve", "pool", "sp")
    SYNC_LAT = 0.12
    ACT_SWITCH = 1.3
    PRIO = "rank"

    def __init__(self, nc):
        self.nc = nc
        self.all = []
        self.last_w = {}
        self.readers = {}
        self.total = 0

    def op(self, eng, fn, reads=(), writes=(), dma_key=None, cost=None, lat=None, tset=None, group=None):
        uid = len(self.all)
        self.total += 1
        deps = set()
        for b in reads:
            d = self.last_w.get(b)
            if d is not None:
                deps.add(d)
        for b in writes:
            d = self.last_w.get(b)
            if d is not None:
                deps.add(d)
            deps.update(self.readers.get(b, ()))
        deps.discard(uid)
        for b in reads:
            self.readers.setdefault(b, []).append(uid)
        for b in writes:
            self.last_w[b] = uid
            self.readers[b] = []
        if eng == "sp":
            assert dma_key is not None
        if dma_key is not None:
            cost = 0.1 if eng == "sp" else 1.0
            lat = 3.0
        if cost is None:
            cost, tset = self._estimate(eng, fn)
        if lat is None:
            lat = cost
        self.all.append(dict(eng=eng, fn=fn, deps=deps, dma_key=dma_key, cost=cost, lat=lat, tset=tset,
                             sig=False, cnt=None, group=group))

    class _Probe:
        def __getattr__(self, name):
            def rec(*a, **k):
                self.call = (name, a, k)
                return None
            return rec

    def _estimate(self, eng, fn):
        pr = Sched._Probe()
        fn(pr)
        name, a, k = pr.call
        out = k.get("out", a[0] if a else None)
        cols = 1
        for d in tuple(out.shape)[1:]:
            cols *= int(d)
        tset = None
        if eng == "pe":
            return max(0.066, cols / 2100.0), None
        if eng == "act":
            f = k.get("func")
            if f in (AF.Exp, AF.Ln):
                tset = 6
            elif f == AF.Silu:
                tset = 18
            elif f == AF.Gelu_apprx_tanh:
                tset = 11
            return 0.22 + cols * 0.00085, tset
        if eng == "dve":
            return 0.12 + cols * 0.00105, None
        if eng == "sp":
            return 0.1, None
        return 0.3 + cols * 0.002, None

    def schedule(self):
        import heapq
        ops = self.all
        n = len(ops)
        ndeps = [len(o["deps"]) for o in ops]
        users = [[] for _ in range(n)]
        for u, o in enumerate(ops):
            for d in o["deps"]:
                users[d].append(u)
        ready_t = [0.0] * n
        fin = [0.0] * n
        future = {e: [] for e in self.ENGS}
        avail = {e: [] for e in self.ENGS}
        free_t = {e: 0.0 for e in self.ENGS}
        order = {e: [] for e in self.ENGS}
        cur_set = [None]
        prio = list(range(n))
        if self.PRIO == "rank":
            rank = [0.0] * n
            for u in range(n - 1, -1, -1):
                m = 0.0
                for v in users[u]:
                    if rank[v] > m:
                        m = rank[v]
                rank[u] = ops[u]["cost"] + m
            top = max(rank)
            prio = [(top - rank[u]) for u in range(n)]
        for u, o in enumerate(ops):
            if ndeps[u] == 0:
                heapq.heappush(future[o["eng"]], (0.0, u))
        groups = {}
        for u, o in enumerate(ops):
            if o["group"] is not None:
                groups.setdefault(o["group"], []).append(u)
        scheduled = [False] * n
        done = 0

        def commit(u, e, t):
            o = ops[u]
            c = o["cost"]
            if e == "act" and o["tset"] is not None and cur_set[0] != o["tset"]:
                c += self.ACT_SWITCH
                cur_set[0] = o["tset"]
            start = max(t, free_t[e], ready_t[u])
            free_t[e] = start + c
            fin[u] = start + (o["lat"] if o["dma_key"] is not None else c)
            order[e].append(u)
            scheduled[u] = True
            for v in users[u]:
                ov = ops[v]
                rt = fin[u] + (0.0 if ov["eng"] == e else self.SYNC_LAT)
                if rt > ready_t[v]:
                    ready_t[v] = rt
                ndeps[v] -= 1
                if ndeps[v] == 0:
                    heapq.heappush(future[ov["eng"]], (ready_t[v], v))

        while done < n:
            best = None
            for e in self.ENGS:
                fu, av = future[e], avail[e]
                while fu and (scheduled[fu[0][1]] or fu[0][0] <= free_t[e]):
                    v_ = heapq.heappop(fu)[1]
                    if not scheduled[v_]:
                        heapq.heappush(av, (prio[v_], v_))
                while av and scheduled[av[0][1]]:
                    heapq.heappop(av)
                if av:
                    cand = (free_t[e], av[0][1], e)
                elif fu:
                    cand = (fu[0][0], fu[0][1], e)
                else:
                    continue
                if best is None or cand < best:
                    best = cand
            t, u, e = best
            if avail[e] and avail[e][0][1] == u:
                heapq.heappop(avail[e])
            else:
                heapq.heappop(future[e])
            commit(u, e, t)
            done += 1
            g_ = ops[u]["group"]
            if g_ is not None:
                for v in groups[g_]:
                    if not scheduled[v]:
                        assert ndeps[v] == 0 and ops[v]["eng"] == e, "group members must share engine and inputs"
                        commit(v, e, free_t[e])
                        done += 1
        self.order = order
        self.est_us = max(free_t.values())

    def emit(self, stack):
        nc = self.nc
        self.schedule()
        ops = self.all
        pos = {}
        for e in self.ENGS:
            for i, u in enumerate(self.order[e]):
                pos[u] = i
        for u, o in enumerate(ops):
            sd = {}
            for d in o["deps"]:
                de = ops[d]["eng"]
                if de == "pe" and o["eng"] == "pe":
                    continue
                key = ("dma", ops[d]["dma_key"]) if ops[d]["dma_key"] is not None else (de, None)
                if key not in sd or pos[sd[key]] < pos[d]:
                    sd[key] = d
            o["sdeps"] = sd
            for d in sd.values():
                ops[d]["sig"] = True
        sems = {}
        for e in ("pe", "act", "dve", "pool"):
            sems[e] = stack.enter_context(nc.semaphore("s_" + e))
            c = 0
            for u in self.order[e]:
                if ops[u]["sig"] and ops[u]["dma_key"] is None:
                    c += 1
                    ops[u]["cnt"] = c
        dsem, dcount, dkey_eng = {}, {}, {}
        for e in self.ENGS:
            for u in self.order[e]:
                o = ops[u]
                k = o["dma_key"]
                if k is None:
                    continue
                assert dkey_eng.setdefault(k, e) == e, "a DMA key must stay on one queue"
                if k not in dsem:
                    dsem[k] = stack.enter_context(nc.semaphore("d_" + str(k)))
                    dcount[k] = 0
                dcount[k] += 16
                o["cnt"] = dcount[k]
                o["sem"] = dsem[k]
        block = stack.enter_context(nc.Block())
        engmap = {"pe": block.tensor, "act": block.scalar, "dve": block.vector,
                  "pool": block.gpsimd, "sp": block.sync}
        final_waits = [(dsem[k], dcount[k]) for k in dsem]
        for e in self.ENGS:
            lst = self.order[e]

            def body(eng, lst=lst, e=e):
                waited = {}
                for u in lst:
                    o = ops[u]
                    for (key, d) in o["sdeps"].items():
                        src = ops[d]
                        if waited.get(key, 0) >= src["cnt"]:
                            continue
                        waited[key] = src["cnt"]
                        eng.wait_ge(src["sem"] if key[0] == "dma" else sems[key[0]], src["cnt"])
                    ins = o["fn"](eng)
                    if o["dma_key"] is not None:
                        ins.then_inc(o["sem"], 16)
                    elif o["sig"]:
                        ins.then_inc(sems[e], 1)
                if e == "sp":
                    for (s_, c_) in final_waits:
                        eng.wait_ge(s_, c_)
            engmap[e](body)


def build_nc(jobs):
    n_x_rows = max(j[2] + j[0] * 128 for j in jobs)
    n_o_rows = max(j[3] + j[1] * 128 for j in jobs)
    max_own = max(j[1] for j in jobs)
    nc = bass.Bass("TRN2", target_bir_lowering=False)
    x_d = nc.dram_tensor("x", [n_x_rows, D], F32, kind="ExternalInput").ap()
    y_d = nc.dram_tensor("y", [n_o_rows, D], F32, kind="ExternalOutput").ap()
    win_d = nc.dram_tensor("w_in", [D, DIN], F32, kind="ExternalInput").ap()
    wout_d = nc.dram_tensor("w_out", [D, D], F32, kind="ExternalInput").ap()
    gpre_d = nc.dram_tensor("gpre", [128, 8], F32, kind="ExternalInput").ap()
    wspT_d = nc.dram_tensor("wspT", [128, 4 * 128], F32, kind="ExternalInput").ap()
    bsp_d = nc.dram_tensor("bsp", [128, 4], F32, kind="ExternalInput").ap()
    gv_d = nc.dram_tensor("gv", [512], F32, kind="ExternalInput").ap()
    wgk_d = nc.dram_tensor("wgk", [33, 512], F32, kind="ExternalInput").ap()
    gnb_d = nc.dram_tensor("gnb", [512], F32, kind="ExternalInput").ap()
    gpost_d = nc.dram_tensor("gpost", [D], F32, kind="ExternalInput").ap()

    with ExitStack() as st:
        def T(name, shape, dt=F32):
            return st.enter_context(nc.sbuf_tensor(name, shape, dt))

        S = Sched(nc)
        ident = T("ident", [128, 128], BF16)
        maskF = T("maskF", [128, 512], BF16)
        maskB = T("maskB", [128, 512], BF16)
        triF = T("triF", [128, 128], BF16)
        triB = T("triB", [128, 128], BF16)
        negcol = T("negcol", [128, 1], BF16)
        Wb = T("Wb", [128, 8, DIN], BF16)
        Wo = T("Wo", [128, 8, D], BF16)
        yt = T("yt", [128, D], F32)
        NB = 2
        UV = [T("UV%d" % i, [128, D], F32) for i in range(NB)]
        ZZ = [T("ZZ%d" % i, [128, D], F32) for i in range(NB)]
        gpre = T("gpre_s", [128, 8], F32)
        wspT = T("wspT_s", [128, 512], BF16)
        bsp = T("bsp_s", [128, 4], F32)
        gvb = T("gvb", [128, 512], F32)
        wgk = T("wgk_s", [33, 512], BF16)
        gnb = T("gnb_s", [128, 512], F32)
        gpost = T("gpost_s", [128, D], F32)
        lrT = T("lrT", [33, 128], BF16)

        def P(e, f, r=(), w=(), c=None, ts=None, g=None):
            S.op(e, f, reads=r, writes=w, cost=c, tset=ts, group=g)

        P("pool", lambda e: e.memset(ident[:], 1.0), w=["ident"])
        P("pool", lambda e: e.affine_select(out=ident[:], in_=ident[:], pattern=[[-1, 128]], compare_op=ALU.is_equal,
                                            fill=0.0, base=0, channel_multiplier=1), r=["ident"], w=["ident"])
        P("pool", lambda e: e.memset(maskF[:], 1.0), w=["maskF"])
        P("pool", lambda e: e.affine_select(out=maskF[:].rearrange("p (h c) -> p h c", h=4), in_=maskF[:].rearrange("p (h c) -> p h c", h=4),
                                            pattern=[[0, 4], [1, 128]], compare_op=ALU.is_ge, fill=0.0, base=0,
                                            channel_multiplier=-1), r=["maskF"], w=["maskF"])
        P("pool", lambda e: e.memset(maskB[:], 1.0), w=["maskB"])
        P("pool", lambda e: e.affine_select(out=maskB[:].rearrange("p (h c) -> p h c", h=4), in_=maskB[:].rearrange("p (h c) -> p h c", h=4),
                                            pattern=[[0, 4], [-1, 128]], compare_op=ALU.is_gt, fill=0.0, base=0,
                                            channel_multiplier=1), r=["maskB"], w=["maskB"])
        P("pool", lambda e: e.memset(triF[:], -1.0 / 16), w=["triF"])
        P("pool", lambda e: e.affine_select(out=triF[:], in_=triF[:], pattern=[[1, 128]], compare_op=ALU.is_ge, fill=0.0,
                                            base=0, channel_multiplier=-1), r=["triF"], w=["triF"])
        P("pool", lambda e: e.memset(triB[:], -1.0 / 16), w=["triB"])
        P("pool", lambda e: e.affine_select(out=triB[:], in_=triB[:], pattern=[[-1, 128]], compare_op=ALU.is_ge, fill=0.0,
                                            base=0, channel_multiplier=1), r=["triB"], w=["triB"])
        P("pool", lambda e: e.memset(negcol[:], -1.0 / 16), w=["negcol"])
        P("pool", lambda e: e.memset(lrT[32:33, :], 1.0), w=["lrT_one"])

        S.op("sp", lambda e: e.dma_start(out=gpre[:], in_=gpre_d[:, :]), writes=["gpre"], dma_key="gpre")
        NC = 25
        vbc = T("vbc", [128, NC, 512], BF16)
        stg = [(UV[0], ["U0", "V0"], "stg0"), (ZZ[0], ["ZA0", "ZB0"], "stg1"),
               (UV[1], ["U1", "V1"], "stg2"), (ZZ[1], ["ZA1", "ZB1"], "stg3")]
        for j in range(4):
            view = vbc[:, 4 * j:4 * j + 4, :].bitcast(F32).rearrange("p a c -> p (a c)")
            stg.append((view, ["vbc%d" % (4 * j + i_) for i_ in range(4)], "stg%d" % (4 + j)))
        NSTG = [len(stg)]
        stg_n = [0]

        def wcol_ids(c0, c1):
            ids = []
            for (a, b_, nm) in ((0, 1792, "WbA"), (1792, 2592, "WbB"), (2592, 3104, "WbC")):
                if c0 < b_ and c1 > a:
                    ids.append(nm)
            return ids

        def load_win_piece(k, c0, c1):
            buf, ids, key = stg[stg_n[0] % NSTG[0]]
            use_dve = (stg_n[0] % 2 == 0)
            stg_n[0] += 1
            S.op("sp", lambda e: e.dma_start(out=buf[:, 0:c1 - c0], in_=win_d[k * 128:(k + 1) * 128, c0:c1]), writes=ids, dma_key=key)
            if use_dve:
                P("dve", lambda e: e.tensor_scalar(out=Wb[:, k, c0:c1], in0=buf[:, 0:c1 - c0], scalar1=gpre[:, k:k + 1], scalar2=None, op0=ALU.mult),
                  r=ids + ["gpre"], w=wcol_ids(c0, c1))
            else:
                P("act", lambda e: e.activation(out=Wb[:, k, c0:c1], in_=buf[:, 0:c1 - c0], func=AF.Identity, scale=gpre[:, k:k + 1]),
                  r=ids + ["gpre"], w=wcol_ids(c0, c1))

        def load_wout_piece(k):
            buf, ids, key = stg[stg_n[0] % NSTG[0]]
            use_dve = (stg_n[0] % 2 == 0)
            stg_n[0] += 1
            S.op("sp", lambda e: e.dma_start(out=buf[:, :], in_=wout_d[k * 128:(k + 1) * 128, :]), writes=ids, dma_key=key)
            if use_dve:
                P("dve", lambda e: e.tensor_copy(out=Wo[:, k, :], in_=buf[:, :]), r=ids, w=["Wo"])
            else:
                P("act", lambda e: e.activation(out=Wo[:, k, :], in_=buf[:, :], func=AF.Copy), r=ids, w=["Wo"])

        for k in range(8):
            load_win_piece(k, 1792, 2592)
        NSTG[0] = 4
        deferred = []
        for (c0, c1) in ((0, 1024), (1024, 1792), (2592, 3104)):
            for k in range(8):
                deferred.append(lambda k=k, c0=c0, c1=c1: load_win_piece(k, c0, c1))
        for k in range(8):
            deferred.append(lambda k=k: load_wout_piece(k))
        S.op("sp", lambda e: e.dma_start(out=yt[0:33, 0:512], in_=wgk_d[:, :]), writes=["yt0", "yt1"], dma_key="ytst")
        P("dve", lambda e: e.tensor_copy(out=wgk[:], in_=yt[0:33, 0:512]), r=["yt0", "yt1"], w=["wgk"])
        S.op("sp", lambda e: e.dma_start(out=yt[:, 0:512], in_=wspT_d[:, :]), writes=["yt0", "yt1"], dma_key="ytst")
        P("dve", lambda e: e.tensor_copy(out=wspT[:], in_=yt[:, 0:512]), r=["yt0", "yt1"], w=["wspT"])
        S.op("sp", lambda e: e.dma_start(out=bsp[:], in_=bsp_d[:, :]), writes=["bsp"], dma_key="bsp")
        S.op("sp", lambda e: e.dma_start(out=gvb[:], in_=gv_d.partition_broadcast(128)), writes=["gvb"], dma_key="gvb")
        S.op("sp", lambda e: e.dma_start(out=gnb[:], in_=gnb_d.partition_broadcast(128)), writes=["gnb"], dma_key="gnb")
        S.op("sp", lambda e: e.dma_start(out=gpost[:], in_=gpost_d.partition_broadcast(128)), writes=["gpost"], dma_key="gpost")

        NXS = 2
        xs = [T("xs%d" % i, [128, D], F32) for i in range(NXS)]
        xbf = [T("xbf%d" % i, [128, D], BF16) for i in range(2)]
        xT = [T("xT%d" % i, [128, 8, 128], BF16) for i in range(2)]
        junk = T("junk", [128, D], BF16)
        junky = T("junky", [128, D], BF16)
        sm = [T("sm%d" % i, [128, 32], F32) for i in range(4)]

        def slots(name, shape, dt=F32, n=NB):
            return [T("%s%d" % (name, i), shape, dt) for i in range(n)]
        U_ = [UV[i][:, 0:512] for i in range(NB)]
        V_ = [UV[i][:, 512:1024] for i in range(NB)]
        ZA = [ZZ[i][:, 0:512] for i in range(NB)]
        ZB = [ZZ[i][:, 512:1024] for i in range(NB)]
        def vbsrc(job, n, b):
            if 1 <= n < job[1] - 1 and n - 1 < NC:
                return (lambda lo, hi: vbc[:, n - 1, lo:hi]), "vbc%d" % (n - 1), True
            return (lambda lo, hi: vb[b][:, lo:hi]), "vb%d" % b, False
        TS = T("TS", [128, 512], F32)
        ON = T("ON", [128, 512], F32)
        qk = slots("qk", [128, 512])
        scr = T("scr", [128, 512], F32)
        scr2 = T("scr2", [128, 512], F32)
        Ep = T("Ep", [128, 512], F32)
        Em = T("Em", [128, 512], F32)
        vlnb = slots("vlnb", [128, 512], BF16)
        vb = slots("vb", [128, 512], BF16)
        lrs = slots("lrs", [128, 32], BF16)
        spb = slots("spb", [128, 512], BF16)
        qin = slots("qin", [128, 1024], BF16)
        kin = slots("kin", [128, 512], BF16)
        qT = slots("qT", [128, 8, 128], BF16)
        kT = slots("kT", [128, 4, 128], BF16)
        scf = slots("scf", [128, 512], BF16)
        scb = slots("scb", [128, 512], BF16)
        cat = T("cat", [128, D], BF16)
        catT = T("catT", [128, 8, 128], BF16)
        dec = [T("dec%d" % i, [128, 2], F32) for i in range(3)]
        Tst = T("Tst", [128, 512], F32)
        Sf = [T("Sf%d" % i, [128, 512], BF16) for i in range(2)]
        Sb = T("Sb", [128, max_own, 256], BF16)

        NPB = 6
        pb = [st.enter_context(nc.psum_tensor("pb%d" % i, [128, 512], F32)) for i in range(NPB)]
        pt = [st.enter_context(nc.psum_tensor("pt%d" % i, [128, 1024], BF16)) for i in range(2)]
        cnt = dict(bank=0, pth=0, tile=0, dec=0)
        for i in range(NB):
            P("pool", lambda e, i=i: e.memset(qin[i][:], 0.0), w=["qin%d_0" % i, "qin%d_1" % i])

        def bank():
            i = cnt["bank"] % NPB
            cnt["bank"] += 1
            return pb[i], "pb%d" % i

        def ptbank():
            i = cnt["pth"] % 2
            cnt["pth"] += 1
            return pt[i], "pt%d" % i

        def bankA():
            i = cnt["bank"] % 2
            cnt["bank"] += 1
            return pb[i], "pb%d" % i

        def bankB():
            i = 2 + cnt["bankb"] % 2
            cnt["bankb"] += 1
            return pb[i], "pb%d" % i
        cnt["bankb"] = 0

        def load_x(row0, g, orow=None):
            s3 = g % NXS
            S.op("sp", lambda e: e.dma_start(out=xs[s3][:], in_=x_d[row0:row0 + 128, :]), writes=["xs%d" % s3],
                 dma_key="xs%d" % s3)
            if orow is not None:
                S.op("sp", lambda e: e.dma_start(out=y_d[orow:orow + 128, :], in_=x_d[row0:row0 + 128, :]),
                     writes=["yrow%d" % orow, "ycpk%d" % (g % 4)], dma_key="ycp%d" % (g % 4))

        rstdc = T("rstdc", [128, 64], F32)
        lrc = T("lrc", [128, max_own, 32], BF16)

        def lrsrc(job, n, b):
            if 1 <= n < job[1]:
                return lrc[:, n, :], "lrc%d" % n, True
            return lrs[b][:], "lrs%d" % b, False

        def front1(g):
            s3, s2, ss = g % NXS, g % 2, g % 4
            kind, job, n, _ = tiles[g]
            own = (1 <= n < job[1])
            if kind == "p2" and own:
                rs, rsid = rstdc[:, n:n + 1], "rstdc%d" % n
            else:
                rs, rsid = (rstdc[:, n:n + 1], "rstdc%d" % n) if own else (sm[ss][:, 2:3], "sm%dc" % ss)
                P("act", lambda e: e.activation(out=junk[:], in_=xs[s3][:], func=AF.Square, accum_out=sm[ss][:, 0:1]),
                  r=["xs%d" % s3], w=["sm%da" % ss, "junk"])
                P("act", lambda e: e.activation(out=sm[ss][:, 1:2], in_=sm[ss][:, 0:1], func=AF.Ln, scale=1.0 / D, bias=EPS),
                  r=["sm%da" % ss], w=["sm%db" % ss])
                P("act", lambda e: e.activation(out=rs, in_=sm[ss][:, 1:2], func=AF.Exp, scale=-0.5),
                  r=["sm%db" % ss], w=[rsid])
            P("dve", lambda e: e.tensor_scalar(out=xbf[s2][:], in0=xs[s3][:], scalar1=rs, scalar2=None, op0=ALU.mult),
              r=["xs%d" % s3, rsid], w=["xbf%d" % s2])

        def front2(g):
            s2 = g % 2
            ph, pid = ptbank()
            for k in range(8):
                P("pe", lambda e, ph=ph, k=k: e.transpose(out=ph[:, k * 128:(k + 1) * 128],
                                                          in_=xbf[s2][:, k * 128:(k + 1) * 128], identity=ident[:]),
                  r=["xbf%d" % s2, "ident"], w=[pid])
            P("dve", lambda e, ph=ph: e.tensor_copy(out=xT[s2][:], in_=ph[:, :].rearrange("p (k t) -> p k t", k=8)),
              r=[pid], w=["xT%d" % s2])

        def genF(g):
            for _ in range(3):
                yield
            front1(g)
            yield
            yield
            front2(g)
            yield

        def inproj(s2, c0, c1):
            bk, bid = bankA()
            wids = wcol_ids(c0, c1)
            for k in range(8):
                P("pe", lambda e, k=k, bk=bk: e.matmul(bk[:, 0:c1 - c0], lhsT=xT[s2][:, k, :], rhs=Wb[:, k, c0:c1],
                                                       start=(k == 0), stop=(k == 7)),
                  r=["xT%d" % s2] + wids, w=[bid])
            return bk, bid

        def gate_chain(b, ncols, c_lo, lr_ap, lr_id):
            ph, pid = ptbank()
            P("pe", lambda e: e.transpose(out=ph[0:32, 0:128], in_=lr_ap, identity=ident[:]),
              r=[lr_id, "ident"], w=[pid])
            P("dve", lambda e: e.tensor_copy(out=lrT[0:32, :], in_=ph[0:32, 0:128]), r=[pid], w=["lrT"])
            bk, bid = bankB()
            P("pe", lambda e: e.matmul(bk[:, 0:ncols], lhsT=lrT[0:33, :], rhs=wgk[0:33, c_lo:c_lo + ncols], start=True, stop=True),
              r=["lrT", "lrT_one", "wgk"], w=[bid])
            P("act", lambda e: e.activation(out=scr2[:, 0:ncols], in_=bk[:, 0:ncols], func=AF.Exp, scale=-1.0),
              r=[bid], w=["scr2"])
            P("act", lambda e: e.activation(out=spb[b][:, 0:ncols], in_=scr2[:, 0:ncols], func=AF.Ln, bias=1.0),
              r=["scr2"], w=["spb%d" % b])

        def p1_A(job, n, first, g):
            b, s2 = g % NB, g % 2
            bk, bid = inproj(s2, 1792, 2080)
            lr_ap, lr_id, _ = lrsrc(job, n, b)
            P("act", lambda e: e.activation(out=lr_ap, in_=bk[:, 256:288], func=AF.Copy), r=[bid], w=[lr_id])
            P("act", lambda e: e.activation(out=qk[b][:, 256:512], in_=bk[:, 0:256], func=AF.Copy), r=[bid], w=["qk%d" % b])
            yield
            bk2, bid2 = inproj(s2, 2080, 2592)
            vget, vid, _ = vbsrc(job, n, b)
            P("dve", lambda e: e.tensor_copy(out=vget(0, 512), in_=bk2[:, :]), r=[bid2], w=[vid])
            yield

        def p1_B(job, n, first, g):
            ntot, nown, xr0, _ = job
            b = g % NB
            lr_ap, lr_id, _ = lrsrc(job, n, b)
            gate_chain(b, 256, 256, lr_ap, lr_id)
            yield
            bc, bcid = bankB()
            P("pe", lambda e: e.matmul(bc[:, 0:256], lhsT=triB[:], rhs=spb[b][:, 0:256], start=True, stop=True),
              r=["triB", "spb%d" % b], w=[bcid])
            bl, blid = bankB()
            for pr in range(2):
                P("pe", lambda e, pr=pr: e.matmul(bl[:, pr:pr + 1], lhsT=spb[b][:, pr * 128:(pr + 1) * 128], rhs=negcol[:, 0:1],
                                                  start=True, stop=True), r=["negcol", "spb%d" % b], w=[blid])
            P("act", lambda e: e.activation(out=Em[:, 0:256], in_=bc[:, 0:256], func=AF.Exp, scale=-1.0), r=[bcid], w=["Em"])
            dcur = cnt["dec"] % 3
            dprev = (cnt["dec"] - 1) % 3
            cnt["dec"] += 1
            P("act", lambda e: e.activation(out=dec[dcur][:, 0:2], in_=bl[:, 0:2], func=AF.Exp), r=[blid], w=["dec%d" % dcur])
            P("dve", lambda e: e.tensor_tensor(out=kin[b][:, 0:256], in0=Em[:, 0:256], in1=qk[b][:, 256:512], op=ALU.mult),
              r=["Em", "qk%d" % b], w=["kin%d" % b])
            yield
            kv, kvid = bankB()
            vget, vid, _ = vbsrc(job, n, b)
            for pr in range(2):
                P("pe", lambda e, pr=pr: e.matmul(kv[:, pr * 256:(pr + 1) * 256], lhsT=kin[b][:, pr * 128:(pr + 1) * 128],
                                                  rhs=vget(pr * 256, (pr + 1) * 256), start=True, stop=True),
                  r=["kin%d" % b, vid], w=[kvid])
            if first:
                P("dve", lambda e: e.tensor_copy(out=Tst[:], in_=kv[:, :]), r=[kvid], w=["Tst0", "Tst1"])
            else:
                for pr in range(2):
                    P("dve", lambda e, pr=pr: e.scalar_tensor_tensor(out=Tst[:, pr * 256:(pr + 1) * 256], in0=Tst[:, pr * 256:(pr + 1) * 256],
                                                                     scalar=dec[dprev][:, pr:pr + 1], in1=kv[:, pr * 256:(pr + 1) * 256],
                                                                     op0=ALU.mult, op1=ALU.add),
                      r=[kvid, "Tst%d" % pr, "dec%d" % dprev], w=["Tst%d" % pr])
            if 1 <= n <= nown:
                for hh in range(2):
                    rs = slice(hh * 64, (hh + 1) * 64)
                    P("dve", lambda e, hh=hh, rs=rs: e.tensor_tensor(
                        out=Sb[rs, n - 1, :].rearrange("p (r c) -> p r c", r=2),
                        in0=Tst[rs, :].rearrange("p (r x) -> p r x", r=2)[:, :, hh * 128:(hh + 1) * 128],
                        in1=dec[dcur][rs, 0:2].unsqueeze(2).to_broadcast([64, 2, 128]), op=ALU.mult),
                      r=["Tst0", "Tst1", "dec%d" % dcur], w=["Sb%d_0%d" % (n - 1, hh), "Sb%d_1%d" % (n - 1, hh)])
            yield

        def p2_A(job, n, first, g):
            b, s2, ss = g % NB, g % 2, g % 4
            if not lrsrc(job, n, b)[2]:
                bk, bid = inproj(s2, 2048, 2080)
                P("act", lambda e, bk=bk: e.activation(out=lrs[b][:], in_=bk[:, 0:32], func=AF.Copy), r=[bid], w=["lrs%d" % b])
                yield
            bk, bid = inproj(s2, 1536, 2048)
            P("act", lambda e, bk=bk: e.activation(out=qk[b][:], in_=bk[:, :], func=AF.Copy), r=[bid], w=["qk%d" % b])
            yield
            if not vbsrc(job, n, b)[2]:
                bk, bid = inproj(s2, 2080, 2592)
                P("dve", lambda e, bk=bk: e.tensor_copy(out=vb[b][:], in_=bk[:, :]), r=[bid], w=["vb%d" % b])
                yield
            bk, bid = inproj(s2, 1024, 1536)
            P("act", lambda e, bk=bk: e.activation(out=ZA[b][:], in_=bk[:, :], func=AF.Copy), r=[bid], w=["ZA%d" % b])
            yield
            bk, bid = inproj(s2, 2592, 3104)
            P("dve", lambda e, bk=bk: e.tensor_copy(out=ZB[b][:], in_=bk[:, :]), r=[bid], w=["ZB%d" % b])
            yield
            bk, bid = inproj(s2, 0, 512)
            P("act", lambda e, bk=bk: e.activation(out=U_[b][:], in_=bk[:, :], func=AF.Copy), r=[bid], w=["U%d" % b])
            yield
            bk, bid = inproj(s2, 512, 1024)
            P("dve", lambda e, bk=bk: e.tensor_copy(out=V_[b][:], in_=bk[:, :]), r=[bid], w=["V%d" % b])
            yield
            yield
            yield

            zin, uin = ["ZA%d" % b, "ZB%d" % b], ["U%d" % b, "V%d" % b]
            P("act", lambda e: e.activation(out=ZA[b][:], in_=ZA[b][:], func=AF.Silu), r=zin, w=["ZA%d" % b], g="sil%d" % g)
            P("act", lambda e: e.activation(out=ZB[b][:], in_=ZB[b][:], func=AF.Silu), r=zin, w=["ZB%d" % b], g="sil%d" % g)
            P("act", lambda e: e.activation(out=U_[b][:], in_=U_[b][:], func=AF.Gelu_apprx_tanh), r=uin, w=["U%d" % b], g="gel%d" % g)
            P("act", lambda e: e.activation(out=V_[b][:], in_=V_[b][:], func=AF.Gelu_apprx_tanh), r=uin, w=["V%d" % b], g="gel%d" % g)
            P("dve", lambda e: e.bn_stats(out=sm[ss][:, 8:14], in_=V_[b][:]), r=["V%d" % b], w=["sm%dd" % ss])
            P("dve", lambda e: e.bn_aggr(out=sm[ss][:, 14:16], in_=sm[ss][:, 8:14]), r=["sm%dd" % ss], w=["sm%de" % ss])
            P("act", lambda e: e.activation(out=sm[ss][:, 16:17], in_=sm[ss][:, 15:16], func=AF.Ln, bias=EPS), r=["sm%de" % ss], w=["sm%df" % ss])
            P("act", lambda e: e.activation(out=sm[ss][:, 17:18], in_=sm[ss][:, 16:17], func=AF.Exp, scale=-0.5), r=["sm%df" % ss], w=["sm%dg" % ss])
            P("dve", lambda e: e.tensor_scalar(out=V_[b][:], in0=V_[b][:], scalar1=sm[ss][:, 14:15], scalar2=sm[ss][:, 17:18],
                                               op0=ALU.subtract, op1=ALU.mult), r=["V%d" % b, "sm%de" % ss, "sm%dg" % ss], w=["V%d" % b])
            P("dve", lambda e: e.tensor_tensor(out=vlnb[b][:], in0=V_[b][:], in1=gvb[:], op=ALU.mult),
              r=["V%d" % b, "gvb"], w=["vlnb%d" % b])
            P("dve", lambda e: e.tensor_tensor(out=U_[b][:], in0=U_[b][:], in1=ZA[b][:], op=ALU.mult),
              r=["U%d" % b, "ZA%d" % b], w=["U%d" % b])
            P("dve", lambda e: e.tensor_tensor(out=ZB[b][:], in0=ZB[b][:], in1=gnb[:], op=ALU.mult),
              r=["ZB%d" % b, "gnb"], w=["ZB%d" % b])
            yield

        def p2_B(job, n, first, g):
            ntot, nown, xr0, or0 = job
            b, s3, ss = g % NB, g % NXS, g % 4
            last_in_seq = (n == ntot - 1)
            lr_ap, lr_id, _ = lrsrc(job, n, b)
            gate_chain(b, 512, 0, lr_ap, lr_id)
            yield
            bc, bcid = bankB()
            P("pe", lambda e: e.matmul(bc[:, 0:256], lhsT=triF[:], rhs=spb[b][:, 0:256], start=True, stop=True),
              r=["triF", "spb%d" % b], w=[bcid])
            P("pe", lambda e: e.matmul(bc[:, 256:512], lhsT=triB[:], rhs=spb[b][:, 256:512], start=True, stop=True),
              r=["triB", "spb%d" % b], w=[bcid])
            bl, blid = bankB()
            for pr in range(2):
                P("pe", lambda e, pr=pr: e.matmul(bl[:, pr:pr + 1], lhsT=spb[b][:, pr * 128:(pr + 1) * 128], rhs=negcol[:, 0:1],
                                                  start=True, stop=True), r=["negcol", "spb%d" % b], w=[blid])
            P("act", lambda e: e.activation(out=Ep[:], in_=bc[:, :], func=AF.Exp, bias=math.log(0.125)), r=[bcid], w=["Ep"])
            P("act", lambda e: e.activation(out=Em[:], in_=bc[:, :], func=AF.Exp, scale=-1.0), r=[bcid], w=["Em"])
            dcur = cnt["dec"] % 3
            dprev = (cnt["dec"] - 1) % 3
            cnt["dec"] += 1
            P("act", lambda e: e.activation(out=dec[dcur][:, 0:2], in_=bl[:, 0:2], func=AF.Exp), r=[blid], w=["dec%d" % dcur])
            for hh in range(2):
                P("dve", lambda e, hh=hh: e.tensor_tensor(
                    out=qin[b][:].rearrange("p (t r h c) -> p t r h c", t=2, r=2, h=2)[:, :, :, hh, hh * 64:(hh + 1) * 64],
                    in0=Ep[:].rearrange("p (t r h d) -> p t r h d", t=2, r=2, h=2)[:, :, :, hh, :],
                    in1=qk[b][:, 0:256].rearrange("p (r h d) -> p r h d", r=2, h=2)[:, :, hh, :].unsqueeze(1).to_broadcast([128, 2, 2, 64]),
                    op=ALU.mult), r=["Ep", "qk%d" % b], w=["qin%d_%d" % (b, hh)])
            P("dve", lambda e: e.tensor_tensor(out=kin[b][:].rearrange("p (t d) -> p t d", t=2), in0=Em[:].rearrange("p (t d) -> p t d", t=2),
                                               in1=qk[b][:, 256:512].unsqueeze(1).to_broadcast([128, 2, 256]), op=ALU.mult),
              r=["Em", "qk%d" % b], w=["kin%d" % b])
            sv, svid = bankB()
            for h in range(4):
                P("pe", lambda e, h=h: e.matmul(sv[:, h * 128:(h + 1) * 128], lhsT=wspT[:, h * 128:(h + 1) * 128],
                                                rhs=vlnb[b][:, h * 128:(h + 1) * 128], start=True, stop=True),
                  r=["wspT", "vlnb%d" % b], w=[svid])
            P("dve", lambda e: e.tensor_tensor(out=TS[:].rearrange("p (h c) -> p h c", h=4), in0=sv[:, :].rearrange("p (h c) -> p h c", h=4),
                                               in1=bsp[:, 0:4].unsqueeze(2).to_broadcast([128, 4, 128]), op=ALU.add),
              r=[svid, "bsp"], w=["TS"])
            P("dve", lambda e: e.tensor_tensor(out=cat[:, 0:512], in0=TS[:], in1=U_[b][:], op=ALU.mult),
              r=["TS", "U%d" % b], w=["cata"])
            yield
            ph, pid = ptbank()
            for k in range(8):
                P("pe", lambda e, ph=ph, k=k: e.transpose(out=ph[:, k * 128:(k + 1) * 128], in_=qin[b][:, k * 128:(k + 1) * 128],
                                                          identity=ident[:]), r=["qin%d_0" % b, "qin%d_1" % b, "ident"], w=[pid])
            P("act", lambda e, ph=ph: e.activation(out=qT[b][:], in_=ph[:, :].rearrange("p (k t) -> p k t", k=8), func=AF.Copy),
              r=[pid], w=["qT%d" % b])
            ph2, pid2 = ptbank()
            for k in range(4):
                P("pe", lambda e, k=k: e.transpose(out=ph2[:, k * 128:(k + 1) * 128], in_=kin[b][:, k * 128:(k + 1) * 128],
                                                   identity=ident[:]), r=["kin%d" % b, "ident"], w=[pid2])
            P("dve", lambda e: e.tensor_copy(out=kT[b][:], in_=ph2[:, 0:512].rearrange("p (k t) -> p k t", k=4)), r=[pid2], w=["kT%d" % b])
            yield
            for (t, dstb, dstid, mk, mkid) in ((0, scf[b], "scf%d" % b, maskF, "maskF"), (1, scb[b], "scb%d" % b, maskB, "maskB")):
                sc, scid = bankB()
                for h in range(4):
                    P("pe", lambda e, h=h, sc=sc, t=t: e.matmul(sc[:, h * 128:(h + 1) * 128], lhsT=kT[b][:, t * 2 + h // 2, :],
                                                                rhs=qT[b][:, t * 4 + h, :], start=True, stop=True),
                      r=["qT%d" % b, "kT%d" % b], w=[scid])
                P("dve", lambda e, sc=sc, dstb=dstb, mk=mk: e.tensor_tensor(out=dstb[:], in0=sc[:, :], in1=mk[:], op=ALU.mult),
                  r=[scid, mkid], w=[dstid])
            kv, kvid = bankB()
            vget, vid, _ = vbsrc(job, n, b)
            for pr in range(2):
                P("pe", lambda e, pr=pr: e.matmul(kv[:, pr * 256:(pr + 1) * 256], lhsT=kin[b][:, pr * 128:(pr + 1) * 128],
                                                  rhs=vget(pr * 256, (pr + 1) * 256), start=True, stop=True),
                  r=["kin%d" % b, vid], w=[kvid])
            yield
            sfc = n % 2
            ob, obid = bankB()
            for h in range(4):
                pr, hh = h // 2, h % 2
                oc = ob[:, h * 128:(h + 1) * 128]
                steps = ["scf", "scb"] + (["sf"] if n > 0 else []) + ([] if last_in_seq else ["sb"])
                for i, kind in enumerate(steps):
                    f_, l_ = (i == 0), (i == len(steps) - 1)
                    if kind == "scf":
                        P("pe", lambda e, oc=oc, h=h, f_=f_, l_=l_: e.matmul(oc, lhsT=scf[b][:, h * 128:(h + 1) * 128],
                                                                             rhs=vget(h * 128, (h + 1) * 128), start=f_, stop=l_),
                          r=["scf%d" % b, vid], w=[obid])
                    elif kind == "scb":
                        P("pe", lambda e, oc=oc, h=h, f_=f_, l_=l_: e.matmul(oc, lhsT=scb[b][:, h * 128:(h + 1) * 128],
                                                                             rhs=vget(h * 128, (h + 1) * 128), start=f_, stop=l_),
                          r=["scb%d" % b, vid], w=[obid])
                    elif kind == "sf":
                        P("pe", lambda e, oc=oc, pr=pr, hh=hh, f_=f_, l_=l_: e.matmul(
                            oc, lhsT=qT[b][:, 2 * pr + hh, :],
                            rhs=Sf[sfc][:, pr * 256 + hh * 128:pr * 256 + (hh + 1) * 128], start=f_, stop=l_),
                          r=["qT%d" % b, "Sf%d" % sfc], w=[obid])
                    else:
                        P("pe", lambda e, oc=oc, pr=pr, hh=hh, f_=f_, l_=l_: e.matmul(
                            oc, lhsT=qT[b][:, 4 + 2 * pr + hh, :],
                            rhs=Sb[:, n, pr * 128:(pr + 1) * 128], start=f_, stop=l_),
                          r=["qT%d" % b, "Sb%d_%d0" % (n, pr), "Sb%d_%d1" % (n, pr)], w=[obid])
            if n == 0:
                P("dve", lambda e: e.tensor_copy(out=Tst[:], in_=kv[:, :]), r=[kvid], w=["Tst0", "Tst1"])
            else:
                for pr in range(2):
                    P("dve", lambda e, pr=pr: e.scalar_tensor_tensor(out=Tst[:, pr * 256:(pr + 1) * 256], in0=Tst[:, pr * 256:(pr + 1) * 256],
                                                                     scalar=dec[dprev][:, pr:pr + 1], in1=kv[:, pr * 256:(pr + 1) * 256],
                                                                     op0=ALU.mult, op1=ALU.add),
                      r=[kvid, "Tst%d" % pr, "dec%d" % dprev], w=["Tst%d" % pr])
            if n + 1 < nown:
                P("dve", lambda e: e.tensor_tensor(out=Sf[1 - sfc][:].rearrange("p (r c) -> p r c", r=2),
                                                   in0=Tst[:].rearrange("p (r c) -> p r c", r=2),
                                                   in1=dec[dcur][:, 0:2].unsqueeze(2).to_broadcast([128, 2, 256]), op=ALU.mult),
                  r=["Tst0", "Tst1", "dec%d" % dcur], w=["Sf%d" % (1 - sfc)])
            P("act", lambda e: e.activation(out=scr[:], in_=ob[:, :], func=AF.Square), r=[obid], w=["scr"])
            P("dve", lambda e: e.reduce_sum(out=sm[ss][:, 20:24], in_=scr[:].rearrange("p (h c) -> p h c", h=4), axis=AX.X),
              r=["scr"], w=["sm%dh" % ss])
            P("act", lambda e: e.activation(out=sm[ss][:, 24:28], in_=sm[ss][:, 20:24], func=AF.Ln, scale=1.0 / 128, bias=EPS),
              r=["sm%dh" % ss], w=["sm%di" % ss])
            P("act", lambda e: e.activation(out=sm[ss][:, 28:32], in_=sm[ss][:, 24:28], func=AF.Exp, scale=-0.5),
              r=["sm%di" % ss], w=["sm%dj" % ss])
            P("dve", lambda e: e.tensor_tensor(out=ON[:].rearrange("p (h c) -> p h c", h=4), in0=ob[:, :].rearrange("p (h c) -> p h c", h=4),
                                               in1=sm[ss][:, 28:32].unsqueeze(2).to_broadcast([128, 4, 128]), op=ALU.mult),
              r=[obid, "sm%dj" % ss], w=["ON"])
            P("dve", lambda e: e.tensor_tensor(out=cat[:, 512:1024], in0=ON[:], in1=ZB[b][:], op=ALU.mult),
              r=["ON", "ZB%d" % b], w=["catb"])
            yield
            yield
            ph, pid = ptbank()
            for k in range(8):
                P("pe", lambda e, ph=ph, k=k: e.transpose(out=ph[:, k * 128:(k + 1) * 128], in_=cat[:, k * 128:(k + 1) * 128],
                                                          identity=ident[:]), r=["cata", "catb", "ident"], w=[pid])
            P("dve", lambda e, ph=ph: e.tensor_copy(out=catT[:], in_=ph[:, :].rearrange("p (k t) -> p k t", k=8)), r=[pid], w=["catT"])
            yield
            ybk = []
            for c in range(2):
                bk, bid = pb[4 + c], "pb%d" % (4 + c)
                for k in range(8):
                    P("pe", lambda e, k=k, bk=bk, c=c: e.matmul(bk[:, :], lhsT=catT[:, k, :], rhs=Wo[:, k, c * 512:(c + 1) * 512],
                                                                start=(k == 0), stop=(k == 7)),
                      r=["catT", "Wo"], w=[bid])
                ybk.append((bk, bid))
            pend[g] = (ybk, ss, s3, or0 + n * 128)
            yield

        pend = {}

        def p2_C(g):
            ybk, ss, s3, orow = pend.pop(g)
            for c, (bk, bid) in enumerate(ybk):
                P("act", lambda e, bk=bk, c=c: e.activation(out=junky[:, c * 512:(c + 1) * 512], in_=bk[:, :], func=AF.Square,
                                                            accum_out=sm[ss][:, 3 + c:4 + c]), r=[bid], w=["sm%dk%d" % (ss, c), "junky%d" % c])
            yield
            P("dve", lambda e: e.tensor_tensor(out=sm[ss][:, 5:6], in0=sm[ss][:, 3:4], in1=sm[ss][:, 4:5], op=ALU.add),
              r=["sm%dk0" % ss, "sm%dk1" % ss], w=["sm%dl" % ss])
            P("act", lambda e: e.activation(out=sm[ss][:, 6:7], in_=sm[ss][:, 5:6], func=AF.Ln, scale=1.0 / D, bias=EPS),
              r=["sm%dl" % ss], w=["sm%dm" % ss])
            P("act", lambda e: e.activation(out=sm[ss][:, 7:8], in_=sm[ss][:, 6:7], func=AF.Exp, scale=-0.5),
              r=["sm%dm" % ss], w=["sm%dn" % ss])
            yield
            for c, (bk, bid) in enumerate(ybk):
                P("dve", lambda e, bk=bk, c=c: e.scalar_tensor_tensor(out=yt[:, c * 512:(c + 1) * 512], in0=bk[:, :], scalar=sm[ss][:, 7:8],
                                                                      in1=gpost[:, c * 512:(c + 1) * 512], op0=ALU.mult, op1=ALU.mult),
                  r=[bid, "sm%dn" % ss, "gpost"], w=["yt%d" % c])
            S.op("pool", lambda e: e.dma_start(out=y_d[orow:orow + 128, :], in_=yt[:], accum_op=ALU.add),
                 reads=["yt0", "yt1", "yrow%d" % orow], writes=["yrow%d" % orow], dma_key="yst")
            yield

        tiles = []
        for job in jobs:
            ntot, nown, xr0, _ = job
            for i, n in enumerate(range(ntot - 1, 0, -1)):
                tiles.append(("p1", job, n, i == 0))
            for n in range(nown):
                tiles.append(("p2", job, n, False))
        first_p2 = next(i for i, t in enumerate(tiles) if t[0] == "p2")
        AHEAD = 2

        def xrow(t):
            return t[1][2] + t[2] * 128

        def genA(i):
            kind, job, n, first = tiles[i]
            return (p1_A if kind == "p1" else p2_A)(job, n, first, i)

        def genB(i):
            kind, job, n, first = tiles[i]
            return (p1_B if kind == "p1" else p2_B)(job, n, first, i)

        def run_interleaved(gens):
            gens = [g_ for g_ in gens if g_ is not None]
            while gens:
                for g_ in list(gens):
                    try:
                        next(g_)
                    except StopIteration:
                        gens.remove(g_)

        def orow_of(t):
            return (t[1][3] + t[2] * 128) if t[0] == "p2" else None

        for i in range(min(AHEAD, len(tiles))):
            load_x(xrow(tiles[i]), i, orow_of(tiles[i]))
        for i in range(min(2, len(tiles))):
            front1(i)
            front2(i)
        run_interleaved([genA(0)])
        def genC(i):
            return p2_C(i) if (i >= 0 and i in pend) else None

        for i in range(len(tiles) + 1):
            if i + AHEAD < len(tiles):
                load_x(xrow(tiles[i + AHEAD]), i + AHEAD, orow_of(tiles[i + AHEAD]))
            if i + 1 == first_p2:
                while deferred:
                    deferred.pop(0)()
            elif deferred and i < len(tiles) and tiles[i][0] == "p1":
                deferred.pop(0)()
            run_interleaved([genC(i - 1),
                             genB(i) if i < len(tiles) else None,
                             genA(i + 1) if i + 1 < len(tiles) else None,
                             genF(i + 2) if i + 2 < len(tiles) else None])
        S.emit(st)
        nc._mk_total_ops = S.total
    return nc


def _layout(flip, xsamp, xpr, norm_pre, w_in, w_sp, b_sp, g_v_a, w_gk_fwd, b_gk_fwd, w_gk_bwd, b_gk_bwd,
            g_norm_b, w_out, norm_post):
    w_in_c = w_in[0]
    lr_f, lr_b = w_in_c[:, 3072:3088], w_in_c[:, 3088:3104]
    wsp, bsp_ = w_sp[0], b_sp[0]
    gf, bf, gb, bb = w_gk_fwd[0], b_gk_fwd[0], w_gk_bwd[0], b_gk_bwd[0]
    if flip:
        xsamp = xsamp[::-1]
        xpr = xpr[::-1]
        lr_f, lr_b = lr_b, lr_f
        wsp = wsp[:, ::-1, ::-1]
        bsp_ = bsp_[:, ::-1]
        gf, bf, gb, bb = gb, bb, gf, bf
    w_in_c = np.concatenate([w_in_c[:, :2048], lr_f, lr_b, w_in_c[:, 2048:3072]], axis=1)
    wgk = np.zeros((33, 512), np.float32)
    wgk[0:16, 0:256] = gf
    wgk[16:32, 256:512] = gb
    wgk[32, 0:256] = bf
    wgk[32, 256:512] = bb
    return {
        "x": np.ascontiguousarray(np.concatenate([xsamp, xpr], axis=0), dtype=np.float32),
        "w_in": np.ascontiguousarray(w_in_c, dtype=np.float32),
        "w_out": np.ascontiguousarray(w_out[0], dtype=np.float32),
        "gpre": np.ascontiguousarray(norm_pre[0].reshape(8, 128).T, dtype=np.float32),
        "wspT": np.ascontiguousarray(np.transpose(wsp, (2, 0, 1)).reshape(128, 512), dtype=np.float32),
        "bsp": np.ascontiguousarray(bsp_.T, dtype=np.float32),
        "gv": np.ascontiguousarray(g_v_a[0], dtype=np.float32),
        "wgk": wgk,
        "gnb": np.ascontiguousarray(np.tile(g_norm_b[0], 4), dtype=np.float32),
        "gpost": np.ascontiguousarray(norm_post[0], dtype=np.float32),
    }


def kernel(**inputs):
    inputs = {k: np.asarray(v) for k, v in inputs.items()}
    nc = build_nc(FULL_JOBS)
    xp, xsm = inputs.pop("x_prompt"), inputs.pop("x_sample")
    in_maps = [_layout(c % 2 == 1, xsm[c // 2], xp[c], **inputs) for c in range(8)]
    res = run_bass_kernel_spmd(nc, in_maps, core_ids=list(range(8)))
    y_prompt = np.empty((8, 2048, D), np.float32)
    y_sample = np.empty((4, 8192, D), np.float32)
    for c in range(8):
        y = np.asarray(res.results[c]["y"])
        ys, yp = y[:4096], y[4096:6144]
        if c % 2 == 1:
            y_sample[c // 2, 4096:] = ys[::-1]
            y_prompt[c] = yp[::-1]
        else:
            y_sample[c // 2, :4096] = ys
            y_prompt[c] = yp
    return (y_prompt, y_sample)
```

```python
import math
from contextlib import ExitStack

import numpy as np
import concourse.bass as bass
import concourse.mybir as mybir
from concourse.bass_utils import run_bass_kernel_spmd

F32 = mybir.dt.float32
BF16 = mybir.dt.bfloat16
AF = mybir.ActivationFunctionType
ALU = mybir.AluOpType
AX = mybir.AxisListType

D = 1024
DIN = 3104
EPS = 1e-6
C1 = math.sqrt(2.0 / math.pi)
C2 = 0.044715

FULL_JOBS = [(64, 32, 0, 0), (16, 16, 8192, 4096)]


class Sched:
    ENGS = ("pe", "act", "dve", "pool", "sp")
    SYNC_LAT = 0.12
    ACT_SWITCH = 1.3
    PRIO = "rank"

    def __init__(self, nc):
        self.nc = nc
        self.all = []
        self.last_w = {}
        self.readers = {}
        self.total = 0

    def op(self, eng, fn, reads=(), writes=(), dma_key=None, cost=None, lat=None, tset=None, group=None):
        uid = len(self.all)
        self.total += 1
        deps = set()
        for b in reads:
            d = self.last_w.get(b)
            if d is not None:
                deps.add(d)
        for b in writes:
            d = self.last_w.get(b)
            if d is not None:
                deps.add(d)
            deps.update(self.readers.get(b, ()))
        deps.discard(uid)
        for b in reads:
            self.readers.setdefault(b, []).append(uid)
        for b in writes:
            self.last_w[b] = uid
            self.readers[b] = []
        if eng == "sp":
            assert dma_key is not None
        if dma_key is not None:
            cost = 0.1 if eng == "sp" else 1.0
            lat = 3.0
        if cost is None:
            cost, tset = self._estimate(eng, fn)
        if lat is None:
            lat = cost
        self.all.append(dict(eng=eng, fn=fn, deps=deps, dma_key=dma_key, cost=cost, lat=lat, tset=tset,
                             sig=False, cnt=None, group=group))

    class _Probe:
        def __getattr__(self, name):
            def rec(*a, **k):
                self.call = (name, a, k)
                return None
            return rec

    def _estimate(self, eng, fn):
        pr = Sched._Probe()
        fn(pr)
        name, a, k = pr.call
        out = k.get("out", a[0] if a else None)
        cols = 1
        for d in tuple(out.shape)[1:]:
            cols *= int(d)
        tset = None
        if eng == "pe":
            return max(0.066, cols / 2230.0), None
        if eng == "act":
            f = k.get("func")
            if f in (AF.Exp, AF.Ln):
                tset = 6
            elif f == AF.Silu:
                tset = 18
            elif f == AF.Gelu_apprx_tanh:
                tset = 11
            return 0.22 + cols * 0.00085, tset
        if eng == "dve":
            return 0.12 + cols * 0.00105, None
        if eng == "sp":
            return 0.1, None
        return 0.3 + cols * 0.002, None

    def schedule(self):
        import heapq
        ops = self.all
        n = len(ops)
        ndeps = [len(o["deps"]) for o in ops]
        users = [[] for _ in range(n)]
        for u, o in enumerate(ops):
            for d in o["deps"]:
                users[d].append(u)
        ready_t = [0.0] * n
        fin = [0.0] * n
        future = {e: [] for e in self.ENGS}
        avail = {e: [] for e in self.ENGS}
        free_t = {e: 0.0 for e in self.ENGS}
        order = {e: [] for e in self.ENGS}
        cur_set = [None]
        prio = list(range(n))
        if self.PRIO == "rank":
            rank = [0.0] * n
            for u in range(n - 1, -1, -1):
                m = 0.0
                for v in users[u]:
                    if rank[v] > m:
                        m = rank[v]
                rank[u] = ops[u]["cost"] + m
            top = max(rank)
            prio = [(top - rank[u]) for u in range(n)]
        for u, o in enumerate(ops):
            if ndeps[u] == 0:
                heapq.heappush(future[o["eng"]], (0.0, u))
        groups = {}
        for u, o in enumerate(ops):
            if o["group"] is not None:
                groups.setdefault(o["group"], []).append(u)
        scheduled = [False] * n
        done = 0

        def commit(u, e, t):
            o = ops[u]
            c = o["cost"]
            if e == "act" and o["tset"] is not None and cur_set[0] != o["tset"]:
                c += self.ACT_SWITCH
                cur_set[0] = o["tset"]
            start = max(t, free_t[e], ready_t[u])
            free_t[e] = start + c
            fin[u] = start + (o["lat"] if o["dma_key"] is not None else c)
            order[e].append(u)
            scheduled[u] = True
            for v in users[u]:
                ov = ops[v]
                rt = fin[u] + (0.0 if ov["eng"] == e else self.SYNC_LAT)
                if rt > ready_t[v]:
                    ready_t[v] = rt
                ndeps[v] -= 1
                if ndeps[v] == 0:
                    heapq.heappush(future[ov["eng"]], (ready_t[v], v))

        while done < n:
            best = None
            for e in self.ENGS:
                fu, av = future[e], avail[e]
                while fu and (scheduled[fu[0][1]] or fu[0][0] <= free_t[e]):
                    v_ = heapq.heappop(fu)[1]
                    if not scheduled[v_]:
                        heapq.heappush(av, (prio[v_], v_))
                while av and scheduled[av[0][1]]:
                    heapq.heappop(av)
                if av:
                    cand = (free_t[e], av[0][1], e)
                elif fu:
                    cand = (fu[0][0], fu[0][1], e)
                else:
                    continue
                if best is None or cand < best:
                    best = cand
            t, u, e = best
            if avail[e] and avail[e][0][1] == u:
                heapq.heappop(avail[e])
            else:
                heapq.heappop(future[e])
            commit(u, e, t)
            done += 1
            g_ = ops[u]["group"]
            if g_ is not None:
                for v in groups[g_]:
                    if not scheduled[v]:
                        assert ndeps[v] == 0 and ops[v]["eng"] == e, "group members must share engine and inputs"
                        commit(v, e, free_t[e])
                        done += 1
        self.order = order
        self.est_us = max(free_t.values())

    def emit(self, stack):
        nc = self.nc
        self.schedule()
        ops = self.all
        pos = {}
        for e in self.ENGS:
            for i, u in enumerate(self.order[e]):
                pos[u] = i
        for u, o in enumerate(ops):
            sd = {}
            for d in o["deps"]:
                de = ops[d]["eng"]
                if de == "pe" and o["eng"] == "pe":
                    continue
                key = ("dma", ops[d]["dma_key"]) if ops[d]["dma_key"] is not None else (de, None)
                if key not in sd or pos[sd[key]] < pos[d]:
                    sd[key] = d
            o["sdeps"] = sd
            for d in sd.values():
                ops[d]["sig"] = True
        sems = {}
        for e in ("pe", "act", "dve", "pool"):
            sems[e] = stack.enter_context(nc.semaphore("s_" + e))
            c = 0
            for u in self.order[e]:
                if ops[u]["sig"] and ops[u]["dma_key"] is None:
                    c += 1
                    ops[u]["cnt"] = c
        dsem, dcount, dkey_eng = {}, {}, {}
        for e in self.ENGS:
            for u in self.order[e]:
                o = ops[u]
                k = o["dma_key"]
                if k is None:
                    continue
                assert dkey_eng.setdefault(k, e) == e, "a DMA key must stay on one queue"
                if k not in dsem:
                    dsem[k] = stack.enter_context(nc.semaphore("d_" + str(k)))
                    dcount[k] = 0
                dcount[k] += 16
                o["cnt"] = dcount[k]
                o["sem"] = dsem[k]
        block = stack.enter_context(nc.Block())
        engmap = {"pe": block.tensor, "act": block.scalar, "dve": block.vector,
                  "pool": block.gpsimd, "sp": block.sync}
        final_waits = [(dsem[k], dcount[k]) for k in dsem]
        for e in self.ENGS:
            lst = self.order[e]

            def body(eng, lst=lst, e=e):
                waited = {}
                for u in lst:
                    o = ops[u]
                    for (key, d) in o["sdeps"].items():
                        src = ops[d]
                        if waited.get(key, 0) >= src["cnt"]:
                            continue
                        waited[key] = src["cnt"]
                        eng.wait_ge(src["sem"] if key[0] == "dma" else sems[key[0]], src["cnt"])
                    ins = o["fn"](eng)
                    if o["dma_key"] is not None:
                        ins.then_inc(o["sem"], 16)
                    elif o["sig"]:
                        ins.then_inc(sems[e], 1)
                if e == "sp":
                    for (s_, c_) in final_waits:
                        eng.wait_ge(s_, c_)
            engmap[e](body)


def build_nc(jobs):
    n_x_rows = max(j[2] + j[0] * 128 for j in jobs)
    n_o_rows = max(j[3] + j[1] * 128 for j in jobs)
    max_own = max(j[1] for j in jobs)
    nc = bass.Bass("TRN2", target_bir_lowering=False)
    x_d = nc.dram_tensor("x", [n_x_rows, D], F32, kind="ExternalInput").ap()
    y_d = nc.dram_tensor("y", [n_o_rows, D], F32, kind="ExternalOutput").ap()
    win_d = nc.dram_tensor("w_in", [D, DIN], F32, kind="ExternalInput").ap()
    wout_d = nc.dram_tensor("w_out", [D, D], F32, kind="ExternalInput").ap()
    gpre_d = nc.dram_tensor("gpre", [128, 8], F32, kind="ExternalInput").ap()
    wspT_d = nc.dram_tensor("wspT", [128, 4 * 128], F32, kind="ExternalInput").ap()
    bsp_d = nc.dram_tensor("bsp", [128, 4], F32, kind="ExternalInput").ap()
    gv_d = nc.dram_tensor("gv", [512], F32, kind="ExternalInput").ap()
    wgk_d = nc.dram_tensor("wgk", [33, 512], F32, kind="ExternalInput").ap()
    gnb_d = nc.dram_tensor("gnb", [512], F32, kind="ExternalInput").ap()
    gpost_d = nc.dram_tensor("gpost", [D], F32, kind="ExternalInput").ap()

    with ExitStack() as st:
        def T(name, shape, dt=F32):
            return st.enter_context(nc.sbuf_tensor(name, shape, dt))

        S = Sched(nc)
        ident = T("ident", [128, 128], BF16)
        maskF = T("maskF", [128, 512], BF16)
        maskB = T("maskB", [128, 512], BF16)
        triF = T("triF", [128, 128], BF16)
        triB = T("triB", [128, 128], BF16)
        negcol = T("negcol", [128, 1], BF16)
        Wb = T("Wb", [128, 8, DIN], BF16)
        Wo = T("Wo", [128, 8, D], BF16)
        yt = T("yt", [128, D], F32)
        NB = 2
        UV = [T("UV%d" % i, [128, D], F32) for i in range(NB)]
        ZZ = [T("ZZ%d" % i, [128, D], F32) for i in range(NB)]
        gpre = T("gpre_s", [128, 8], F32)
        wspT = T("wspT_s", [128, 512], BF16)
        bsp = T("bsp_s", [128, 4], F32)
        gvb = T("gvb", [128, 512], F32)
        wgk = T("wgk_s", [33, 512], BF16)
        gnb = T("gnb_s", [128, 512], F32)
        gpost = T("gpost_s", [128, D], F32)
        lrT = T("lrT", [33, 128], BF16)

        def P(e, f, r=(), w=(), c=None, ts=None, g=None):
            S.op(e, f, reads=r, writes=w, cost=c, tset=ts, group=g)

        P("pool", lambda e: e.memset(ident[:], 1.0), w=["ident"])
        P("pool", lambda e: e.affine_select(out=ident[:], in_=ident[:], pattern=[[-1, 128]], compare_op=ALU.is_equal,
                                            fill=0.0, base=0, channel_multiplier=1), r=["ident"], w=["ident"])
        P("pool", lambda e: e.memset(maskF[:], 1.0), w=["maskF"])
        P("pool", lambda e: e.affine_select(out=maskF[:].rearrange("p (h c) -> p h c", h=4), in_=maskF[:].rearrange("p (h c) -> p h c", h=4),
                                            pattern=[[0, 4], [1, 128]], compare_op=ALU.is_ge, fill=0.0, base=0,
                                            channel_multiplier=-1), r=["maskF"], w=["maskF"])
        P("pool", lambda e: e.memset(maskB[:], 1.0), w=["maskB"])
        P("pool", lambda e: e.affine_select(out=maskB[:].rearrange("p (h c) -> p h c", h=4), in_=maskB[:].rearrange("p (h c) -> p h c", h=4),
                                            pattern=[[0, 4], [-1, 128]], compare_op=ALU.is_gt, fill=0.0, base=0,
                                            channel_multiplier=1), r=["maskB"], w=["maskB"])
        P("pool", lambda e: e.memset(triF[:], -1.0 / 16), w=["triF"])
        P("pool", lambda e: e.affine_select(out=triF[:], in_=triF[:], pattern=[[1, 128]], compare_op=ALU.is_ge, fill=0.0,
                                            base=0, channel_multiplier=-1), r=["triF"], w=["triF"])
        P("pool", lambda e: e.memset(triB[:], -1.0 / 16), w=["triB"])
        P("pool", lambda e: e.affine_select(out=triB[:], in_=triB[:], pattern=[[-1, 128]], compare_op=ALU.is_ge, fill=0.0,
                                            base=0, channel_multiplier=1), r=["triB"], w=["triB"])
        P("pool", lambda e: e.memset(negcol[:], -1.0 / 16), w=["negcol"])
        P("pool", lambda e: e.memset(lrT[32:33, :], 1.0), w=["lrT_one"])

        S.op("sp", lambda e: e.dma_start(out=gpre[:], in_=gpre_d[:, :]), writes=["gpre"], dma_key="gpre")
        NC = 25
        vbc = T("vbc", [128, NC, 512], BF16)
        stg = [(UV[0], ["U0", "V0"], "stg0"), (ZZ[0], ["ZA0", "ZB0"], "stg1"),
               (UV[1], ["U1", "V1"], "stg2"), (ZZ[1], ["ZA1", "ZB1"], "stg3")]
        for j in range(4):
            view = vbc[:, 4 * j:4 * j + 4, :].bitcast(F32).rearrange("p a c -> p (a c)")
            stg.append((view, ["vbc%d" % (4 * j + i_) for i_ in range(4)], "stg%d" % (4 + j)))
        NSTG = [len(stg)]
        stg_n = [0]

        def wcol_ids(c0, c1):
            ids = []
            for (a, b_, nm) in ((0, 1792, "WbA"), (1792, 2592, "WbB"), (2592, 3104, "WbC")):
                if c0 < b_ and c1 > a:
                    ids.append(nm)
            return ids

        def load_win_piece(k, c0, c1):
            buf, ids, key = stg[stg_n[0] % NSTG[0]]
            use_dve = (stg_n[0] % 2 == 0)
            stg_n[0] += 1
            S.op("sp", lambda e: e.dma_start(out=buf[:, 0:c1 - c0], in_=win_d[k * 128:(k + 1) * 128, c0:c1]), writes=ids, dma_key=key)
            if use_dve:
                P("dve", lambda e: e.tensor_scalar(out=Wb[:, k, c0:c1], in0=buf[:, 0:c1 - c0], scalar1=gpre[:, k:k + 1], scalar2=None, op0=ALU.mult),
                  r=ids + ["gpre"], w=wcol_ids(c0, c1))
            else:
                P("act", lambda e: e.activation(out=Wb[:, k, c0:c1], in_=buf[:, 0:c1 - c0], func=AF.Identity, scale=gpre[:, k:k + 1]),
                  r=ids + ["gpre"], w=wcol_ids(c0, c1))

        def load_wout_piece(k):
            buf, ids, key = stg[stg_n[0] % NSTG[0]]
            use_dve = (stg_n[0] % 2 == 0)
            stg_n[0] += 1
            S.op("sp", lambda e: e.dma_start(out=buf[:, :], in_=wout_d[k * 128:(k + 1) * 128, :]), writes=ids, dma_key=key)
            if use_dve:
                P("dve", lambda e: e.tensor_copy(out=Wo[:, k, :], in_=buf[:, :]), r=ids, w=["Wo"])
            else:
                P("act", lambda e: e.activation(out=Wo[:, k, :], in_=buf[:, :], func=AF.Copy), r=ids, w=["Wo"])

        for k in range(8):
            load_win_piece(k, 1792, 2592)
        NSTG[0] = 4
        deferred = []
        for (c0, c1) in ((0, 1024), (1024, 1792), (2592, 3104)):
            for k in range(8):
                deferred.append(lambda k=k, c0=c0, c1=c1: load_win_piece(k, c0, c1))
        for k in range(8):
            deferred.append(lambda k=k: load_wout_piece(k))
        S.op("sp", lambda e: e.dma_start(out=yt[0:33, 0:512], in_=wgk_d[:, :]), writes=["yt0", "yt1"], dma_key="ytst")
        P("dve", lambda e: e.tensor_copy(out=wgk[:], in_=yt[0:33, 0:512]), r=["yt0", "yt1"], w=["wgk"])
        S.op("sp", lambda e: e.dma_start(out=yt[:, 0:512], in_=wspT_d[:, :]), writes=["yt0", "yt1"], dma_key="ytst")
        P("dve", lambda e: e.tensor_copy(out=wspT[:], in_=yt[:, 0:512]), r=["yt0", "yt1"], w=["wspT"])
        S.op("sp", lambda e: e.dma_start(out=bsp[:], in_=bsp_d[:, :]), writes=["bsp"], dma_key="bsp")
        S.op("sp", lambda e: e.dma_start(out=gvb[:], in_=gv_d.partition_broadcast(128)), writes=["gvb"], dma_key="gvb")
        S.op("sp", lambda e: e.dma_start(out=gnb[:], in_=gnb_d.partition_broadcast(128)), writes=["gnb"], dma_key="gnb")
        S.op("sp", lambda e: e.dma_start(out=gpost[:], in_=gpost_d.partition_broadcast(128)), writes=["gpost"], dma_key="gpost")

        NXS = 2
        xs = [T("xs%d" % i, [128, D], F32) for i in range(NXS)]
        xbf = [T("xbf%d" % i, [128, D], BF16) for i in range(2)]
        xT = [T("xT%d" % i, [128, 8, 128], BF16) for i in range(2)]
        junk = T("junk", [128, D], BF16)
        junky = T("junky", [128, D], BF16)
        sm = [T("sm%d" % i, [128, 32], F32) for i in range(4)]

        def slots(name, shape, dt=F32, n=NB):
            return [T("%s%d" % (name, i), shape, dt) for i in range(n)]
        U_ = [UV[i][:, 0:512] for i in range(NB)]
        V_ = [UV[i][:, 512:1024] for i in range(NB)]
        ZA = [ZZ[i][:, 0:512] for i in range(NB)]
        ZB = [ZZ[i][:, 512:1024] for i in range(NB)]
        def vbsrc(job, n, b):
            if 1 <= n < job[1] - 1 and n - 1 < NC:
                return (lambda lo, hi: vbc[:, n - 1, lo:hi]), "vbc%d" % (n - 1), True
            return (lambda lo, hi: vb[b][:, lo:hi]), "vb%d" % b, False
        TS = T("TS", [128, 512], F32)
        ON = T("ON", [128, 512], F32)
        qk = slots("qk", [128, 512])
        scr = T("scr", [128, 512], F32)
        scr2 = T("scr2", [128, 512], F32)
        Ep = T("Ep", [128, 512], F32)
        Em = T("Em", [128, 512], F32)
        vlnb = slots("vlnb", [128, 512], BF16)
        vb = slots("vb", [128, 512], BF16)
        lrs = slots("lrs", [128, 32], BF16)
        spb = slots("spb", [128, 512], BF16)
        qin = slots("qin", [128, 1024], BF16)
        kin = slots("kin", [128, 512], BF16)
        qT = slots("qT", [128, 8, 128], BF16)
        kT = slots("kT", [128, 4, 128], BF16)
        scf = slots("scf", [128, 512], BF16)
        scb = slots("scb", [128, 512], BF16)
        cat = T("cat", [128, D], BF16)
        catT = T("catT", [128, 8, 128], BF16)
        dec = [T("dec%d" % i, [128, 2], F32) for i in range(3)]
        Tst = T("Tst", [128, 512], F32)
        Sf = [T("Sf%d" % i, [128, 512], BF16) for i in range(2)]
        Sb = T("Sb", [128, max_own, 256], BF16)

        NPB = 6
        pb = [st.enter_context(nc.psum_tensor("pb%d" % i, [128, 512], F32)) for i in range(NPB)]
        pt = [st.enter_context(nc.psum_tensor("pt%d" % i, [128, 1024], BF16)) for i in range(2)]
        cnt = dict(bank=0, pth=0, tile=0, dec=0)
        for i in range(NB):
            P("pool", lambda e, i=i: e.memset(qin[i][:], 0.0), w=["qin%d_0" % i, "qin%d_1" % i])

        def bank():
            i = cnt["bank"] % NPB
            cnt["bank"] += 1
            return pb[i], "pb%d" % i

        def ptbank():
            i = cnt["pth"] % 2
            cnt["pth"] += 1
            return pt[i], "pt%d" % i

        def bankA():
            i = cnt["bank"] % 2
            cnt["bank"] += 1
            return pb[i], "pb%d" % i

        def bankB():
            i = 2 + cnt["bankb"] % 2
            cnt["bankb"] += 1
            return pb[i], "pb%d" % i
        cnt["bankb"] = 0

        def load_x(row0, g, orow=None):
            s3 = g % NXS
            S.op("sp", lambda e: e.dma_start(out=xs[s3][:], in_=x_d[row0:row0 + 128, :]), writes=["xs%d" % s3],
                 dma_key="xs%d" % s3)
            if orow is not None:
                S.op("sp", lambda e: e.dma_start(out=y_d[orow:orow + 128, :], in_=x_d[row0:row0 + 128, :]),
                     writes=["yrow%d" % orow, "ycpk%d" % (g % 4)], dma_key="ycp%d" % (g % 4))

        rstdc = T("rstdc", [128, 64], F32)
        lrc = T("lrc", [128, max_own, 32], BF16)

        def lrsrc(job, n, b):
            if 1 <= n < job[1]:
                return lrc[:, n, :], "lrc%d" % n, True
            return lrs[b][:], "lrs%d" % b, False

        def front1(g):
            s3, s2, ss = g % NXS, g % 2, g % 4
            kind, job, n, _ = tiles[g]
            own = (1 <= n < job[1])
            if kind == "p2" and own:
                rs, rsid = rstdc[:, n:n + 1], "rstdc%d" % n
            else:
                rs, rsid = (rstdc[:, n:n + 1], "rstdc%d" % n) if own else (sm[ss][:, 2:3], "sm%dc" % ss)
                P("act", lambda e: e.activation(out=junk[:], in_=xs[s3][:], func=AF.Square, accum_out=sm[ss][:, 0:1]),
                  r=["xs%d" % s3], w=["sm%da" % ss, "junk"])
                P("act", lambda e: e.activation(out=sm[ss][:, 1:2], in_=sm[ss][:, 0:1], func=AF.Ln, scale=1.0 / D, bias=EPS),
                  r=["sm%da" % ss], w=["sm%db" % ss])
                P("act", lambda e: e.activation(out=rs, in_=sm[ss][:, 1:2], func=AF.Exp, scale=-0.5),
                  r=["sm%db" % ss], w=[rsid])
            P("dve", lambda e: e.tensor_scalar(out=xbf[s2][:], in0=xs[s3][:], scalar1=rs, scalar2=None, op0=ALU.mult),
              r=["xs%d" % s3, rsid], w=["xbf%d" % s2])

        def front2(g):
            s2 = g % 2
            ph, pid = ptbank()
            for k in range(8):
                P("pe", lambda e, ph=ph, k=k: e.transpose(out=ph[:, k * 128:(k + 1) * 128],
                                                          in_=xbf[s2][:, k * 128:(k + 1) * 128], identity=ident[:]),
                  r=["xbf%d" % s2, "ident"], w=[pid])
            P("dve", lambda e, ph=ph: e.tensor_copy(out=xT[s2][:], in_=ph[:, :].rearrange("p (k t) -> p k t", k=8)),
              r=[pid], w=["xT%d" % s2])

        def genF(g):
            for _ in range(3):
                yield
            front1(g)
            yield
            yield
            front2(g)
            yield

        def inproj(s2, c0, c1):
            bk, bid = bankA()
            wids = wcol_ids(c0, c1)
            for k in range(8):
                P("pe", lambda e, k=k, bk=bk: e.matmul(bk[:, 0:c1 - c0], lhsT=xT[s2][:, k, :], rhs=Wb[:, k, c0:c1],
                                                       start=(k == 0), stop=(k == 7)),
                  r=["xT%d" % s2] + wids, w=[bid])
            return bk, bid

        def gate_chain(b, ncols, c_lo, lr_ap, lr_id):
            ph, pid = ptbank()
            P("pe", lambda e: e.transpose(out=ph[0:32, 0:128], in_=lr_ap, identity=ident[:]),
              r=[lr_id, "ident"], w=[pid])
            P("dve", lambda e: e.tensor_copy(out=lrT[0:32, :], in_=ph[0:32, 0:128]), r=[pid], w=["lrT"])
            bk, bid = bankB()
            P("pe", lambda e: e.matmul(bk[:, 0:ncols], lhsT=lrT[0:33, :], rhs=wgk[0:33, c_lo:c_lo + ncols], start=True, stop=True),
              r=["lrT", "lrT_one", "wgk"], w=[bid])
            P("act", lambda e: e.activation(out=scr2[:, 0:ncols], in_=bk[:, 0:ncols], func=AF.Exp, scale=-1.0),
              r=[bid], w=["scr2"])
            P("act", lambda e: e.activation(out=spb[b][:, 0:ncols], in_=scr2[:, 0:ncols], func=AF.Ln, bias=1.0),
              r=["scr2"], w=["spb%d" % b])

        def p1_A(job, n, first, g):
            b, s2 = g % NB, g % 2
            bk, bid = inproj(s2, 1792, 2080)
            lr_ap, lr_id, _ = lrsrc(job, n, b)
            P("act", lambda e: e.activation(out=lr_ap, in_=bk[:, 256:288], func=AF.Copy), r=[bid], w=[lr_id])
            P("act", lambda e: e.activation(out=qk[b][:, 256:512], in_=bk[:, 0:256], func=AF.Copy), r=[bid], w=["qk%d" % b])
            yield
            bk2, bid2 = inproj(s2, 2080, 2592)
            vget, vid, _ = vbsrc(job, n, b)
            P("dve", lambda e: e.tensor_copy(out=vget(0, 512), in_=bk2[:, :]), r=[bid2], w=[vid])
            yield

        def p1_B(job, n, first, g):
            ntot, nown, xr0, _ = job
            b = g % NB
            lr_ap, lr_id, _ = lrsrc(job, n, b)
            gate_chain(b, 256, 256, lr_ap, lr_id)
            yield
            bc, bcid = bankB()
            P("pe", lambda e: e.matmul(bc[:, 0:256], lhsT=triB[:], rhs=spb[b][:, 0:256], start=True, stop=True),
              r=["triB", "spb%d" % b], w=[bcid])
            bl, blid = bankB()
            for pr in range(2):
                P("pe", lambda e, pr=pr: e.matmul(bl[:, pr:pr + 1], lhsT=spb[b][:, pr * 128:(pr + 1) * 128], rhs=negcol[:, 0:1],
                                                  start=True, stop=True), r=["negcol", "spb%d" % b], w=[blid])
            P("act", lambda e: e.activation(out=Em[:, 0:256], in_=bc[:, 0:256], func=AF.Exp, scale=-1.0), r=[bcid], w=["Em"])
            dcur = cnt["dec"] % 3
            dprev = (cnt["dec"] - 1) % 3
            cnt["dec"] += 1
            P("act", lambda e: e.activation(out=dec[dcur][:, 0:2], in_=bl[:, 0:2], func=AF.Exp), r=[blid], w=["dec%d" % dcur])
            P("dve", lambda e: e.tensor_tensor(out=kin[b][:, 0:256], in0=Em[:, 0:256], in1=qk[b][:, 256:512], op=ALU.mult),
              r=["Em", "qk%d" % b], w=["kin%d" % b])
            yield
            kv, kvid = bankB()
            vget, vid, _ = vbsrc(job, n, b)
            for pr in range(2):
                P("pe", lambda e, pr=pr: e.matmul(kv[:, pr * 256:(pr + 1) * 256], lhsT=kin[b][:, pr * 128:(pr + 1) * 128],
                                                  rhs=vget(pr * 256, (pr + 1) * 256), start=True, stop=True),
                  r=["kin%d" % b, vid], w=[kvid])
            if first:
                P("dve", lambda e: e.tensor_copy(out=Tst[:], in_=kv[:, :]), r=[kvid], w=["Tst0", "Tst1"])
            else:
                for pr in range(2):
                    P("dve", lambda e, pr=pr: e.scalar_tensor_tensor(out=Tst[:, pr * 256:(pr + 1) * 256], in0=Tst[:, pr * 256:(pr + 1) * 256],
                                                                     scalar=dec[dprev][:, pr:pr + 1], in1=kv[:, pr * 256:(pr + 1) * 256],
                                                                     op0=ALU.mult, op1=ALU.add),
                      r=[kvid, "Tst%d" % pr, "dec%d" % dprev], w=["Tst%d" % pr])
            if 1 <= n <= nown:
                for hh in range(2):
                    rs = slice(hh * 64, (hh + 1) * 64)
                    P("dve", lambda e, hh=hh, rs=rs: e.tensor_tensor(
                        out=Sb[rs, n - 1, :].rearrange("p (r c) -> p r c", r=2),
                        in0=Tst[rs, :].rearrange("p (r x) -> p r x", r=2)[:, :, hh * 128:(hh + 1) * 128],
                        in1=dec[dcur][rs, 0:2].unsqueeze(2).to_broadcast([64, 2, 128]), op=ALU.mult),
                      r=["Tst0", "Tst1", "dec%d" % dcur], w=["Sb%d_0%d" % (n - 1, hh), "Sb%d_1%d" % (n - 1, hh)])
            yield

        def p2_A(job, n, first, g):
            b, s2, ss = g % NB, g % 2, g % 4
            if not lrsrc(job, n, b)[2]:
                bk, bid = inproj(s2, 2048, 2080)
                P("act", lambda e, bk=bk: e.activation(out=lrs[b][:], in_=bk[:, 0:32], func=AF.Copy), r=[bid], w=["lrs%d" % b])
                yield
            bk, bid = inproj(s2, 1536, 2048)
            P("act", lambda e, bk=bk: e.activation(out=qk[b][:], in_=bk[:, :], func=AF.Copy), r=[bid], w=["qk%d" % b])
            yield
            if not vbsrc(job, n, b)[2]:
                bk, bid = inproj(s2, 2080, 2592)
                P("dve", lambda e, bk=bk: e.tensor_copy(out=vb[b][:], in_=bk[:, :]), r=[bid], w=["vb%d" % b])
                yield
            bk, bid = inproj(s2, 1024, 1536)
            P("act", lambda e, bk=bk: e.activation(out=ZA[b][:], in_=bk[:, :], func=AF.Copy), r=[bid], w=["ZA%d" % b])
            yield
            bk, bid = inproj(s2, 2592, 3104)
            P("dve", lambda e, bk=bk: e.tensor_copy(out=ZB[b][:], in_=bk[:, :]), r=[bid], w=["ZB%d" % b])
            yield
            bk, bid = inproj(s2, 0, 512)
            P("act", lambda e, bk=bk: e.activation(out=U_[b][:], in_=bk[:, :], func=AF.Copy), r=[bid], w=["U%d" % b])
            yield
            bk, bid = inproj(s2, 512, 1024)
            P("dve", lambda e, bk=bk: e.tensor_copy(out=V_[b][:], in_=bk[:, :]), r=[bid], w=["V%d" % b])
            yield
            yield
            yield

            zin, uin = ["ZA%d" % b, "ZB%d" % b], ["U%d" % b, "V%d" % b]
            P("act", lambda e: e.activation(out=ZA[b][:], in_=ZA[b][:], func=AF.Silu), r=zin, w=["ZA%d" % b], g="sil%d" % g)
            P("act", lambda e: e.activation(out=ZB[b][:], in_=ZB[b][:], func=AF.Silu), r=zin, w=["ZB%d" % b], g="sil%d" % g)
            P("act", lambda e: e.activation(out=U_[b][:], in_=U_[b][:], func=AF.Gelu_apprx_tanh), r=uin, w=["U%d" % b], g="gel%d" % g)
            P("act", lambda e: e.activation(out=V_[b][:], in_=V_[b][:], func=AF.Gelu_apprx_tanh), r=uin, w=["V%d" % b], g="gel%d" % g)
            P("dve", lambda e: e.bn_stats(out=sm[ss][:, 8:14], in_=V_[b][:]), r=["V%d" % b], w=["sm%dd" % ss])
            P("dve", lambda e: e.bn_aggr(out=sm[ss][:, 14:16], in_=sm[ss][:, 8:14]), r=["sm%dd" % ss], w=["sm%de" % ss])
            P("act", lambda e: e.activation(out=sm[ss][:, 16:17], in_=sm[ss][:, 15:16], func=AF.Ln, bias=EPS), r=["sm%de" % ss], w=["sm%df" % ss])
            P("act", lambda e: e.activation(out=sm[ss][:, 17:18], in_=sm[ss][:, 16:17], func=AF.Exp, scale=-0.5), r=["sm%df" % ss], w=["sm%dg" % ss])
            P("dve", lambda e: e.tensor_scalar(out=V_[b][:], in0=V_[b][:], scalar1=sm[ss][:, 14:15], scalar2=sm[ss][:, 17:18],
                                               op0=ALU.subtract, op1=ALU.mult), r=["V%d" % b, "sm%de" % ss, "sm%dg" % ss], w=["V%d" % b])
            P("dve", lambda e: e.tensor_tensor(out=vlnb[b][:], in0=V_[b][:], in1=gvb[:], op=ALU.mult),
              r=["V%d" % b, "gvb"], w=["vlnb%d" % b])
            P("dve", lambda e: e.tensor_tensor(out=U_[b][:], in0=U_[b][:], in1=ZA[b][:], op=ALU.mult),
              r=["U%d" % b, "ZA%d" % b], w=["U%d" % b])
            P("dve", lambda e: e.tensor_tensor(out=ZB[b][:], in0=ZB[b][:], in1=gnb[:], op=ALU.mult),
              r=["ZB%d" % b, "gnb"], w=["ZB%d" % b])
            yield

        def p2_B(job, n, first, g):
            ntot, nown, xr0, or0 = job
            b, s3, ss = g % NB, g % NXS, g % 4
            last_in_seq = (n == ntot - 1)
            lr_ap, lr_id, _ = lrsrc(job, n, b)
            gate_chain(b, 512, 0, lr_ap, lr_id)
            yield
            bc, bcid = bankB()
            P("pe", lambda e: e.matmul(bc[:, 0:256], lhsT=triF[:], rhs=spb[b][:, 0:256], start=True, stop=True),
              r=["triF", "spb%d" % b], w=[bcid])
            P("pe", lambda e: e.matmul(bc[:, 256:512], lhsT=triB[:], rhs=spb[b][:, 256:512], start=True, stop=True),
              r=["triB", "spb%d" % b], w=[bcid])
            bl, blid = bankB()
            for pr in range(2):
                P("pe", lambda e, pr=pr: e.matmul(bl[:, pr:pr + 1], lhsT=spb[b][:, pr * 128:(pr + 1) * 128], rhs=negcol[:, 0:1],
                                                  start=True, stop=True), r=["negcol", "spb%d" % b], w=[blid])
            P("act", lambda e: e.activation(out=Ep[:], in_=bc[:, :], func=AF.Exp, bias=math.log(0.125)), r=[bcid], w=["Ep"])
            P("act", lambda e: e.activation(out=Em[:], in_=bc[:, :], func=AF.Exp, scale=-1.0), r=[bcid], w=["Em"])
            dcur = cnt["dec"] % 3
            dprev = (cnt["dec"] - 1) % 3
            cnt["dec"] += 1
            P("act", lambda e: e.activation(out=dec[dcur][:, 0:2], in_=bl[:, 0:2], func=AF.Exp), r=[blid], w=["dec%d" % dcur])
            for hh in range(2):
                P("dve", lambda e, hh=hh: e.tensor_tensor(
                    out=qin[b][:].rearrange("p (t r h c) -> p t r h c", t=2, r=2, h=2)[:, :, :, hh, hh * 64:(hh + 1) * 64],
                    in0=Ep[:].rearrange("p (t r h d) -> p t r h d", t=2, r=2, h=2)[:, :, :, hh, :],
                    in1=qk[b][:, 0:256].rearrange("p (r h d) -> p r h d", r=2, h=2)[:, :, hh, :].unsqueeze(1).to_broadcast([128, 2, 2, 64]),
                    op=ALU.mult), r=["Ep", "qk%d" % b], w=["qin%d_%d" % (b, hh)])
            P("dve", lambda e: e.tensor_tensor(out=kin[b][:].rearrange("p (t d) -> p t d", t=2), in0=Em[:].rearrange("p (t d) -> p t d", t=2),
                                               in1=qk[b][:, 256:512].unsqueeze(1).to_broadcast([128, 2, 256]), op=ALU.mult),
              r=["Em", "qk%d" % b], w=["kin%d" % b])
            sv, svid = bankB()
            for h in range(4):
                P("pe", lambda e, h=h: e.matmul(sv[:, h * 128:(h + 1) * 128], lhsT=wspT[:, h * 128:(h + 1) * 128],
                                                rhs=vlnb[b][:, h * 128:(h + 1) * 128], start=True, stop=True),
                  r=["wspT", "vlnb%d" % b], w=[svid])
            P("dve", lambda e: e.tensor_tensor(out=TS[:].rearrange("p (h c) -> p h c", h=4), in0=sv[:, :].rearrange("p (h c) -> p h c", h=4),
                                               in1=bsp[:, 0:4].unsqueeze(2).to_broadcast([128, 4, 128]), op=ALU.add),
              r=[svid, "bsp"], w=["TS"])
            P("dve", lambda e: e.tensor_tensor(out=cat[:, 0:512], in0=TS[:], in1=U_[b][:], op=ALU.mult),
              r=["TS", "U%d" % b], w=["cata"])
            yield
            ph, pid = ptbank()
            for k in range(8):
                P("pe", lambda e, ph=ph, k=k: e.transpose(out=ph[:, k * 128:(k + 1) * 128], in_=qin[b][:, k * 128:(k + 1) * 128],
                                                          identity=ident[:]), r=["qin%d_0" % b, "qin%d_1" % b, "ident"], w=[pid])
            P("act", lambda e, ph=ph: e.activation(out=qT[b][:], in_=ph[:, :].rearrange("p (k t) -> p k t", k=8), func=AF.Copy),
              r=[pid], w=["qT%d" % b])
            ph2, pid2 = ptbank()
            for k in range(4):
                P("pe", lambda e, k=k: e.transpose(out=ph2[:, k * 128:(k + 1) * 128], in_=kin[b][:, k * 128:(k + 1) * 128],
                                                   identity=ident[:]), r=["kin%d" % b, "ident"], w=[pid2])
            P("dve", lambda e: e.tensor_copy(out=kT[b][:], in_=ph2[:, 0:512].rearrange("p (k t) -> p k t", k=4)), r=[pid2], w=["kT%d" % b])
            yield
            for (t, dstb, dstid, mk, mkid) in ((0, scf[b], "scf%d" % b, maskF, "maskF"), (1, scb[b], "scb%d" % b, maskB, "maskB")):
                sc, scid = bankB()
                for h in range(4):
                    P("pe", lambda e, h=h, sc=sc, t=t: e.matmul(sc[:, h * 128:(h + 1) * 128], lhsT=kT[b][:, t * 2 + h // 2, :],
                                                                rhs=qT[b][:, t * 4 + h, :], start=True, stop=True),
                      r=["qT%d" % b, "kT%d" % b], w=[scid])
                P("dve", lambda e, sc=sc, dstb=dstb, mk=mk: e.tensor_tensor(out=dstb[:], in0=sc[:, :], in1=mk[:], op=ALU.mult),
                  r=[scid, mkid], w=[dstid])
            kv, kvid = bankB()
            vget, vid, _ = vbsrc(job, n, b)
            for pr in range(2):
                P("pe", lambda e, pr=pr: e.matmul(kv[:, pr * 256:(pr + 1) * 256], lhsT=kin[b][:, pr * 128:(pr + 1) * 128],
                                                  rhs=vget(pr * 256, (pr + 1) * 256), start=True, stop=True),
                  r=["kin%d" % b, vid], w=[kvid])
            yield
            sfc = n % 2
            ob, obid = bankB()
            for h in range(4):
                pr, hh = h // 2, h % 2
                oc = ob[:, h * 128:(h + 1) * 128]
                steps = ["scf", "scb"] + (["sf"] if n > 0 else []) + ([] if last_in_seq else ["sb"])
                for i, kind in enumerate(steps):
                    f_, l_ = (i == 0), (i == len(steps) - 1)
                    if kind == "scf":
                        P("pe", lambda e, oc=oc, h=h, f_=f_, l_=l_: e.matmul(oc, lhsT=scf[b][:, h * 128:(h + 1) * 128],
                                                                             rhs=vget(h * 128, (h + 1) * 128), start=f_, stop=l_),
                          r=["scf%d" % b, vid], w=[obid])
                    elif kind == "scb":
                        P("pe", lambda e, oc=oc, h=h, f_=f_, l_=l_: e.matmul(oc, lhsT=scb[b][:, h * 128:(h + 1) * 128],
                                                                             rhs=vget(h * 128, (h + 1) * 128), start=f_, stop=l_),
                          r=["scb%d" % b, vid], w=[obid])
                    elif kind == "sf":
                        P("pe", lambda e, oc=oc, pr=pr, hh=hh, f_=f_, l_=l_: e.matmul(
                            oc, lhsT=qT[b][:, 2 * pr + hh, :],
                            rhs=Sf[sfc][:, pr * 256 + hh * 128:pr * 256 + (hh + 1) * 128], start=f_, stop=l_),
                          r=["qT%d" % b, "Sf%d" % sfc], w=[obid])
                    else:
                        P("pe", lambda e, oc=oc, pr=pr, hh=hh, f_=f_, l_=l_: e.matmul(
                            oc, lhsT=qT[b][:, 4 + 2 * pr + hh, :],
                            rhs=Sb[:, n, pr * 128:(pr + 1) * 128], start=f_, stop=l_),
                          r=["qT%d" % b, "Sb%d_%d0" % (n, pr), "Sb%d_%d1" % (n, pr)], w=[obid])
            if n == 0:
                P("dve", lambda e: e.tensor_copy(out=Tst[:], in_=kv[:, :]), r=[kvid], w=["Tst0", "Tst1"])
            else:
                for pr in range(2):
                    P("dve", lambda e, pr=pr: e.scalar_tensor_tensor(out=Tst[:, pr * 256:(pr + 1) * 256], in0=Tst[:, pr * 256:(pr + 1) * 256],
                                                                     scalar=dec[dprev][:, pr:pr + 1], in1=kv[:, pr * 256:(pr + 1) * 256],
                                                                     op0=ALU.mult, op1=ALU.add),
                      r=[kvid, "Tst%d" % pr, "dec%d" % dprev], w=["Tst%d" % pr])
            if n + 1 < nown:
                P("dve", lambda e: e.tensor_tensor(out=Sf[1 - sfc][:].rearrange("p (r c) -> p r c", r=2),
                                                   in0=Tst[:].rearrange("p (r c) -> p r c", r=2),
                                                   in1=dec[dcur][:, 0:2].unsqueeze(2).to_broadcast([128, 2, 256]), op=ALU.mult),
                  r=["Tst0", "Tst1", "dec%d" % dcur], w=["Sf%d" % (1 - sfc)])
            P("act", lambda e: e.activation(out=scr[:], in_=ob[:, :], func=AF.Square), r=[obid], w=["scr"])
            P("dve", lambda e: e.reduce_sum(out=sm[ss][:, 20:24], in_=scr[:].rearrange("p (h c) -> p h c", h=4), axis=AX.X),
              r=["scr"], w=["sm%dh" % ss])
            P("act", lambda e: e.activation(out=sm[ss][:, 24:28], in_=sm[ss][:, 20:24], func=AF.Ln, scale=1.0 / 128, bias=EPS),
              r=["sm%dh" % ss], w=["sm%di" % ss])
            P("act", lambda e: e.activation(out=sm[ss][:, 28:32], in_=sm[ss][:, 24:28], func=AF.Exp, scale=-0.5),
              r=["sm%di" % ss], w=["sm%dj" % ss])
            P("dve", lambda e: e.tensor_tensor(out=ON[:].rearrange("p (h c) -> p h c", h=4), in0=ob[:, :].rearrange("p (h c) -> p h c", h=4),
                                               in1=sm[ss][:, 28:32].unsqueeze(2).to_broadcast([128, 4, 128]), op=ALU.mult),
              r=[obid, "sm%dj" % ss], w=["ON"])
            P("dve", lambda e: e.tensor_tensor(out=cat[:, 512:1024], in0=ON[:], in1=ZB[b][:], op=ALU.mult),
              r=["ON", "ZB%d" % b], w=["catb"])
            yield
            yield
            ph, pid = ptbank()
            for k in range(8):
                P("pe", lambda e, ph=ph, k=k: e.transpose(out=ph[:, k * 128:(k + 1) * 128], in_=cat[:, k * 128:(k + 1) * 128],
                                                          identity=ident[:]), r=["cata", "catb", "ident"], w=[pid])
            P("dve", lambda e, ph=ph: e.tensor_copy(out=catT[:], in_=ph[:, :].rearrange("p (k t) -> p k t", k=8)), r=[pid], w=["catT"])
            yield
            ybk = []
            for c in range(2):
                bk, bid = pb[4 + c], "pb%d" % (4 + c)
                for k in range(8):
                    P("pe", lambda e, k=k, bk=bk, c=c: e.matmul(bk[:, :], lhsT=catT[:, k, :], rhs=Wo[:, k, c * 512:(c + 1) * 512],
                                                                start=(k == 0), stop=(k == 7)),
                      r=["catT", "Wo"], w=[bid])
                ybk.append((bk, bid))
            pend[g] = (ybk, ss, s3, or0 + n * 128)
            yield

        pend = {}

        def p2_C(g):
            ybk, ss, s3, orow = pend.pop(g)
            for c, (bk, bid) in enumerate(ybk):
                P("act", lambda e, bk=bk, c=c: e.activation(out=junky[:, c * 512:(c + 1) * 512], in_=bk[:, :], func=AF.Square,
                                                            accum_out=sm[ss][:, 3 + c:4 + c]), r=[bid], w=["sm%dk%d" % (ss, c), "junky%d" % c])
            yield
            P("dve", lambda e: e.tensor_tensor(out=sm[ss][:, 5:6], in0=sm[ss][:, 3:4], in1=sm[ss][:, 4:5], op=ALU.add),
              r=["sm%dk0" % ss, "sm%dk1" % ss], w=["sm%dl" % ss])
            P("act", lambda e: e.activation(out=sm[ss][:, 6:7], in_=sm[ss][:, 5:6], func=AF.Ln, scale=1.0 / D, bias=EPS),
              r=["sm%dl" % ss], w=["sm%dm" % ss])
            P("act", lambda e: e.activation(out=sm[ss][:, 7:8], in_=sm[ss][:, 6:7], func=AF.Exp, scale=-0.5),
              r=["sm%dm" % ss], w=["sm%dn" % ss])
            yield
            for c, (bk, bid) in enumerate(ybk):
                P("dve", lambda e, bk=bk, c=c: e.scalar_tensor_tensor(out=yt[:, c * 512:(c + 1) * 512], in0=bk[:, :], scalar=sm[ss][:, 7:8],
                                                                      in1=gpost[:, c * 512:(c + 1) * 512], op0=ALU.mult, op1=ALU.mult),
                  r=[bid, "sm%dn" % ss, "gpost"], w=["yt%d" % c])
            S.op("pool", lambda e: e.dma_start(out=y_d[orow:orow + 128, :], in_=yt[:], accum_op=ALU.add),
                 reads=["yt0", "yt1", "yrow%d" % orow], writes=["yrow%d" % orow], dma_key="yst")
            yield

        tiles = []
        for job in jobs:
            ntot, nown, xr0, _ = job
            for i, n in enumerate(range(ntot - 1, 0, -1)):
                tiles.append(("p1", job, n, i == 0))
            for n in range(nown):
                tiles.append(("p2", job, n, False))
        first_p2 = next(i for i, t in enumerate(tiles) if t[0] == "p2")
        AHEAD = 2

        def xrow(t):
            return t[1][2] + t[2] * 128

        def genA(i):
            kind, job, n, first = tiles[i]
            return (p1_A if kind == "p1" else p2_A)(job, n, first, i)

        def genB(i):
            kind, job, n, first = tiles[i]
            return (p1_B if kind == "p1" else p2_B)(job, n, first, i)

        def run_interleaved(gens):
            gens = [g_ for g_ in gens if g_ is not None]
            while gens:
                for g_ in list(gens):
                    try:
                        next(g_)
                    except StopIteration:
                        gens.remove(g_)

        def orow_of(t):
            return (t[1][3] + t[2] * 128) if t[0] == "p2" else None

        for i in range(min(AHEAD, len(tiles))):
            load_x(xrow(tiles[i]), i, orow_of(tiles[i]))
        for i in range(min(2, len(tiles))):
            front1(i)
            front2(i)
        run_interleaved([genA(0)])
        def genC(i):
            return p2_C(i) if (i >= 0 and i in pend) else None

        for i in range(len(tiles) + 1):
            if i + AHEAD < len(tiles):
                load_x(xrow(tiles[i + AHEAD]), i + AHEAD, orow_of(tiles[i + AHEAD]))
            if i + 1 == first_p2:
                while deferred:
                    deferred.pop(0)()
            elif deferred and i < len(tiles) and tiles[i][0] == "p1":
                deferred.pop(0)()
            run_interleaved([genC(i - 1),
                             genB(i) if i < len(tiles) else None,
                             genA(i + 1) if i + 1 < len(tiles) else None,
                             genF(i + 2) if i + 2 < len(tiles) else None])
        S.emit(st)
        nc._mk_total_ops = S.total
    return nc


def _layout(flip, xsamp, xpr, norm_pre, w_in, w_sp, b_sp, g_v_a, w_gk_fwd, b_gk_fwd, w_gk_bwd, b_gk_bwd,
            g_norm_b, w_out, norm_post):
    w_in_c = w_in[0]
    lr_f, lr_b = w_in_c[:, 3072:3088], w_in_c[:, 3088:3104]
    wsp, bsp_ = w_sp[0], b_sp[0]
    gf, bf, gb, bb = w_gk_fwd[0], b_gk_fwd[0], w_gk_bwd[0], b_gk_bwd[0]
    if flip:
        xsamp = xsamp[::-1]
        xpr = xpr[::-1]
        lr_f, lr_b = lr_b, lr_f
        wsp = wsp[:, ::-1, ::-1]
        bsp_ = bsp_[:, ::-1]
        gf, bf, gb, bb = gb, bb, gf, bf
    w_in_c = np.concatenate([w_in_c[:, :2048], lr_f, lr_b, w_in_c[:, 2048:3072]], axis=1)
    wgk = np.zeros((33, 512), np.float32)
    wgk[0:16, 0:256] = gf
    wgk[16:32, 256:512] = gb
    wgk[32, 0:256] = bf
    wgk[32, 256:512] = bb
    return {
        "x": np.ascontiguousarray(np.concatenate([xsamp, xpr], axis=0), dtype=np.float32),
        "w_in": np.ascontiguousarray(w_in_c, dtype=np.float32),
        "w_out": np.ascontiguousarray(w_out[0], dtype=np.float32),
        "gpre": np.ascontiguousarray(norm_pre[0].reshape(8, 128).T, dtype=np.float32),
        "wspT": np.ascontiguousarray(np.transpose(wsp, (2, 0, 1)).reshape(128, 512), dtype=np.float32),
        "bsp": np.ascontiguousarray(bsp_.T, dtype=np.float32),
        "gv": np.ascontiguousarray(g_v_a[0], dtype=np.float32),
        "wgk": wgk,
        "gnb": np.ascontiguousarray(np.tile(g_norm_b[0], 4), dtype=np.float32),
        "gpost": np.ascontiguousarray(norm_post[0], dtype=np.float32),
    }


def kernel(**inputs):
    inputs = {k: np.asarray(v) for k, v in inputs.items()}
    nc = build_nc(FULL_JOBS)
    xp, xsm = inputs.pop("x_prompt"), inputs.pop("x_sample")
    in_maps = [_layout(c % 2 == 1, xsm[c // 2], xp[c], **inputs) for c in range(8)]
    res = run_bass_kernel_spmd(nc, in_maps, core_ids=list(range(8)))
    y_prompt = np.empty((8, 2048, D), np.float32)
    y_sample = np.empty((4, 8192, D), np.float32)
    for c in range(8):
        y = np.asarray(res.results[c]["y"])
        ys, yp = y[:4096], y[4096:6144]
        if c % 2 == 1:
            y_sample[c // 2, 4096:] = ys[::-1]
            y_prompt[c] = yp[::-1]
        else:
            y_sample[c // 2, :4096] = ys
            y_prompt[c] = yp
    return (y_prompt, y_sample)
```

```python
import math
from contextlib import ExitStack

import numpy as np
import concourse.bass as bass
import concourse.mybir as mybir
from concourse.bass_utils import run_bass_kernel_spmd

F32 = mybir.dt.float32
BF16 = mybir.dt.bfloat16
AF = mybir.ActivationFunctionType
ALU = mybir.AluOpType
AX = mybir.AxisListType

D = 1024
DIN = 3104
EPS = 1e-6
C1 = math.sqrt(2.0 / math.pi)
C2 = 0.044715

FULL_JOBS = [(64, 32, 0, 0), (16, 16, 8192, 4096)]


class Sched:
    ENGS = ("pe", "act", "dve", "pool", "sp")
    SYNC_LAT = 0.12
    ACT_SWITCH = 1.3
    PRIO = "rank"

    def __init__(self, nc):
        self.nc = nc
        self.all = []
        self.last_w = {}
        self.readers = {}
        self.total = 0

    def op(self, eng, fn, reads=(), writes=(), dma_key=None, cost=None, lat=None, tset=None, group=None):
        uid = len(self.all)
        self.total += 1
        deps = set()
        for b in reads:
            d = self.last_w.get(b)
            if d is not None:
                deps.add(d)
        for b in writes:
            d = self.last_w.get(b)
            if d is not None:
                deps.add(d)
            deps.update(self.readers.get(b, ()))
        deps.discard(uid)
        for b in reads:
            self.readers.setdefault(b, []).append(uid)
        for b in writes:
            self.last_w[b] = uid
            self.readers[b] = []
        if eng == "sp":
            assert dma_key is not None
        if dma_key is not None:
            cost = 0.1 if eng == "sp" else 1.0
            lat = 3.0
        if cost is None:
            cost, tset = self._estimate(eng, fn)
        if lat is None:
            lat = cost
        self.all.append(dict(eng=eng, fn=fn, deps=deps, dma_key=dma_key, cost=cost, lat=lat, tset=tset,
                             sig=False, cnt=None, group=group))

    class _Probe:
        def __getattr__(self, name):
            def rec(*a, **k):
                self.call = (name, a, k)
                return None
            return rec

    def _estimate(self, eng, fn):
        pr = Sched._Probe()
        fn(pr)
        name, a, k = pr.call
        out = k.get("out", a[0] if a else None)
        cols = 1
        for d in tuple(out.shape)[1:]:
            cols *= int(d)
        tset = None
        if eng == "pe":
            return max(0.066, cols / 2170.0), None
        if eng == "act":
            f = k.get("func")
            if f in (AF.Exp, AF.Ln):
                tset = 6
            elif f == AF.Silu:
                tset = 18
            elif f == AF.Gelu_apprx_tanh:
                tset = 11
            return 0.22 + cols * 0.00085, tset
        if eng == "dve":
            return 0.12 + cols * 0.00105, None
        if eng == "sp":
            return 0.1, None
        return 0.3 + cols * 0.002, None

    def schedule(self):
        import heapq
        ops = self.all
        n = len(ops)
        ndeps = [len(o["deps"]) for o in ops]
        users = [[] for _ in range(n)]
        for u, o in enumerate(ops):
            for d in o["deps"]:
                users[d].append(u)
        ready_t = [0.0] * n
        fin = [0.0] * n
        future = {e: [] for e in self.ENGS}
        avail = {e: [] for e in self.ENGS}
        free_t = {e: 0.0 for e in self.ENGS}
        order = {e: [] for e in self.ENGS}
        cur_set = [None]
        prio = list(range(n))
        if self.PRIO == "rank":
            rank = [0.0] * n
            for u in range(n - 1, -1, -1):
                m = 0.0
                for v in users[u]:
                    if rank[v] > m:
                        m = rank[v]
                rank[u] = ops[u]["cost"] + m
            top = max(rank)
            prio = [(top - rank[u]) for u in range(n)]
        for u, o in enumerate(ops):
            if ndeps[u] == 0:
                heapq.heappush(future[o["eng"]], (0.0, u))
        groups = {}
        for u, o in enumerate(ops):
            if o["group"] is not None:
                groups.setdefault(o["group"], []).append(u)
        scheduled = [False] * n
        done = 0

        def commit(u, e, t):
            o = ops[u]
            c = o["cost"]
            if e == "act" and o["tset"] is not None and cur_set[0] != o["tset"]:
                c += self.ACT_SWITCH
                cur_set[0] = o["tset"]
            start = max(t, free_t[e], ready_t[u])
            free_t[e] = start + c
            fin[u] = start + (o["lat"] if o["dma_key"] is not None else c)
            order[e].append(u)
            scheduled[u] = True
            for v in users[u]:
                ov = ops[v]
                rt = fin[u] + (0.0 if ov["eng"] == e else self.SYNC_LAT)
                if rt > ready_t[v]:
                    ready_t[v] = rt
                ndeps[v] -= 1
                if ndeps[v] == 0:
                    heapq.heappush(future[ov["eng"]], (ready_t[v], v))

        while done < n:
            best = None
            for e in self.ENGS:
                fu, av = future[e], avail[e]
                while fu and (scheduled[fu[0][1]] or fu[0][0] <= free_t[e]):
                    v_ = heapq.heappop(fu)[1]
                    if not scheduled[v_]:
                        heapq.heappush(av, (prio[v_], v_))
                while av and scheduled[av[0][1]]:
                    heapq.heappop(av)
                if av:
                    cand = (free_t[e], av[0][1], e)
                elif fu:
                    cand = (fu[0][0], fu[0][1], e)
                else:
                    continue
                if best is None or cand < best:
                    best = cand
            t, u, e = best
            if avail[e] and avail[e][0][1] == u:
                heapq.heappop(avail[e])
            else:
                heapq.heappop(future[e])
            commit(u, e, t)
            done += 1
            g_ = ops[u]["group"]
            if g_ is not None:
                for v in groups[g_]:
                    if not scheduled[v]:
                        assert ndeps[v] == 0 and ops[v]["eng"] == e, "group members must share engine and inputs"
                        commit(v, e, free_t[e])
                        done += 1
        self.order = order
        self.est_us = max(free_t.values())

    def emit(self, stack):
        nc = self.nc
        self.schedule()
        ops = self.all
        pos = {}
        for e in self.ENGS:
            for i, u in enumerate(self.order[e]):
                pos[u] = i
        for u, o in enumerate(ops):
            sd = {}
            for d in o["deps"]:
                de = ops[d]["eng"]
                if de == "pe" and o["eng"] == "pe":
                    continue
                key = ("dma", ops[d]["dma_key"]) if ops[d]["dma_key"] is not None else (de, None)
                if key not in sd or pos[sd[key]] < pos[d]:
                    sd[key] = d
            o["sdeps"] = sd
            for d in sd.values():
                ops[d]["sig"] = True
        sems = {}
        for e in ("pe", "act", "dve", "pool"):
            sems[e] = stack.enter_context(nc.semaphore("s_" + e))
            c = 0
            for u in self.order[e]:
                if ops[u]["sig"] and ops[u]["dma_key"] is None:
                    c += 1
                    ops[u]["cnt"] = c
        dsem, dcount, dkey_eng = {}, {}, {}
        for e in self.ENGS:
            for u in self.order[e]:
                o = ops[u]
                k = o["dma_key"]
                if k is None:
                    continue
                assert dkey_eng.setdefault(k, e) == e, "a DMA key must stay on one queue"
                if k not in dsem:
                    dsem[k] = stack.enter_context(nc.semaphore("d_" + str(k)))
                    dcount[k] = 0
                dcount[k] += 16
                o["cnt"] = dcount[k]
                o["sem"] = dsem[k]
        block = stack.enter_context(nc.Block())
        engmap = {"pe": block.tensor, "act": block.scalar, "dve": block.vector,
                  "pool": block.gpsimd, "sp": block.sync}
        final_waits = [(dsem[k], dcount[k]) for k in dsem]
        for e in self.ENGS:
            lst = self.order[e]

            def body(eng, lst=lst, e=e):
                waited = {}
                for u in lst:
                    o = ops[u]
                    for (key, d) in o["sdeps"].items():
                        src = ops[d]
                        if waited.get(key, 0) >= src["cnt"]:
                            continue
                        waited[key] = src["cnt"]
                        eng.wait_ge(src["sem"] if key[0] == "dma" else sems[key[0]], src["cnt"])
                    ins = o["fn"](eng)
                    if o["dma_key"] is not None:
                        ins.then_inc(o["sem"], 16)
                    elif o["sig"]:
                        ins.then_inc(sems[e], 1)
                if e == "sp":
                    for (s_, c_) in final_waits:
                        eng.wait_ge(s_, c_)
            engmap[e](body)


def build_nc(jobs):
    n_x_rows = max(j[2] + j[0] * 128 for j in jobs)
    n_o_rows = max(j[3] + j[1] * 128 for j in jobs)
    max_own = max(j[1] for j in jobs)
    nc = bass.Bass("TRN2", target_bir_lowering=False)
    x_d = nc.dram_tensor("x", [n_x_rows, D], F32, kind="ExternalInput").ap()
    y_d = nc.dram_tensor("y", [n_o_rows, D], F32, kind="ExternalOutput").ap()
    win_d = nc.dram_tensor("w_in", [D, DIN], F32, kind="ExternalInput").ap()
    wout_d = nc.dram_tensor("w_out", [D, D], F32, kind="ExternalInput").ap()
    gpre_d = nc.dram_tensor("gpre", [128, 8], F32, kind="ExternalInput").ap()
    wspT_d = nc.dram_tensor("wspT", [128, 4 * 128], F32, kind="ExternalInput").ap()
    bsp_d = nc.dram_tensor("bsp", [128, 4], F32, kind="ExternalInput").ap()
    gv_d = nc.dram_tensor("gv", [512], F32, kind="ExternalInput").ap()
    wgk_d = nc.dram_tensor("wgk", [33, 512], F32, kind="ExternalInput").ap()
    gnb_d = nc.dram_tensor("gnb", [512], F32, kind="ExternalInput").ap()
    gpost_d = nc.dram_tensor("gpost", [D], F32, kind="ExternalInput").ap()

    with ExitStack() as st:
        def T(name, shape, dt=F32):
            return st.enter_context(nc.sbuf_tensor(name, shape, dt))

        S = Sched(nc)
        ident = T("ident", [128, 128], BF16)
        maskF = T("maskF", [128, 512], BF16)
        maskB = T("maskB", [128, 512], BF16)
        triF = T("triF", [128, 128], BF16)
        triB = T("triB", [128, 128], BF16)
        negcol = T("negcol", [128, 1], BF16)
        Wb = T("Wb", [128, 8, DIN], BF16)
        Wo = T("Wo", [128, 8, D], BF16)
        yt = T("yt", [128, D], F32)
        NB = 2
        UV = [T("UV%d" % i, [128, D], F32) for i in range(NB)]
        ZZ = [T("ZZ%d" % i, [128, D], F32) for i in range(NB)]
        gpre = T("gpre_s", [128, 8], F32)
        wspT = T("wspT_s", [128, 512], BF16)
        bsp = T("bsp_s", [128, 4], F32)
        gvb = T("gvb", [128, 512], F32)
        wgk = T("wgk_s", [33, 512], BF16)
        gnb = T("gnb_s", [128, 512], F32)
        gpost = T("gpost_s", [128, D], F32)
        lrT = T("lrT", [33, 128], BF16)

        def P(e, f, r=(), w=(), c=None, ts=None, g=None):
            S.op(e, f, reads=r, writes=w, cost=c, tset=ts, group=g)

        P("pool", lambda e: e.memset(ident[:], 1.0), w=["ident"])
        P("pool", lambda e: e.affine_select(out=ident[:], in_=ident[:], pattern=[[-1, 128]], compare_op=ALU.is_equal,
                                            fill=0.0, base=0, channel_multiplier=1), r=["ident"], w=["ident"])
        P("pool", lambda e: e.memset(maskF[:], 1.0), w=["maskF"])
        P("pool", lambda e: e.affine_select(out=maskF[:].rearrange("p (h c) -> p h c", h=4), in_=maskF[:].rearrange("p (h c) -> p h c", h=4),
                                            pattern=[[0, 4], [1, 128]], compare_op=ALU.is_ge, fill=0.0, base=0,
                                            channel_multiplier=-1), r=["maskF"], w=["maskF"])
        P("pool", lambda e: e.memset(maskB[:], 1.0), w=["maskB"])
        P("pool", lambda e: e.affine_select(out=maskB[:].rearrange("p (h c) -> p h c", h=4), in_=maskB[:].rearrange("p (h c) -> p h c", h=4),
                                            pattern=[[0, 4], [-1, 128]], compare_op=ALU.is_gt, fill=0.0, base=0,
                                            channel_multiplier=1), r=["maskB"], w=["maskB"])
        P("pool", lambda e: e.memset(triF[:], -1.0 / 16), w=["triF"])
        P("pool", lambda e: e.affine_select(out=triF[:], in_=triF[:], pattern=[[1, 128]], compare_op=ALU.is_ge, fill=0.0,
                                            base=0, channel_multiplier=-1), r=["triF"], w=["triF"])
        P("pool", lambda e: e.memset(triB[:], -1.0 / 16), w=["triB"])
        P("pool", lambda e: e.affine_select(out=triB[:], in_=triB[:], pattern=[[-1, 128]], compare_op=ALU.is_ge, fill=0.0,
                                            base=0, channel_multiplier=1), r=["triB"], w=["triB"])
        P("pool", lambda e: e.memset(negcol[:], -1.0 / 16), w=["negcol"])
        P("pool", lambda e: e.memset(lrT[32:33, :], 1.0), w=["lrT_one"])

        S.op("sp", lambda e: e.dma_start(out=gpre[:], in_=gpre_d[:, :]), writes=["gpre"], dma_key="gpre")
        NC = 25
        vbc = T("vbc", [128, NC, 512], BF16)
        stg = [(UV[0], ["U0", "V0"], "stg0"), (ZZ[0], ["ZA0", "ZB0"], "stg1"),
               (UV[1], ["U1", "V1"], "stg2"), (ZZ[1], ["ZA1", "ZB1"], "stg3")]
        for j in range(4):
            view = vbc[:, 4 * j:4 * j + 4, :].bitcast(F32).rearrange("p a c -> p (a c)")
            stg.append((view, ["vbc%d" % (4 * j + i_) for i_ in range(4)], "stg%d" % (4 + j)))
        NSTG = [len(stg)]
        stg_n = [0]

        def wcol_ids(c0, c1):
            ids = []
            for (a, b_, nm) in ((0, 1792, "WbA"), (1792, 2592, "WbB"), (2592, 3104, "WbC")):
                if c0 < b_ and c1 > a:
                    ids.append(nm)
            return ids

        def load_win_piece(k, c0, c1):
            buf, ids, key = stg[stg_n[0] % NSTG[0]]
            use_dve = (stg_n[0] % 2 == 0)
            stg_n[0] += 1
            S.op("sp", lambda e: e.dma_start(out=buf[:, 0:c1 - c0], in_=win_d[k * 128:(k + 1) * 128, c0:c1]), writes=ids, dma_key=key)
            if use_dve:
                P("dve", lambda e: e.tensor_scalar(out=Wb[:, k, c0:c1], in0=buf[:, 0:c1 - c0], scalar1=gpre[:, k:k + 1], scalar2=None, op0=ALU.mult),
                  r=ids + ["gpre"], w=wcol_ids(c0, c1))
            else:
                P("act", lambda e: e.activation(out=Wb[:, k, c0:c1], in_=buf[:, 0:c1 - c0], func=AF.Identity, scale=gpre[:, k:k + 1]),
                  r=ids + ["gpre"], w=wcol_ids(c0, c1))

        def load_wout_piece(k):
            buf, ids, key = stg[stg_n[0] % NSTG[0]]
            use_dve = (stg_n[0] % 2 == 0)
            stg_n[0] += 1
            S.op("sp", lambda e: e.dma_start(out=buf[:, :], in_=wout_d[k * 128:(k + 1) * 128, :]), writes=ids, dma_key=key)
            if use_dve:
                P("dve", lambda e: e.tensor_copy(out=Wo[:, k, :], in_=buf[:, :]), r=ids, w=["Wo"])
            else:
                P("act", lambda e: e.activation(out=Wo[:, k, :], in_=buf[:, :], func=AF.Copy), r=ids, w=["Wo"])

        for k in range(8):
            load_win_piece(k, 1792, 2592)
        NSTG[0] = 4
        deferred = []
        for (c0, c1) in ((0, 1024), (1024, 1792), (2592, 3104)):
            for k in range(8):
                deferred.append(lambda k=k, c0=c0, c1=c1: load_win_piece(k, c0, c1))
        for k in range(8):
            deferred.append(lambda k=k: load_wout_piece(k))
        S.op("sp", lambda e: e.dma_start(out=yt[0:33, 0:512], in_=wgk_d[:, :]), writes=["yt0", "yt1"], dma_key="ytst")
        P("dve", lambda e: e.tensor_copy(out=wgk[:], in_=yt[0:33, 0:512]), r=["yt0", "yt1"], w=["wgk"])
        S.op("sp", lambda e: e.dma_start(out=yt[:, 0:512], in_=wspT_d[:, :]), writes=["yt0", "yt1"], dma_key="ytst")
        P("dve", lambda e: e.tensor_copy(out=wspT[:], in_=yt[:, 0:512]), r=["yt0", "yt1"], w=["wspT"])
        S.op("sp", lambda e: e.dma_start(out=bsp[:], in_=bsp_d[:, :]), writes=["bsp"], dma_key="bsp")
        S.op("sp", lambda e: e.dma_start(out=gvb[:], in_=gv_d.partition_broadcast(128)), writes=["gvb"], dma_key="gvb")
        S.op("sp", lambda e: e.dma_start(out=gnb[:], in_=gnb_d.partition_broadcast(128)), writes=["gnb"], dma_key="gnb")
        S.op("sp", lambda e: e.dma_start(out=gpost[:], in_=gpost_d.partition_broadcast(128)), writes=["gpost"], dma_key="gpost")

        NXS = 2
        xs = [T("xs%d" % i, [128, D], F32) for i in range(NXS)]
        xbf = [T("xbf%d" % i, [128, D], BF16) for i in range(2)]
        xT = [T("xT%d" % i, [128, 8, 128], BF16) for i in range(2)]
        junk = T("junk", [128, D], BF16)
        junky = T("junky", [128, D], BF16)
        sm = [T("sm%d" % i, [128, 32], F32) for i in range(4)]

        def slots(name, shape, dt=F32, n=NB):
            return [T("%s%d" % (name, i), shape, dt) for i in range(n)]
        U_ = [UV[i][:, 0:512] for i in range(NB)]
        V_ = [UV[i][:, 512:1024] for i in range(NB)]
        ZA = [ZZ[i][:, 0:512] for i in range(NB)]
        ZB = [ZZ[i][:, 512:1024] for i in range(NB)]
        def vbsrc(job, n, b):
            if 1 <= n < job[1] - 1 and n - 1 < NC:
                return (lambda lo, hi: vbc[:, n - 1, lo:hi]), "vbc%d" % (n - 1), True
            return (lambda lo, hi: vb[b][:, lo:hi]), "vb%d" % b, False
        TS = T("TS", [128, 512], F32)
        ON = T("ON", [128, 512], F32)
        qk = slots("qk", [128, 512])
        scr = T("scr", [128, 512], F32)
        scr2 = T("scr2", [128, 512], F32)
        Ep = T("Ep", [128, 512], F32)
        Em = T("Em", [128, 512], F32)
        vlnb = slots("vlnb", [128, 512], BF16)
        vb = slots("vb", [128, 512], BF16)
        lrs = slots("lrs", [128, 32], BF16)
        spb = slots("spb", [128, 512], BF16)
        qin = slots("qin", [128, 1024], BF16)
        kin = slots("kin", [128, 512], BF16)
        qT = slots("qT", [128, 8, 128], BF16)
        kT = slots("kT", [128, 4, 128], BF16)
        scf = slots("scf", [128, 512], BF16)
        scb = slots("scb", [128, 512], BF16)
        cat = T("cat", [128, D], BF16)
        catT = T("catT", [128, 8, 128], BF16)
        dec = [T("dec%d" % i, [128, 2], F32) for i in range(3)]
        Tst = T("Tst", [128, 512], F32)
        Sf = [T("Sf%d" % i, [128, 512], BF16) for i in range(2)]
        Sb = T("Sb", [128, max_own, 256], BF16)

        NPB = 6
        pb = [st.enter_context(nc.psum_tensor("pb%d" % i, [128, 512], F32)) for i in range(NPB)]
        pt = [st.enter_context(nc.psum_tensor("pt%d" % i, [128, 1024], BF16)) for i in range(2)]
        cnt = dict(bank=0, pth=0, tile=0, dec=0)
        for i in range(NB):
            P("pool", lambda e, i=i: e.memset(qin[i][:], 0.0), w=["qin%d_0" % i, "qin%d_1" % i])

        def bank():
            i = cnt["bank"] % NPB
            cnt["bank"] += 1
            return pb[i], "pb%d" % i

        def ptbank():
            i = cnt["pth"] % 2
            cnt["pth"] += 1
            return pt[i], "pt%d" % i

        def bankA():
            i = cnt["bank"] % 2
            cnt["bank"] += 1
            return pb[i], "pb%d" % i

        def bankB():
            i = 2 + cnt["bankb"] % 2
            cnt["bankb"] += 1
            return pb[i], "pb%d" % i
        cnt["bankb"] = 0

        def load_x(row0, g, orow=None):
            s3 = g % NXS
            S.op("sp", lambda e: e.dma_start(out=xs[s3][:], in_=x_d[row0:row0 + 128, :]), writes=["xs%d" % s3],
                 dma_key="xs%d" % s3)
            if orow is not None:
                S.op("pool", lambda e: e.dma_start(out=y_d[orow:orow + 128, :], in_=x_d[row0:row0 + 128, :]),
                     writes=["yrow%d" % orow, "ycpk%d" % (g % 4)], dma_key="ycp%d" % (g % 4))

        rstdc = T("rstdc", [128, 64], F32)
        lrc = T("lrc", [128, max_own, 32], BF16)

        def lrsrc(job, n, b):
            if 1 <= n < job[1]:
                return lrc[:, n, :], "lrc%d" % n, True
            return lrs[b][:], "lrs%d" % b, False

        def front1(g):
            s3, s2, ss = g % NXS, g % 2, g % 4
            kind, job, n, _ = tiles[g]
            own = (1 <= n < job[1])
            if kind == "p2" and own:
                rs, rsid = rstdc[:, n:n + 1], "rstdc%d" % n
            else:
                rs, rsid = (rstdc[:, n:n + 1], "rstdc%d" % n) if own else (sm[ss][:, 2:3], "sm%dc" % ss)
                P("act", lambda e: e.activation(out=junk[:], in_=xs[s3][:], func=AF.Square, accum_out=sm[ss][:, 0:1]),
                  r=["xs%d" % s3], w=["sm%da" % ss, "junk"])
                P("act", lambda e: e.activation(out=sm[ss][:, 1:2], in_=sm[ss][:, 0:1], func=AF.Ln, scale=1.0 / D, bias=EPS),
                  r=["sm%da" % ss], w=["sm%db" % ss])
                P("act", lambda e: e.activation(out=rs, in_=sm[ss][:, 1:2], func=AF.Exp, scale=-0.5),
                  r=["sm%db" % ss], w=[rsid])
            P("dve", lambda e: e.tensor_scalar(out=xbf[s2][:], in0=xs[s3][:], scalar1=rs, scalar2=None, op0=ALU.mult),
              r=["xs%d" % s3, rsid], w=["xbf%d" % s2])

        def front2(g):
            s2 = g % 2
            ph, pid = ptbank()
            for k in range(8):
                P("pe", lambda e, ph=ph, k=k: e.transpose(out=ph[:, k * 128:(k + 1) * 128],
                                                          in_=xbf[s2][:, k * 128:(k + 1) * 128], identity=ident[:]),
                  r=["xbf%d" % s2, "ident"], w=[pid])
            P("dve", lambda e, ph=ph: e.tensor_copy(out=xT[s2][:], in_=ph[:, :].rearrange("p (k t) -> p k t", k=8)),
              r=[pid], w=["xT%d" % s2])

        def genF(g):
            for _ in range(3):
                yield
            front1(g)
            yield
            yield
            front2(g)
            yield

        def inproj(s2, c0, c1):
            bk, bid = bankA()
            wids = wcol_ids(c0, c1)
            for k in range(8):
                P("pe", lambda e, k=k, bk=bk: e.matmul(bk[:, 0:c1 - c0], lhsT=xT[s2][:, k, :], rhs=Wb[:, k, c0:c1],
                                                       start=(k == 0), stop=(k == 7)),
                  r=["xT%d" % s2] + wids, w=[bid])
            return bk, bid

        def gate_chain(b, ncols, c_lo, lr_ap, lr_id):
            ph, pid = ptbank()
            P("pe", lambda e: e.transpose(out=ph[0:32, 0:128], in_=lr_ap, identity=ident[:]),
              r=[lr_id, "ident"], w=[pid])
            P("dve", lambda e: e.tensor_copy(out=lrT[0:32, :], in_=ph[0:32, 0:128]), r=[pid], w=["lrT"])
            bk, bid = bankB()
            P("pe", lambda e: e.matmul(bk[:, 0:ncols], lhsT=lrT[0:33, :], rhs=wgk[0:33, c_lo:c_lo + ncols], start=True, stop=True),
              r=["lrT", "lrT_one", "wgk"], w=[bid])
            P("act", lambda e: e.activation(out=scr2[:, 0:ncols], in_=bk[:, 0:ncols], func=AF.Exp, scale=-1.0),
              r=[bid], w=["scr2"])
            P("act", lambda e: e.activation(out=spb[b][:, 0:ncols], in_=scr2[:, 0:ncols], func=AF.Ln, bias=1.0),
              r=["scr2"], w=["spb%d" % b])

        def p1_A(job, n, first, g):
            b, s2 = g % NB, g % 2
            bk, bid = inproj(s2, 1792, 2080)
            lr_ap, lr_id, _ = lrsrc(job, n, b)
            P("act", lambda e: e.activation(out=lr_ap, in_=bk[:, 256:288], func=AF.Copy), r=[bid], w=[lr_id])
            P("act", lambda e: e.activation(out=qk[b][:, 256:512], in_=bk[:, 0:256], func=AF.Copy), r=[bid], w=["qk%d" % b])
            yield
            bk2, bid2 = inproj(s2, 2080, 2592)
            vget, vid, _ = vbsrc(job, n, b)
            P("dve", lambda e: e.tensor_copy(out=vget(0, 512), in_=bk2[:, :]), r=[bid2], w=[vid])
            yield

        def p1_B(job, n, first, g):
            ntot, nown, xr0, _ = job
            b = g % NB
            lr_ap, lr_id, _ = lrsrc(job, n, b)
            gate_chain(b, 256, 256, lr_ap, lr_id)
            yield
            bc, bcid = bankB()
            P("pe", lambda e: e.matmul(bc[:, 0:256], lhsT=triB[:], rhs=spb[b][:, 0:256], start=True, stop=True),
              r=["triB", "spb%d" % b], w=[bcid])
            bl, blid = bankB()
            for pr in range(2):
                P("pe", lambda e, pr=pr: e.matmul(bl[:, pr:pr + 1], lhsT=spb[b][:, pr * 128:(pr + 1) * 128], rhs=negcol[:, 0:1],
                                                  start=True, stop=True), r=["negcol", "spb%d" % b], w=[blid])
            P("act", lambda e: e.activation(out=Em[:, 0:256], in_=bc[:, 0:256], func=AF.Exp, scale=-1.0), r=[bcid], w=["Em"])
            dcur = cnt["dec"] % 3
            dprev = (cnt["dec"] - 1) % 3
            cnt["dec"] += 1
            P("act", lambda e: e.activation(out=dec[dcur][:, 0:2], in_=bl[:, 0:2], func=AF.Exp), r=[blid], w=["dec%d" % dcur])
            P("dve", lambda e: e.tensor_tensor(out=kin[b][:, 0:256], in0=Em[:, 0:256], in1=qk[b][:, 256:512], op=ALU.mult),
              r=["Em", "qk%d" % b], w=["kin%d" % b])
            yield
            kv, kvid = bankB()
            vget, vid, _ = vbsrc(job, n, b)
            for pr in range(2):
                P("pe", lambda e, pr=pr: e.matmul(kv[:, pr * 256:(pr + 1) * 256], lhsT=kin[b][:, pr * 128:(pr + 1) * 128],
                                                  rhs=vget(pr * 256, (pr + 1) * 256), start=True, stop=True),
                  r=["kin%d" % b, vid], w=[kvid])
            if first:
                P("dve", lambda e: e.tensor_copy(out=Tst[:], in_=kv[:, :]), r=[kvid], w=["Tst0", "Tst1"])
            else:
                for pr in range(2):
                    P("dve", lambda e, pr=pr: e.scalar_tensor_tensor(out=Tst[:, pr * 256:(pr + 1) * 256], in0=Tst[:, pr * 256:(pr + 1) * 256],
                                                                     scalar=dec[dprev][:, pr:pr + 1], in1=kv[:, pr * 256:(pr + 1) * 256],
                                                                     op0=ALU.mult, op1=ALU.add),
                      r=[kvid, "Tst%d" % pr, "dec%d" % dprev], w=["Tst%d" % pr])
            if 1 <= n <= nown:
                for hh in range(2):
                    rs = slice(hh * 64, (hh + 1) * 64)
                    P("dve", lambda e, hh=hh, rs=rs: e.tensor_tensor(
                        out=Sb[rs, n - 1, :].rearrange("p (r c) -> p r c", r=2),
                        in0=Tst[rs, :].rearrange("p (r x) -> p r x", r=2)[:, :, hh * 128:(hh + 1) * 128],
                        in1=dec[dcur][rs, 0:2].unsqueeze(2).to_broadcast([64, 2, 128]), op=ALU.mult),
                      r=["Tst0", "Tst1", "dec%d" % dcur], w=["Sb%d_0%d" % (n - 1, hh), "Sb%d_1%d" % (n - 1, hh)])
            yield

        def p2_A(job, n, first, g):
            b, s2, ss = g % NB, g % 2, g % 4
            if not lrsrc(job, n, b)[2]:
                bk, bid = inproj(s2, 2048, 2080)
                P("act", lambda e, bk=bk: e.activation(out=lrs[b][:], in_=bk[:, 0:32], func=AF.Copy), r=[bid], w=["lrs%d" % b])
                yield
            bk, bid = inproj(s2, 1536, 2048)
            P("act", lambda e, bk=bk: e.activation(out=qk[b][:], in_=bk[:, :], func=AF.Copy), r=[bid], w=["qk%d" % b])
            yield
            if not vbsrc(job, n, b)[2]:
                bk, bid = inproj(s2, 2080, 2592)
                P("dve", lambda e, bk=bk: e.tensor_copy(out=vb[b][:], in_=bk[:, :]), r=[bid], w=["vb%d" % b])
                yield
            bk, bid = inproj(s2, 1024, 1536)
            P("act", lambda e, bk=bk: e.activation(out=ZA[b][:], in_=bk[:, :], func=AF.Copy), r=[bid], w=["ZA%d" % b])
            yield
            bk, bid = inproj(s2, 2592, 3104)
            P("dve", lambda e, bk=bk: e.tensor_copy(out=ZB[b][:], in_=bk[:, :]), r=[bid], w=["ZB%d" % b])
            yield
            bk, bid = inproj(s2, 0, 512)
            P("act", lambda e, bk=bk: e.activation(out=U_[b][:], in_=bk[:, :], func=AF.Copy), r=[bid], w=["U%d" % b])
            yield
            bk, bid = inproj(s2, 512, 1024)
            P("dve", lambda e, bk=bk: e.tensor_copy(out=V_[b][:], in_=bk[:, :]), r=[bid], w=["V%d" % b])
            yield
            yield
            yield

            zin, uin = ["ZA%d" % b, "ZB%d" % b], ["U%d" % b, "V%d" % b]
            P("act", lambda e: e.activation(out=ZA[b][:], in_=ZA[b][:], func=AF.Silu), r=zin, w=["ZA%d" % b], g="sil%d" % g)
            P("act", lambda e: e.activation(out=ZB[b][:], in_=ZB[b][:], func=AF.Silu), r=zin, w=["ZB%d" % b], g="sil%d" % g)
            P("act", lambda e: e.activation(out=U_[b][:], in_=U_[b][:], func=AF.Gelu_apprx_tanh), r=uin, w=["U%d" % b], g="gel%d" % g)
            P("act", lambda e: e.activation(out=V_[b][:], in_=V_[b][:], func=AF.Gelu_apprx_tanh), r=uin, w=["V%d" % b], g="gel%d" % g)
            P("dve", lambda e: e.bn_stats(out=sm[ss][:, 8:14], in_=V_[b][:]), r=["V%d" % b], w=["sm%dd" % ss])
            P("dve", lambda e: e.bn_aggr(out=sm[ss][:, 14:16], in_=sm[ss][:, 8:14]), r=["sm%dd" % ss], w=["sm%de" % ss])
            P("act", lambda e: e.activation(out=sm[ss][:, 16:17], in_=sm[ss][:, 15:16], func=AF.Ln, bias=EPS), r=["sm%de" % ss], w=["sm%df" % ss])
            P("act", lambda e: e.activation(out=sm[ss][:, 17:18], in_=sm[ss][:, 16:17], func=AF.Exp, scale=-0.5), r=["sm%df" % ss], w=["sm%dg" % ss])
            P("dve", lambda e: e.tensor_scalar(out=V_[b][:], in0=V_[b][:], scalar1=sm[ss][:, 14:15], scalar2=sm[ss][:, 17:18],
                                               op0=ALU.subtract, op1=ALU.mult), r=["V%d" % b, "sm%de" % ss, "sm%dg" % ss], w=["V%d" % b])
            P("dve", lambda e: e.tensor_tensor(out=vlnb[b][:], in0=V_[b][:], in1=gvb[:], op=ALU.mult),
              r=["V%d" % b, "gvb"], w=["vlnb%d" % b])
            P("dve", lambda e: e.tensor_tensor(out=U_[b][:], in0=U_[b][:], in1=ZA[b][:], op=ALU.mult),
              r=["U%d" % b, "ZA%d" % b], w=["U%d" % b])
            P("dve", lambda e: e.tensor_tensor(out=ZB[b][:], in0=ZB[b][:], in1=gnb[:], op=ALU.mult),
              r=["ZB%d" % b, "gnb"], w=["ZB%d" % b])
            yield

        def p2_B(job, n, first, g):
            ntot, nown, xr0, or0 = job
            b, s3, ss = g % NB, g % NXS, g % 4
            last_in_seq = (n == ntot - 1)
            lr_ap, lr_id, _ = lrsrc(job, n, b)
            gate_chain(b, 512, 0, lr_ap, lr_id)
            yield
            bc, bcid = bankB()
            P("pe", lambda e: e.matmul(bc[:, 0:256], lhsT=triF[:], rhs=spb[b][:, 0:256], start=True, stop=True),
              r=["triF", "spb%d" % b], w=[bcid])
            P("pe", lambda e: e.matmul(bc[:, 256:512], lhsT=triB[:], rhs=spb[b][:, 256:512], start=True, stop=True),
              r=["triB", "spb%d" % b], w=[bcid])
            bl, blid = bankB()
            for pr in range(2):
                P("pe", lambda e, pr=pr: e.matmul(bl[:, pr:pr + 1], lhsT=spb[b][:, pr * 128:(pr + 1) * 128], rhs=negcol[:, 0:1],
                                                  start=True, stop=True), r=["negcol", "spb%d" % b], w=[blid])
            P("act", lambda e: e.activation(out=Ep[:], in_=bc[:, :], func=AF.Exp, bias=math.log(0.125)), r=[bcid], w=["Ep"])
            P("act", lambda e: e.activation(out=Em[:], in_=bc[:, :], func=AF.Exp, scale=-1.0), r=[bcid], w=["Em"])
            dcur = cnt["dec"] % 3
            dprev = (cnt["dec"] - 1) % 3
            cnt["dec"] += 1
            P("act", lambda e: e.activation(out=dec[dcur][:, 0:2], in_=bl[:, 0:2], func=AF.Exp), r=[blid], w=["dec%d" % dcur])
            for hh in range(2):
                P("dve", lambda e, hh=hh: e.tensor_tensor(
                    out=qin[b][:].rearrange("p (t r h c) -> p t r h c", t=2, r=2, h=2)[:, :, :, hh, hh * 64:(hh + 1) * 64],
                    in0=Ep[:].rearrange("p (t r h d) -> p t r h d", t=2, r=2, h=2)[:, :, :, hh, :],
                    in1=qk[b][:, 0:256].rearrange("p (r h d) -> p r h d", r=2, h=2)[:, :, hh, :].unsqueeze(1).to_broadcast([128, 2, 2, 64]),
                    op=ALU.mult), r=["Ep", "qk%d" % b], w=["qin%d_%d" % (b, hh)])
            P("dve", lambda e: e.tensor_tensor(out=kin[b][:].rearrange("p (t d) -> p t d", t=2), in0=Em[:].rearrange("p (t d) -> p t d", t=2),
                                               in1=qk[b][:, 256:512].unsqueeze(1).to_broadcast([128, 2, 256]), op=ALU.mult),
              r=["Em", "qk%d" % b], w=["kin%d" % b])
            sv, svid = bankB()
            for h in range(4):
                P("pe", lambda e, h=h: e.matmul(sv[:, h * 128:(h + 1) * 128], lhsT=wspT[:, h * 128:(h + 1) * 128],
                                                rhs=vlnb[b][:, h * 128:(h + 1) * 128], start=True, stop=True),
                  r=["wspT", "vlnb%d" % b], w=[svid])
            P("dve", lambda e: e.tensor_tensor(out=TS[:].rearrange("p (h c) -> p h c", h=4), in0=sv[:, :].rearrange("p (h c) -> p h c", h=4),
                                               in1=bsp[:, 0:4].unsqueeze(2).to_broadcast([128, 4, 128]), op=ALU.add),
              r=[svid, "bsp"], w=["TS"])
            P("dve", lambda e: e.tensor_tensor(out=cat[:, 0:512], in0=TS[:], in1=U_[b][:], op=ALU.mult),
              r=["TS", "U%d" % b], w=["cata"])
            yield
            ph, pid = ptbank()
            for k in range(8):
                P("pe", lambda e, ph=ph, k=k: e.transpose(out=ph[:, k * 128:(k + 1) * 128], in_=qin[b][:, k * 128:(k + 1) * 128],
                                                          identity=ident[:]), r=["qin%d_0" % b, "qin%d_1" % b, "ident"], w=[pid])
            P("act", lambda e, ph=ph: e.activation(out=qT[b][:], in_=ph[:, :].rearrange("p (k t) -> p k t", k=8), func=AF.Copy),
              r=[pid], w=["qT%d" % b])
            ph2, pid2 = ptbank()
            for k in range(4):
                P("pe", lambda e, k=k: e.transpose(out=ph2[:, k * 128:(k + 1) * 128], in_=kin[b][:, k * 128:(k + 1) * 128],
                                                   identity=ident[:]), r=["kin%d" % b, "ident"], w=[pid2])
            P("dve", lambda e: e.tensor_copy(out=kT[b][:], in_=ph2[:, 0:512].rearrange("p (k t) -> p k t", k=4)), r=[pid2], w=["kT%d" % b])
            yield
            for (t, dstb, dstid, mk, mkid) in ((0, scf[b], "scf%d" % b, maskF, "maskF"), (1, scb[b], "scb%d" % b, maskB, "maskB")):
                sc, scid = bankB()
                for h in range(4):
                    P("pe", lambda e, h=h, sc=sc, t=t: e.matmul(sc[:, h * 128:(h + 1) * 128], lhsT=kT[b][:, t * 2 + h // 2, :],
                                                                rhs=qT[b][:, t * 4 + h, :], start=True, stop=True),
                      r=["qT%d" % b, "kT%d" % b], w=[scid])
                P("dve", lambda e, sc=sc, dstb=dstb, mk=mk: e.tensor_tensor(out=dstb[:], in0=sc[:, :], in1=mk[:], op=ALU.mult),
                  r=[scid, mkid], w=[dstid])
            kv, kvid = bankB()
            vget, vid, _ = vbsrc(job, n, b)
            for pr in range(2):
                P("pe", lambda e, pr=pr: e.matmul(kv[:, pr * 256:(pr + 1) * 256], lhsT=kin[b][:, pr * 128:(pr + 1) * 128],
                                                  rhs=vget(pr * 256, (pr + 1) * 256), start=True, stop=True),
                  r=["kin%d" % b, vid], w=[kvid])
            yield
            sfc = n % 2
            ob, obid = bankB()
            for h in range(4):
                pr, hh = h // 2, h % 2
                oc = ob[:, h * 128:(h + 1) * 128]
                steps = ["scf", "scb"] + (["sf"] if n > 0 else []) + ([] if last_in_seq else ["sb"])
                for i, kind in enumerate(steps):
                    f_, l_ = (i == 0), (i == len(steps) - 1)
                    if kind == "scf":
                        P("pe", lambda e, oc=oc, h=h, f_=f_, l_=l_: e.matmul(oc, lhsT=scf[b][:, h * 128:(h + 1) * 128],
                                                                             rhs=vget(h * 128, (h + 1) * 128), start=f_, stop=l_),
                          r=["scf%d" % b, vid], w=[obid])
                    elif kind == "scb":
                        P("pe", lambda e, oc=oc, h=h, f_=f_, l_=l_: e.matmul(oc, lhsT=scb[b][:, h * 128:(h + 1) * 128],
                                                                             rhs=vget(h * 128, (h + 1) * 128), start=f_, stop=l_),
                          r=["scb%d" % b, vid], w=[obid])
                    elif kind == "sf":
                        P("pe", lambda e, oc=oc, pr=pr, hh=hh, f_=f_, l_=l_: e.matmul(
                            oc, lhsT=qT[b][:, 2 * pr + hh, :],
                            rhs=Sf[sfc][:, pr * 256 + hh * 128:pr * 256 + (hh + 1) * 128], start=f_, stop=l_),
                          r=["qT%d" % b, "Sf%d" % sfc], w=[obid])
                    else:
                        P("pe", lambda e, oc=oc, pr=pr, hh=hh, f_=f_, l_=l_: e.matmul(
                            oc, lhsT=qT[b][:, 4 + 2 * pr + hh, :],
                            rhs=Sb[:, n, pr * 128:(pr + 1) * 128], start=f_, stop=l_),
                          r=["qT%d" % b, "Sb%d_%d0" % (n, pr), "Sb%d_%d1" % (n, pr)], w=[obid])
            if n == 0:
                P("dve", lambda e: e.tensor_copy(out=Tst[:], in_=kv[:, :]), r=[kvid], w=["Tst0", "Tst1"])
            else:
                for pr in range(2):
                    P("dve", lambda e, pr=pr: e.scalar_tensor_tensor(out=Tst[:, pr * 256:(pr + 1) * 256], in0=Tst[:, pr * 256:(pr + 1) * 256],
                                                                     scalar=dec[dprev][:, pr:pr + 1], in1=kv[:, pr * 256:(pr + 1) * 256],
                                                                     op0=ALU.mult, op1=ALU.add),
                      r=[kvid, "Tst%d" % pr, "dec%d" % dprev], w=["Tst%d" % pr])
            if n + 1 < nown:
                P("dve", lambda e: e.tensor_tensor(out=Sf[1 - sfc][:].rearrange("p (r c) -> p r c", r=2),
                                                   in0=Tst[:].rearrange("p (r c) -> p r c", r=2),
                                                   in1=dec[dcur][:, 0:2].unsqueeze(2).to_broadcast([128, 2, 256]), op=ALU.mult),
                  r=["Tst0", "Tst1", "dec%d" % dcur], w=["Sf%d" % (1 - sfc)])
            P("act", lambda e: e.activation(out=scr[:], in_=ob[:, :], func=AF.Square), r=[obid], w=["scr"])
            P("dve", lambda e: e.reduce_sum(out=sm[ss][:, 20:24], in_=scr[:].rearrange("p (h c) -> p h c", h=4), axis=AX.X),
              r=["scr"], w=["sm%dh" % ss])
            P("act", lambda e: e.activation(out=sm[ss][:, 24:28], in_=sm[ss][:, 20:24], func=AF.Ln, scale=1.0 / 128, bias=EPS),
              r=["sm%dh" % ss], w=["sm%di" % ss])
            P("act", lambda e: e.activation(out=sm[ss][:, 28:32], in_=sm[ss][:, 24:28], func=AF.Exp, scale=-0.5),
              r=["sm%di" % ss], w=["sm%dj" % ss])
            P("dve", lambda e: e.tensor_tensor(out=ON[:].rearrange("p (h c) -> p h c", h=4), in0=ob[:, :].rearrange("p (h c) -> p h c", h=4),
                                               in1=sm[ss][:, 28:32].unsqueeze(2).to_broadcast([128, 4, 128]), op=ALU.mult),
              r=[obid, "sm%dj" % ss], w=["ON"])
            P("dve", lambda e: e.tensor_tensor(out=cat[:, 512:1024], in0=ON[:], in1=ZB[b][:], op=ALU.mult),
              r=["ON", "ZB%d" % b], w=["catb"])
            yield
            yield
            ph, pid = ptbank()
            for k in range(8):
                P("pe", lambda e, ph=ph, k=k: e.transpose(out=ph[:, k * 128:(k + 1) * 128], in_=cat[:, k * 128:(k + 1) * 128],
                                                          identity=ident[:]), r=["cata", "catb", "ident"], w=[pid])
            P("dve", lambda e, ph=ph: e.tensor_copy(out=catT[:], in_=ph[:, :].rearrange("p (k t) -> p k t", k=8)), r=[pid], w=["catT"])
            yield
            ybk = []
            for c in range(2):
                bk, bid = pb[4 + c], "pb%d" % (4 + c)
                for k in range(8):
                    P("pe", lambda e, k=k, bk=bk, c=c: e.matmul(bk[:, :], lhsT=catT[:, k, :], rhs=Wo[:, k, c * 512:(c + 1) * 512],
                                                                start=(k == 0), stop=(k == 7)),
                      r=["catT", "Wo"], w=[bid])
                ybk.append((bk, bid))
            pend[g] = (ybk, ss, s3, or0 + n * 128)
            yield

        pend = {}

        def p2_C(g):
            ybk, ss, s3, orow = pend.pop(g)
            for c, (bk, bid) in enumerate(ybk):
                P("act", lambda e, bk=bk, c=c: e.activation(out=junky[:, c * 512:(c + 1) * 512], in_=bk[:, :], func=AF.Square,
                                                            accum_out=sm[ss][:, 3 + c:4 + c]), r=[bid], w=["sm%dk%d" % (ss, c), "junky%d" % c])
            yield
            P("dve", lambda e: e.tensor_tensor(out=sm[ss][:, 5:6], in0=sm[ss][:, 3:4], in1=sm[ss][:, 4:5], op=ALU.add),
              r=["sm%dk0" % ss, "sm%dk1" % ss], w=["sm%dl" % ss])
            P("act", lambda e: e.activation(out=sm[ss][:, 6:7], in_=sm[ss][:, 5:6], func=AF.Ln, scale=1.0 / D, bias=EPS),
              r=["sm%dl" % ss], w=["sm%dm" % ss])
            P("act", lambda e: e.activation(out=sm[ss][:, 7:8], in_=sm[ss][:, 6:7], func=AF.Exp, scale=-0.5),
              r=["sm%dm" % ss], w=["sm%dn" % ss])
            yield
            for c, (bk, bid) in enumerate(ybk):
                P("dve", lambda e, bk=bk, c=c: e.scalar_tensor_tensor(out=yt[:, c * 512:(c + 1) * 512], in0=bk[:, :], scalar=sm[ss][:, 7:8],
                                                                      in1=gpost[:, c * 512:(c + 1) * 512], op0=ALU.mult, op1=ALU.mult),
                  r=[bid, "sm%dn" % ss, "gpost"], w=["yt%d" % c])
            S.op("pool", lambda e: e.dma_start(out=y_d[orow:orow + 128, :], in_=yt[:], accum_op=ALU.add),
                 reads=["yt0", "yt1", "yrow%d" % orow], writes=["yrow%d" % orow], dma_key="yst")
            yield

        tiles = []
        for job in jobs:
            ntot, nown, xr0, _ = job
            for i, n in enumerate(range(ntot - 1, 0, -1)):
                tiles.append(("p1", job, n, i == 0))
            for n in range(nown):
                tiles.append(("p2", job, n, False))
        first_p2 = next(i for i, t in enumerate(tiles) if t[0] == "p2")
        AHEAD = 2

        def xrow(t):
            return t[1][2] + t[2] * 128

        def genA(i):
            kind, job, n, first = tiles[i]
            return (p1_A if kind == "p1" else p2_A)(job, n, first, i)

        def genB(i):
            kind, job, n, first = tiles[i]
            return (p1_B if kind == "p1" else p2_B)(job, n, first, i)

        def run_interleaved(gens):
            gens = [g_ for g_ in gens if g_ is not None]
            while gens:
                for g_ in list(gens):
                    try:
                        next(g_)
                    except StopIteration:
                        gens.remove(g_)

        def orow_of(t):
            return (t[1][3] + t[2] * 128) if t[0] == "p2" else None

        for i in range(min(AHEAD, len(tiles))):
            load_x(xrow(tiles[i]), i, orow_of(tiles[i]))
        for i in range(min(2, len(tiles))):
            front1(i)
            front2(i)
        run_interleaved([genA(0)])
        def genC(i):
            return p2_C(i) if (i >= 0 and i in pend) else None

        for i in range(len(tiles) + 1):
            if i + AHEAD < len(tiles):
                load_x(xrow(tiles[i + AHEAD]), i + AHEAD, orow_of(tiles[i + AHEAD]))
            if i + 1 == first_p2:
                while deferred:
                    deferred.pop(0)()
            elif deferred and i < len(tiles) and tiles[i][0] == "p1":
                deferred.pop(0)()
            run_interleaved([genC(i - 1),
                             genB(i) if i < len(tiles) else None,
                             genA(i + 1) if i + 1 < len(tiles) else None,
                             genF(i + 2) if i + 2 < len(tiles) else None])
        S.emit(st)
        nc._mk_total_ops = S.total
    return nc


def _layout(flip, xsamp, xpr, norm_pre, w_in, w_sp, b_sp, g_v_a, w_gk_fwd, b_gk_fwd, w_gk_bwd, b_gk_bwd,
            g_norm_b, w_out, norm_post):
    w_in_c = w_in[0]
    lr_f, lr_b = w_in_c[:, 3072:3088], w_in_c[:, 3088:3104]
    wsp, bsp_ = w_sp[0], b_sp[0]
    gf, bf, gb, bb = w_gk_fwd[0], b_gk_fwd[0], w_gk_bwd[0], b_gk_bwd[0]
    if flip:
        xsamp = xsamp[::-1]
        xpr = xpr[::-1]
        lr_f, lr_b = lr_b, lr_f
        wsp = wsp[:, ::-1, ::-1]
        bsp_ = bsp_[:, ::-1]
        gf, bf, gb, bb = gb, bb, gf, bf
    w_in_c = np.concatenate([w_in_c[:, :2048], lr_f, lr_b, w_in_c[:, 2048:3072]], axis=1)
    wgk = np.zeros((33, 512), np.float32)
    wgk[0:16, 0:256] = gf
    wgk[16:32, 256:512] = gb
    wgk[32, 0:256] = bf
    wgk[32, 256:512] = bb
    return {
        "x": np.ascontiguousarray(np.concatenate([xsamp, xpr], axis=0), dtype=np.float32),
        "w_in": np.ascontiguousarray(w_in_c, dtype=np.float32),
        "w_out": np.ascontiguousarray(w_out[0], dtype=np.float32),
        "gpre": np.ascontiguousarray(norm_pre[0].reshape(8, 128).T, dtype=np.float32),
        "wspT": np.ascontiguousarray(np.transpose(wsp, (2, 0, 1)).reshape(128, 512), dtype=np.float32),
        "bsp": np.ascontiguousarray(bsp_.T, dtype=np.float32),
        "gv": np.ascontiguousarray(g_v_a[0], dtype=np.float32),
        "wgk": wgk,
        "gnb": np.ascontiguousarray(np.tile(g_norm_b[0], 4), dtype=np.float32),
        "gpost": np.ascontiguousarray(norm_post[0], dtype=np.float32),
    }


def kernel(**inputs):
    inputs = {k: np.asarray(v) for k, v in inputs.items()}
    nc = build_nc(FULL_JOBS)
    xp, xsm = inputs.pop("x_prompt"), inputs.pop("x_sample")
    in_maps = [_layout(c % 2 == 1, xsm[c // 2], xp[c], **inputs) for c in range(8)]
    res = run_bass_kernel_spmd(nc, in_maps, core_ids=list(range(8)))
    y_prompt = np.empty((8, 2048, D), np.float32)
    y_sample = np.empty((4, 8192, D), np.float32)
    for c in range(8):
        y = np.asarray(res.results[c]["y"])
        ys, yp = y[:4096], y[4096:6144]
        if c % 2 == 1:
            y_sample[c // 2, 4096:] = ys[::-1]
            y_prompt[c] = yp[::-1]
        else:
            y_sample[c // 2, :4096] = ys
            y_prompt[c] = yp
    return (y_prompt, y_sample)
```

```python
import math
from contextlib import ExitStack

import numpy as np
import concourse.bass as bass
import concourse.mybir as mybir
from concourse.bass_utils import run_bass_kernel_spmd

F32 = mybir.dt.float32
BF16 = mybir.dt.bfloat16
AF = mybir.ActivationFunctionType
ALU = mybir.AluOpType
AX = mybir.AxisListType

D = 1024
DIN = 3104
EPS = 1e-6
C1 = math.sqrt(2.0 / math.pi)
C2 = 0.044715

FULL_JOBS = [(64, 32, 0, 0), (16, 16, 8192, 4096)]


class Sched:
    ENGS = ("pe", "act", "dve", "pool", "sp")
    SYNC_LAT = 0.12
    ACT_SWITCH = 1.3
    PRIO = "rank"

    def __init__(self, nc):
        self.nc = nc
        self.all = []
        self.last_w = {}
        self.readers = {}
        self.total = 0

    def op(self, eng, fn, reads=(), writes=(), dma_key=None, cost=None, lat=None, tset=None, group=None):
        uid = len(self.all)
        self.total += 1
        deps = set()
        for b in reads:
            d = self.last_w.get(b)
            if d is not None:
                deps.add(d)
        for b in writes:
            d = self.last_w.get(b)
            if d is not None:
                deps.add(d)
            deps.update(self.readers.get(b, ()))
        deps.discard(uid)
        for b in reads:
            self.readers.setdefault(b, []).append(uid)
        for b in writes:
            self.last_w[b] = uid
            self.readers[b] = []
        if eng == "sp":
            assert dma_key is not None
        if dma_key is not None:
            cost = 0.1 if eng == "sp" else 1.0
            lat = 3.0
        if cost is None:
            cost, tset = self._estimate(eng, fn)
        if lat is None:
            lat = cost
        self.all.append(dict(eng=eng, fn=fn, deps=deps, dma_key=dma_key, cost=cost, lat=lat, tset=tset,
                             sig=False, cnt=None, group=group))

    class _Probe:
        def __getattr__(self, name):
            def rec(*a, **k):
                self.call = (name, a, k)
                return None
            return rec

    def _estimate(self, eng, fn):
        pr = Sched._Probe()
        fn(pr)
        name, a, k = pr.call
        out = k.get("out", a[0] if a else None)
        cols = 1
        for d in tuple(out.shape)[1:]:
            cols *= int(d)
        tset = None
        if eng == "pe":
            return max(0.066, cols / 2170.0), None
        if eng == "act":
            f = k.get("func")
            if f in (AF.Exp, AF.Ln):
                tset = 6
            elif f == AF.Silu:
                tset = 18
            elif f == AF.Gelu_apprx_tanh:
                tset = 11
            return 0.22 + cols * 0.00085, tset
        if eng == "dve":
            return 0.12 + cols * 0.00105, None
        if eng == "sp":
            return 0.1, None
        return 0.3 + cols * 0.002, None

    def schedule(self):
        import heapq
        ops = self.all
        n = len(ops)
        ndeps = [len(o["deps"]) for o in ops]
        users = [[] for _ in range(n)]
        for u, o in enumerate(ops):
            for d in o["deps"]:
                users[d].append(u)
        ready_t = [0.0] * n
        fin = [0.0] * n
        future = {e: [] for e in self.ENGS}
        avail = {e: [] for e in self.ENGS}
        free_t = {e: 0.0 for e in self.ENGS}
        order = {e: [] for e in self.ENGS}
        cur_set = [None]
        prio = list(range(n))
        if self.PRIO == "rank":
            rank = [0.0] * n
            for u in range(n - 1, -1, -1):
                m = 0.0
                for v in users[u]:
                    if rank[v] > m:
                        m = rank[v]
                rank[u] = ops[u]["cost"] + m
            top = max(rank)
            prio = [(top - rank[u]) for u in range(n)]
        for u, o in enumerate(ops):
            if ndeps[u] == 0:
                heapq.heappush(future[o["eng"]], (0.0, u))
        groups = {}
        for u, o in enumerate(ops):
            if o["group"] is not None:
                groups.setdefault(o["group"], []).append(u)
        scheduled = [False] * n
        done = 0

        def commit(u, e, t):
            o = ops[u]
            c = o["cost"]
            if e == "act" and o["tset"] is not None and cur_set[0] != o["tset"]:
                c += self.ACT_SWITCH
                cur_set[0] = o["tset"]
            start = max(t, free_t[e], ready_t[u])
            free_t[e] = start + c
            fin[u] = start + (o["lat"] if o["dma_key"] is not None else c)
            order[e].append(u)
            scheduled[u] = True
            for v in users[u]:
                ov = ops[v]
                rt = fin[u] + (0.0 if ov["eng"] == e else self.SYNC_LAT)
                if rt > ready_t[v]:
                    ready_t[v] = rt
                ndeps[v] -= 1
                if ndeps[v] == 0:
                    heapq.heappush(future[ov["eng"]], (ready_t[v], v))

        while done < n:
            best = None
            for e in self.ENGS:
                fu, av = future[e], avail[e]
                while fu and (scheduled[fu[0][1]] or fu[0][0] <= free_t[e]):
                    v_ = heapq.heappop(fu)[1]
                    if not scheduled[v_]:
                        heapq.heappush(av, (prio[v_], v_))
                while av and scheduled[av[0][1]]:
                    heapq.heappop(av)
                if av:
                    cand = (free_t[e], av[0][1], e)
                elif fu:
                    cand = (fu[0][0], fu[0][1], e)
                else:
                    continue
                if best is None or cand < best:
                    best = cand
            t, u, e = best
            if avail[e] and avail[e][0][1] == u:
                heapq.heappop(avail[e])
            else:
                heapq.heappop(future[e])
            commit(u, e, t)
            done += 1
            g_ = ops[u]["group"]
            if g_ is not None:
                for v in groups[g_]:
                    if not scheduled[v]:
                        assert ndeps[v] == 0 and ops[v]["eng"] == e, "group members must share engine and inputs"
                        commit(v, e, free_t[e])
                        done += 1
        self.order = order
        self.est_us = max(free_t.values())

    def emit(self, stack):
        nc = self.nc
        self.schedule()
        ops = self.all
        pos = {}
        for e in self.ENGS:
            for i, u in enumerate(self.order[e]):
                pos[u] = i
        for u, o in enumerate(ops):
            sd = {}
            for d in o["deps"]:
                de = ops[d]["eng"]
                if de == "pe" and o["eng"] == "pe":
                    continue
                key = ("dma", ops[d]["dma_key"]) if ops[d]["dma_key"] is not None else (de, None)
                if key not in sd or pos[sd[key]] < pos[d]:
                    sd[key] = d
            o["sdeps"] = sd
            for d in sd.values():
                ops[d]["sig"] = True
        sems = {}
        for e in ("pe", "act", "dve", "pool"):
            sems[e] = stack.enter_context(nc.semaphore("s_" + e))
            c = 0
            for u in self.order[e]:
                if ops[u]["sig"] and ops[u]["dma_key"] is None:
                    c += 1
                    ops[u]["cnt"] = c
        dsem, dcount, dkey_eng = {}, {}, {}
        for e in self.ENGS:
            for u in self.order[e]:
                o = ops[u]
                k = o["dma_key"]
                if k is None:
                    continue
                assert dkey_eng.setdefault(k, e) == e, "a DMA key must stay on one queue"
                if k not in dsem:
                    dsem[k] = stack.enter_context(nc.semaphore("d_" + str(k)))
                    dcount[k] = 0
                dcount[k] += 16
                o["cnt"] = dcount[k]
                o["sem"] = dsem[k]
        block = stack.enter_context(nc.Block())
        engmap = {"pe": block.tensor, "act": block.scalar, "dve": block.vector,
                  "pool": block.gpsimd, "sp": block.sync}
        final_waits = [(dsem[k], dcount[k]) for k in dsem]
        for e in self.ENGS:
            lst = self.order[e]

            def body(eng, lst=lst, e=e):
                waited = {}
                for u in lst:
                    o = ops[u]
                    for (key, d) in o["sdeps"].items():
                        src = ops[d]
                        if waited.get(key, 0) >= src["cnt"]:
                            continue
                        waited[key] = src["cnt"]
                        eng.wait_ge(src["sem"] if key[0] == "dma" else sems[key[0]], src["cnt"])
                    ins = o["fn"](eng)
                    if o["dma_key"] is not None:
                        ins.then_inc(o["sem"], 16)
                    elif o["sig"]:
                        ins.then_inc(sems[e], 1)
                if e == "sp":
                    for (s_, c_) in final_waits:
                        eng.wait_ge(s_, c_)
            engmap[e](body)


def build_nc(jobs):
    n_x_rows = max(j[2] + j[0] * 128 for j in jobs)
    n_o_rows = max(j[3] + j[1] * 128 for j in jobs)
    max_own = max(j[1] for j in jobs)
    nc = bass.Bass("TRN2", target_bir_lowering=False)
    x_d = nc.dram_tensor("x", [n_x_rows, D], F32, kind="ExternalInput").ap()
    y_d = nc.dram_tensor("y", [n_o_rows, D], F32, kind="ExternalOutput").ap()
    win_d = nc.dram_tensor("w_in", [D, DIN], F32, kind="ExternalInput").ap()
    wout_d = nc.dram_tensor("w_out", [D, D], F32, kind="ExternalInput").ap()
    gpre_d = nc.dram_tensor("gpre", [128, 8], F32, kind="ExternalInput").ap()
    wspT_d = nc.dram_tensor("wspT", [128, 4 * 128], F32, kind="ExternalInput").ap()
    bsp_d = nc.dram_tensor("bsp", [128, 4], F32, kind="ExternalInput").ap()
    gv_d = nc.dram_tensor("gv", [512], F32, kind="ExternalInput").ap()
    wgk_d = nc.dram_tensor("wgk", [33, 512], F32, kind="ExternalInput").ap()
    gnb_d = nc.dram_tensor("gnb", [512], F32, kind="ExternalInput").ap()
    gpost_d = nc.dram_tensor("gpost", [D], F32, kind="ExternalInput").ap()

    with ExitStack() as st:
        def T(name, shape, dt=F32):
            return st.enter_context(nc.sbuf_tensor(name, shape, dt))

        S = Sched(nc)
        ident = T("ident", [128, 128], BF16)
        maskF = T("maskF", [128, 512], BF16)
        maskB = T("maskB", [128, 512], BF16)
        triF = T("triF", [128, 128], BF16)
        triB = T("triB", [128, 128], BF16)
        negcol = T("negcol", [128, 1], BF16)
        Wb = T("Wb", [128, 8, DIN], BF16)
        Wo = T("Wo", [128, 8, D], BF16)
        yt = T("yt", [128, D], F32)
        NB = 2
        UV = [T("UV%d" % i, [128, D], F32) for i in range(NB)]
        ZZ = [T("ZZ%d" % i, [128, D], F32) for i in range(NB)]
        gpre = T("gpre_s", [128, 8], F32)
        wspT = T("wspT_s", [128, 512], BF16)
        bsp = T("bsp_s", [128, 4], F32)
        gvb = T("gvb", [128, 512], F32)
        wgk = T("wgk_s", [33, 512], BF16)
        gnb = T("gnb_s", [128, 512], F32)
        gpost = T("gpost_s", [128, D], F32)
        lrT = T("lrT", [33, 128], BF16)

        def P(e, f, r=(), w=(), c=None, ts=None, g=None):
            S.op(e, f, reads=r, writes=w, cost=c, tset=ts, group=g)

        P("pool", lambda e: e.memset(ident[:], 1.0), w=["ident"])
        P("pool", lambda e: e.affine_select(out=ident[:], in_=ident[:], pattern=[[-1, 128]], compare_op=ALU.is_equal,
                                            fill=0.0, base=0, channel_multiplier=1), r=["ident"], w=["ident"])
        P("pool", lambda e: e.memset(maskF[:], 1.0), w=["maskF"])
        P("pool", lambda e: e.affine_select(out=maskF[:].rearrange("p (h c) -> p h c", h=4), in_=maskF[:].rearrange("p (h c) -> p h c", h=4),
                                            pattern=[[0, 4], [1, 128]], compare_op=ALU.is_ge, fill=0.0, base=0,
                                            channel_multiplier=-1), r=["maskF"], w=["maskF"])
        P("pool", lambda e: e.memset(maskB[:], 1.0), w=["maskB"])
        P("pool", lambda e: e.affine_select(out=maskB[:].rearrange("p (h c) -> p h c", h=4), in_=maskB[:].rearrange("p (h c) -> p h c", h=4),
                                            pattern=[[0, 4], [-1, 128]], compare_op=ALU.is_gt, fill=0.0, base=0,
                                            channel_multiplier=1), r=["maskB"], w=["maskB"])
        P("pool", lambda e: e.memset(triF[:], -1.0 / 16), w=["triF"])
        P("pool", lambda e: e.affine_select(out=triF[:], in_=triF[:], pattern=[[1, 128]], compare_op=ALU.is_ge, fill=0.0,
                                            base=0, channel_multiplier=-1), r=["triF"], w=["triF"])
        P("pool", lambda e: e.memset(triB[:], -1.0 / 16), w=["triB"])
        P("pool", lambda e: e.affine_select(out=triB[:], in_=triB[:], pattern=[[-1, 128]], compare_op=ALU.is_ge, fill=0.0,
                                            base=0, channel_multiplier=1), r=["triB"], w=["triB"])
        P("pool", lambda e: e.memset(negcol[:], -1.0 / 16), w=["negcol"])
        P("pool", lambda e: e.memset(lrT[32:33, :], 1.0), w=["lrT_one"])

        S.op("sp", lambda e: e.dma_start(out=gpre[:], in_=gpre_d[:, :]), writes=["gpre"], dma_key="gpre")
        NC = 25
        vbc = T("vbc", [128, NC, 512], BF16)
        stg = [(UV[0], ["U0", "V0"], "stg0"), (ZZ[0], ["ZA0", "ZB0"], "stg1"),
               (UV[1], ["U1", "V1"], "stg2"), (ZZ[1], ["ZA1", "ZB1"], "stg3")]
        for j in range(4):
            view = vbc[:, 4 * j:4 * j + 4, :].bitcast(F32).rearrange("p a c -> p (a c)")
            stg.append((view, ["vbc%d" % (4 * j + i_) for i_ in range(4)], "stg%d" % (4 + j)))
        NSTG = [len(stg)]
        stg_n = [0]

        def wcol_ids(c0, c1):
            ids = []
            for (a, b_, nm) in ((0, 1792, "WbA"), (1792, 2592, "WbB"), (2592, 3104, "WbC")):
                if c0 < b_ and c1 > a:
                    ids.append(nm)
            return ids

        DQ = ["sp"]

        def load_win_piece(k, c0, c1):
            buf, ids, key = stg[stg_n[0] % NSTG[0]]
            use_dve = (stg_n[0] % 2 == 0)
            stg_n[0] += 1
            q_ = DQ[0]
            key = key if q_ == "sp" else "g" + key
            S.op(q_, lambda e: e.dma_start(out=buf[:, 0:c1 - c0], in_=win_d[k * 128:(k + 1) * 128, c0:c1]), writes=ids, dma_key=key)
            if use_dve:
                P("dve", lambda e: e.tensor_scalar(out=Wb[:, k, c0:c1], in0=buf[:, 0:c1 - c0], scalar1=gpre[:, k:k + 1], scalar2=None, op0=ALU.mult),
                  r=ids + ["gpre"], w=wcol_ids(c0, c1))
            else:
                P("act", lambda e: e.activation(out=Wb[:, k, c0:c1], in_=buf[:, 0:c1 - c0], func=AF.Identity, scale=gpre[:, k:k + 1]),
                  r=ids + ["gpre"], w=wcol_ids(c0, c1))

        def load_wout_piece(k):
            buf, ids, key = stg[stg_n[0] % NSTG[0]]
            use_dve = (stg_n[0] % 2 == 0)
            stg_n[0] += 1
            q_ = DQ[0]
            key = key if q_ == "sp" else "g" + key
            S.op(q_, lambda e: e.dma_start(out=buf[:, :], in_=wout_d[k * 128:(k + 1) * 128, :]), writes=ids, dma_key=key)
            if use_dve:
                P("dve", lambda e: e.tensor_copy(out=Wo[:, k, :], in_=buf[:, :]), r=ids, w=["Wo"])
            else:
                P("act", lambda e: e.activation(out=Wo[:, k, :], in_=buf[:, :], func=AF.Copy), r=ids, w=["Wo"])

        for k in range(8):
            load_win_piece(k, 1792, 2592)
        NSTG[0] = 4
        DQ[0] = "pool"
        deferred = []
        for (c0, c1) in ((0, 1024), (1024, 1792), (2592, 3104)):
            for k in range(8):
                deferred.append(lambda k=k, c0=c0, c1=c1: load_win_piece(k, c0, c1))
        for k in range(8):
            deferred.append(lambda k=k: load_wout_piece(k))
        S.op("sp", lambda e: e.dma_start(out=yt[0:33, 0:512], in_=wgk_d[:, :]), writes=["yt0", "yt1"], dma_key="ytst")
        P("dve", lambda e: e.tensor_copy(out=wgk[:], in_=yt[0:33, 0:512]), r=["yt0", "yt1"], w=["wgk"])
        S.op("sp", lambda e: e.dma_start(out=yt[:, 0:512], in_=wspT_d[:, :]), writes=["yt0", "yt1"], dma_key="ytst")
        P("dve", lambda e: e.tensor_copy(out=wspT[:], in_=yt[:, 0:512]), r=["yt0", "yt1"], w=["wspT"])
        S.op("sp", lambda e: e.dma_start(out=bsp[:], in_=bsp_d[:, :]), writes=["bsp"], dma_key="bsp")
        S.op("sp", lambda e: e.dma_start(out=gvb[:], in_=gv_d.partition_broadcast(128)), writes=["gvb"], dma_key="gvb")
        S.op("sp", lambda e: e.dma_start(out=gnb[:], in_=gnb_d.partition_broadcast(128)), writes=["gnb"], dma_key="gnb")
        S.op("sp", lambda e: e.dma_start(out=gpost[:], in_=gpost_d.partition_broadcast(128)), writes=["gpost"], dma_key="gpost")

        NXS = 2
        xs = [T("xs%d" % i, [128, D], F32) for i in range(NXS)]
        xbf = [T("xbf%d" % i, [128, D], BF16) for i in range(2)]
        xT = [T("xT%d" % i, [128, 8, 128], BF16) for i in range(2)]
        junk = T("junk", [128, D], BF16)
        junky = T("junky", [128, D], BF16)
        sm = [T("sm%d" % i, [128, 32], F32) for i in range(4)]

        def slots(name, shape, dt=F32, n=NB):
            return [T("%s%d" % (name, i), shape, dt) for i in range(n)]
        U_ = [UV[i][:, 0:512] for i in range(NB)]
        V_ = [UV[i][:, 512:1024] for i in range(NB)]
        ZA = [ZZ[i][:, 0:512] for i in range(NB)]
        ZB = [ZZ[i][:, 512:1024] for i in range(NB)]
        def vbsrc(job, n, b):
            if 1 <= n < job[1] - 1 and n - 1 < NC:
                return (lambda lo, hi: vbc[:, n - 1, lo:hi]), "vbc%d" % (n - 1), True
            return (lambda lo, hi: vb[b][:, lo:hi]), "vb%d" % b, False
        TS = T("TS", [128, 512], F32)
        ON = T("ON", [128, 512], F32)
        qk = slots("qk", [128, 512])
        scr = T("scr", [128, 512], F32)
        scr2 = T("scr2", [128, 512], F32)
        Ep = T("Ep", [128, 512], F32)
        Em = T("Em", [128, 512], F32)
        vlnb = slots("vlnb", [128, 512], BF16)
        vb = slots("vb", [128, 512], BF16)
        lrs = slots("lrs", [128, 32], BF16)
        spb = slots("spb", [128, 512], BF16)
        qin = slots("qin", [128, 1024], BF16)
        kin = slots("kin", [128, 512], BF16)
        qT = slots("qT", [128, 8, 128], BF16)
        kT = slots("kT", [128, 4, 128], BF16)
        scf = slots("scf", [128, 512], BF16)
        scb = slots("scb", [128, 512], BF16)
        cat = T("cat", [128, D], BF16)
        catT = T("catT", [128, 8, 128], BF16)
        dec = [T("dec%d" % i, [128, 2], F32) for i in range(3)]
        Tst = T("Tst", [128, 512], F32)
        Sf = [T("Sf%d" % i, [128, 512], BF16) for i in range(2)]
        Sb = T("Sb", [128, max_own, 256], BF16)

        NPB = 6
        pb = [st.enter_context(nc.psum_tensor("pb%d" % i, [128, 512], F32)) for i in range(NPB)]
        pt = [st.enter_context(nc.psum_tensor("pt%d" % i, [128, 1024], BF16)) for i in range(2)]
        cnt = dict(bank=0, pth=0, tile=0, dec=0)
        for i in range(NB):
            P("pool", lambda e, i=i: e.memset(qin[i][:], 0.0), w=["qin%d_0" % i, "qin%d_1" % i])

        def bank():
            i = cnt["bank"] % NPB
            cnt["bank"] += 1
            return pb[i], "pb%d" % i

        def ptbank():
            i = cnt["pth"] % 2
            cnt["pth"] += 1
            return pt[i], "pt%d" % i

        def bankA():
            i = cnt["bank"] % 2
            cnt["bank"] += 1
            return pb[i], "pb%d" % i

        def bankB():
            i = 2 + cnt["bankb"] % 2
            cnt["bankb"] += 1
            return pb[i], "pb%d" % i
        cnt["bankb"] = 0

        def load_x(row0, g, orow=None):
            s3 = g % NXS
            S.op("sp", lambda e: e.dma_start(out=xs[s3][:], in_=x_d[row0:row0 + 128, :]), writes=["xs%d" % s3],
                 dma_key="xs%d" % s3)
            if orow is not None:
                S.op("pool", lambda e: e.dma_start(out=y_d[orow:orow + 128, :], in_=x_d[row0:row0 + 128, :]),
                     writes=["yrow%d" % orow, "ycpk%d" % (g % 4)], dma_key="ycp%d" % (g % 4))

        rstdc = T("rstdc", [128, 64], F32)
        lrc = T("lrc", [128, max_own, 32], BF16)

        def lrsrc(job, n, b):
            if 1 <= n < job[1]:
                return lrc[:, n, :], "lrc%d" % n, True
            return lrs[b][:], "lrs%d" % b, False

        def front1(g):
            s3, s2, ss = g % NXS, g % 2, g % 4
            kind, job, n, _ = tiles[g]
            own = (1 <= n < job[1])
            if kind == "p2" and own:
                rs, rsid = rstdc[:, n:n + 1], "rstdc%d" % n
            else:
                rs, rsid = (rstdc[:, n:n + 1], "rstdc%d" % n) if own else (sm[ss][:, 2:3], "sm%dc" % ss)
                P("act", lambda e: e.activation(out=junk[:], in_=xs[s3][:], func=AF.Square, accum_out=sm[ss][:, 0:1]),
                  r=["xs%d" % s3], w=["sm%da" % ss, "junk"])
                P("act", lambda e: e.activation(out=sm[ss][:, 1:2], in_=sm[ss][:, 0:1], func=AF.Ln, scale=1.0 / D, bias=EPS),
                  r=["sm%da" % ss], w=["sm%db" % ss])
                P("act", lambda e: e.activation(out=rs, in_=sm[ss][:, 1:2], func=AF.Exp, scale=-0.5),
                  r=["sm%db" % ss], w=[rsid])
            P("dve", lambda e: e.tensor_scalar(out=xbf[s2][:], in0=xs[s3][:], scalar1=rs, scalar2=None, op0=ALU.mult),
              r=["xs%d" % s3, rsid], w=["xbf%d" % s2])

        def front2(g):
            s2 = g % 2
            ph, pid = ptbank()
            for k in range(8):
                P("pe", lambda e, ph=ph, k=k: e.transpose(out=ph[:, k * 128:(k + 1) * 128],
                                                          in_=xbf[s2][:, k * 128:(k + 1) * 128], identity=ident[:]),
                  r=["xbf%d" % s2, "ident"], w=[pid])
            P("dve", lambda e, ph=ph: e.tensor_copy(out=xT[s2][:], in_=ph[:, :].rearrange("p (k t) -> p k t", k=8)),
              r=[pid], w=["xT%d" % s2])

        def genF(g):
            for _ in range(3):
                yield
            front1(g)
            yield
            yield
            front2(g)
            yield

        def inproj(s2, c0, c1):
            bk, bid = bankA()
            wids = wcol_ids(c0, c1)
            for k in range(8):
                P("pe", lambda e, k=k, bk=bk: e.matmul(bk[:, 0:c1 - c0], lhsT=xT[s2][:, k, :], rhs=Wb[:, k, c0:c1],
                                                       start=(k == 0), stop=(k == 7)),
                  r=["xT%d" % s2] + wids, w=[bid])
            return bk, bid

        def gate_chain(b, ncols, c_lo, lr_ap, lr_id):
            ph, pid = ptbank()
            P("pe", lambda e: e.transpose(out=ph[0:32, 0:128], in_=lr_ap, identity=ident[:]),
              r=[lr_id, "ident"], w=[pid])
            P("dve", lambda e: e.tensor_copy(out=lrT[0:32, :], in_=ph[0:32, 0:128]), r=[pid], w=["lrT"])
            bk, bid = bankB()
            P("pe", lambda e: e.matmul(bk[:, 0:ncols], lhsT=lrT[0:33, :], rhs=wgk[0:33, c_lo:c_lo + ncols], start=True, stop=True),
              r=["lrT", "lrT_one", "wgk"], w=[bid])
            P("act", lambda e: e.activation(out=scr2[:, 0:ncols], in_=bk[:, 0:ncols], func=AF.Exp, scale=-1.0),
              r=[bid], w=["scr2"])
            P("act", lambda e: e.activation(out=spb[b][:, 0:ncols], in_=scr2[:, 0:ncols], func=AF.Ln, bias=1.0),
              r=["scr2"], w=["spb%d" % b])

        def p1_A(job, n, first, g):
            b, s2 = g % NB, g % 2
            bk, bid = inproj(s2, 1792, 2080)
            lr_ap, lr_id, _ = lrsrc(job, n, b)
            P("act", lambda e: e.activation(out=lr_ap, in_=bk[:, 256:288], func=AF.Copy), r=[bid], w=[lr_id])
            P("act", lambda e: e.activation(out=qk[b][:, 256:512], in_=bk[:, 0:256], func=AF.Copy), r=[bid], w=["qk%d" % b])
            yield
            bk2, bid2 = inproj(s2, 2080, 2592)
            vget, vid, _ = vbsrc(job, n, b)
            P("dve", lambda e: e.tensor_copy(out=vget(0, 512), in_=bk2[:, :]), r=[bid2], w=[vid])
            yield

        def p1_B(job, n, first, g):
            ntot, nown, xr0, _ = job
            b = g % NB
            lr_ap, lr_id, _ = lrsrc(job, n, b)
            gate_chain(b, 256, 256, lr_ap, lr_id)
            yield
            bc, bcid = bankB()
            P("pe", lambda e: e.matmul(bc[:, 0:256], lhsT=triB[:], rhs=spb[b][:, 0:256], start=True, stop=True),
              r=["triB", "spb%d" % b], w=[bcid])
            bl, blid = bankB()
            for pr in range(2):
                P("pe", lambda e, pr=pr: e.matmul(bl[:, pr:pr + 1], lhsT=spb[b][:, pr * 128:(pr + 1) * 128], rhs=negcol[:, 0:1],
                                                  start=True, stop=True), r=["negcol", "spb%d" % b], w=[blid])
            P("act", lambda e: e.activation(out=Em[:, 0:256], in_=bc[:, 0:256], func=AF.Exp, scale=-1.0), r=[bcid], w=["Em"])
            dcur = cnt["dec"] % 3
            dprev = (cnt["dec"] - 1) % 3
            cnt["dec"] += 1
            P("act", lambda e: e.activation(out=dec[dcur][:, 0:2], in_=bl[:, 0:2], func=AF.Exp), r=[blid], w=["dec%d" % dcur])
            P("dve", lambda e: e.tensor_tensor(out=kin[b][:, 0:256], in0=Em[:, 0:256], in1=qk[b][:, 256:512], op=ALU.mult),
              r=["Em", "qk%d" % b], w=["kin%d" % b])
            yield
            kv, kvid = bankB()
            vget, vid, _ = vbsrc(job, n, b)
            for pr in range(2):
                P("pe", lambda e, pr=pr: e.matmul(kv[:, pr * 256:(pr + 1) * 256], lhsT=kin[b][:, pr * 128:(pr + 1) * 128],
                                                  rhs=vget(pr * 256, (pr + 1) * 256), start=True, stop=True),
                  r=["kin%d" % b, vid], w=[kvid])
            if first:
                P("dve", lambda e: e.tensor_copy(out=Tst[:], in_=kv[:, :]), r=[kvid], w=["Tst0", "Tst1"])
            else:
                for pr in range(2):
                    P("dve", lambda e, pr=pr: e.scalar_tensor_tensor(out=Tst[:, pr * 256:(pr + 1) * 256], in0=Tst[:, pr * 256:(pr + 1) * 256],
                                                                     scalar=dec[dprev][:, pr:pr + 1], in1=kv[:, pr * 256:(pr + 1) * 256],
                                                                     op0=ALU.mult, op1=ALU.add),
                      r=[kvid, "Tst%d" % pr, "dec%d" % dprev], w=["Tst%d" % pr])
            if 1 <= n <= nown:
                for hh in range(2):
                    rs = slice(hh * 64, (hh + 1) * 64)
                    P("dve", lambda e, hh=hh, rs=rs: e.tensor_tensor(
                        out=Sb[rs, n - 1, :].rearrange("p (r c) -> p r c", r=2),
                        in0=Tst[rs, :].rearrange("p (r x) -> p r x", r=2)[:, :, hh * 128:(hh + 1) * 128],
                        in1=dec[dcur][rs, 0:2].unsqueeze(2).to_broadcast([64, 2, 128]), op=ALU.mult),
                      r=["Tst0", "Tst1", "dec%d" % dcur], w=["Sb%d_0%d" % (n - 1, hh), "Sb%d_1%d" % (n - 1, hh)])
            yield

        def p2_A(job, n, first, g):
            b, s2, ss = g % NB, g % 2, g % 4
            if not lrsrc(job, n, b)[2]:
                bk, bid = inproj(s2, 2048, 2080)
                P("act", lambda e, bk=bk: e.activation(out=lrs[b][:], in_=bk[:, 0:32], func=AF.Copy), r=[bid], w=["lrs%d" % b])
                yield
            bk, bid = inproj(s2, 1536, 2048)
            P("act", lambda e, bk=bk: e.activation(out=qk[b][:], in_=bk[:, :], func=AF.Copy), r=[bid], w=["qk%d" % b])
            yield
            if not vbsrc(job, n, b)[2]:
                bk, bid = inproj(s2, 2080, 2592)
                P("dve", lambda e, bk=bk: e.tensor_copy(out=vb[b][:], in_=bk[:, :]), r=[bid], w=["vb%d" % b])
                yield
            bk, bid = inproj(s2, 1024, 1536)
            P("act", lambda e, bk=bk: e.activation(out=ZA[b][:], in_=bk[:, :], func=AF.Copy), r=[bid], w=["ZA%d" % b])
            yield
            bk, bid = inproj(s2, 2592, 3104)
            P("dve", lambda e, bk=bk: e.tensor_copy(out=ZB[b][:], in_=bk[:, :]), r=[bid], w=["ZB%d" % b])
            yield
            bk, bid = inproj(s2, 0, 512)
            P("act", lambda e, bk=bk: e.activation(out=U_[b][:], in_=bk[:, :], func=AF.Copy), r=[bid], w=["U%d" % b])
            yield
            bk, bid = inproj(s2, 512, 1024)
            P("dve", lambda e, bk=bk: e.tensor_copy(out=V_[b][:], in_=bk[:, :]), r=[bid], w=["V%d" % b])
            yield
            yield
            yield

            zin, uin = ["ZA%d" % b, "ZB%d" % b], ["U%d" % b, "V%d" % b]
            P("act", lambda e: e.activation(out=ZA[b][:], in_=ZA[b][:], func=AF.Silu), r=zin, w=["ZA%d" % b], g="sil%d" % g)
            P("act", lambda e: e.activation(out=ZB[b][:], in_=ZB[b][:], func=AF.Silu), r=zin, w=["ZB%d" % b], g="sil%d" % g)
            P("act", lambda e: e.activation(out=U_[b][:], in_=U_[b][:], func=AF.Gelu_apprx_tanh), r=uin, w=["U%d" % b], g="gel%d" % g)
            P("act", lambda e: e.activation(out=V_[b][:], in_=V_[b][:], func=AF.Gelu_apprx_tanh), r=uin, w=["V%d" % b], g="gel%d" % g)
            P("dve", lambda e: e.bn_stats(out=sm[ss][:, 8:14], in_=V_[b][:]), r=["V%d" % b], w=["sm%dd" % ss])
            P("dve", lambda e: e.bn_aggr(out=sm[ss][:, 14:16], in_=sm[ss][:, 8:14]), r=["sm%dd" % ss], w=["sm%de" % ss])
            P("act", lambda e: e.activation(out=sm[ss][:, 16:17], in_=sm[ss][:, 15:16], func=AF.Ln, bias=EPS), r=["sm%de" % ss], w=["sm%df" % ss])
            P("act", lambda e: e.activation(out=sm[ss][:, 17:18], in_=sm[ss][:, 16:17], func=AF.Exp, scale=-0.5), r=["sm%df" % ss], w=["sm%dg" % ss])
            P("dve", lambda e: e.tensor_scalar(out=V_[b][:], in0=V_[b][:], scalar1=sm[ss][:, 14:15], scalar2=sm[ss][:, 17:18],
                                               op0=ALU.subtract, op1=ALU.mult), r=["V%d" % b, "sm%de" % ss, "sm%dg" % ss], w=["V%d" % b])
            P("dve", lambda e: e.tensor_tensor(out=vlnb[b][:], in0=V_[b][:], in1=gvb[:], op=ALU.mult),
              r=["V%d" % b, "gvb"], w=["vlnb%d" % b])
            P("dve", lambda e: e.tensor_tensor(out=U_[b][:], in0=U_[b][:], in1=ZA[b][:], op=ALU.mult),
              r=["U%d" % b, "ZA%d" % b], w=["U%d" % b])
            P("dve", lambda e: e.tensor_tensor(out=ZB[b][:], in0=ZB[b][:], in1=gnb[:], op=ALU.mult),
              r=["ZB%d" % b, "gnb"], w=["ZB%d" % b])
            yield

        def p2_B(job, n, first, g):
            ntot, nown, xr0, or0 = job
            b, s3, ss = g % NB, g % NXS, g % 4
            last_in_seq = (n == ntot - 1)
            lr_ap, lr_id, _ = lrsrc(job, n, b)
            gate_chain(b, 512, 0, lr_ap, lr_id)
            yield
            bc, bcid = bankB()
            P("pe", lambda e: e.matmul(bc[:, 0:256], lhsT=triF[:], rhs=spb[b][:, 0:256], start=True, stop=True),
              r=["triF", "spb%d" % b], w=[bcid])
            P("pe", lambda e: e.matmul(bc[:, 256:512], lhsT=triB[:], rhs=spb[b][:, 256:512], start=True, stop=True),
              r=["triB", "spb%d" % b], w=[bcid])
            bl, blid = bankB()
            for pr in range(2):
                P("pe", lambda e, pr=pr: e.matmul(bl[:, pr:pr + 1], lhsT=spb[b][:, pr * 128:(pr + 1) * 128], rhs=negcol[:, 0:1],
                                                  start=True, stop=True), r=["negcol", "spb%d" % b], w=[blid])
            P("act", lambda e: e.activation(out=Ep[:], in_=bc[:, :], func=AF.Exp, bias=math.log(0.125)), r=[bcid], w=["Ep"])
            P("act", lambda e: e.activation(out=Em[:], in_=bc[:, :], func=AF.Exp, scale=-1.0), r=[bcid], w=["Em"])
            dcur = cnt["dec"] % 3
            dprev = (cnt["dec"] - 1) % 3
            cnt["dec"] += 1
            P("act", lambda e: e.activation(out=dec[dcur][:, 0:2], in_=bl[:, 0:2], func=AF.Exp), r=[blid], w=["dec%d" % dcur])
            for hh in range(2):
                P("dve", lambda e, hh=hh: e.tensor_tensor(
                    out=qin[b][:].rearrange("p (t r h c) -> p t r h c", t=2, r=2, h=2)[:, :, :, hh, hh * 64:(hh + 1) * 64],
                    in0=Ep[:].rearrange("p (t r h d) -> p t r h d", t=2, r=2, h=2)[:, :, :, hh, :],
                    in1=qk[b][:, 0:256].rearrange("p (r h d) -> p r h d", r=2, h=2)[:, :, hh, :].unsqueeze(1).to_broadcast([128, 2, 2, 64]),
                    op=ALU.mult), r=["Ep", "qk%d" % b], w=["qin%d_%d" % (b, hh)])
            P("dve", lambda e: e.tensor_tensor(out=kin[b][:].rearrange("p (t d) -> p t d", t=2), in0=Em[:].rearrange("p (t d) -> p t d", t=2),
                                               in1=qk[b][:, 256:512].unsqueeze(1).to_broadcast([128, 2, 256]), op=ALU.mult),
              r=["Em", "qk%d" % b], w=["kin%d" % b])
            sv, svid = bankB()
            for h in range(4):
                P("pe", lambda e, h=h: e.matmul(sv[:, h * 128:(h + 1) * 128], lhsT=wspT[:, h * 128:(h + 1) * 128],
                                                rhs=vlnb[b][:, h * 128:(h + 1) * 128], start=True, stop=True),
                  r=["wspT", "vlnb%d" % b], w=[svid])
            P("dve", lambda e: e.tensor_tensor(out=TS[:].rearrange("p (h c) -> p h c", h=4), in0=sv[:, :].rearrange("p (h c) -> p h c", h=4),
                                               in1=bsp[:, 0:4].unsqueeze(2).to_broadcast([128, 4, 128]), op=ALU.add),
              r=[svid, "bsp"], w=["TS"])
            P("dve", lambda e: e.tensor_tensor(out=cat[:, 0:512], in0=TS[:], in1=U_[b][:], op=ALU.mult),
              r=["TS", "U%d" % b], w=["cata"])
            yield
            ph, pid = ptbank()
            for k in range(8):
                P("pe", lambda e, ph=ph, k=k: e.transpose(out=ph[:, k * 128:(k + 1) * 128], in_=qin[b][:, k * 128:(k + 1) * 128],
                                                          identity=ident[:]), r=["qin%d_0" % b, "qin%d_1" % b, "ident"], w=[pid])
            P("act", lambda e, ph=ph: e.activation(out=qT[b][:], in_=ph[:, :].rearrange("p (k t) -> p k t", k=8), func=AF.Copy),
              r=[pid], w=["qT%d" % b])
            ph2, pid2 = ptbank()
            for k in range(4):
                P("pe", lambda e, k=k: e.transpose(out=ph2[:, k * 128:(k + 1) * 128], in_=kin[b][:, k * 128:(k + 1) * 128],
                                                   identity=ident[:]), r=["kin%d" % b, "ident"], w=[pid2])
            P("dve", lambda e: e.tensor_copy(out=kT[b][:], in_=ph2[:, 0:512].rearrange("p (k t) -> p k t", k=4)), r=[pid2], w=["kT%d" % b])
            yield
            for (t, dstb, dstid, mk, mkid) in ((0, scf[b], "scf%d" % b, maskF, "maskF"), (1, scb[b], "scb%d" % b, maskB, "maskB")):
                sc, scid = bankB()
                for h in range(4):
                    P("pe", lambda e, h=h, sc=sc, t=t: e.matmul(sc[:, h * 128:(h + 1) * 128], lhsT=kT[b][:, t * 2 + h // 2, :],
                                                                rhs=qT[b][:, t * 4 + h, :], start=True, stop=True),
                      r=["qT%d" % b, "kT%d" % b], w=[scid])
                P("dve", lambda e, sc=sc, dstb=dstb, mk=mk: e.tensor_tensor(out=dstb[:], in0=sc[:, :], in1=mk[:], op=ALU.mult),
                  r=[scid, mkid], w=[dstid])
            kv, kvid = bankB()
            vget, vid, _ = vbsrc(job, n, b)
            for pr in range(2):
                P("pe", lambda e, pr=pr: e.matmul(kv[:, pr * 256:(pr + 1) * 256], lhsT=kin[b][:, pr * 128:(pr + 1) * 128],
                                                  rhs=vget(pr * 256, (pr + 1) * 256), start=True, stop=True),
                  r=["kin%d" % b, vid], w=[kvid])
            yield
            sfc = n % 2
            ob, obid = bankB()
            for h in range(4):
                pr, hh = h // 2, h % 2
                oc = ob[:, h * 128:(h + 1) * 128]
                steps = ["scf", "scb"] + (["sf"] if n > 0 else []) + ([] if last_in_seq else ["sb"])
                for i, kind in enumerate(steps):
                    f_, l_ = (i == 0), (i == len(steps) - 1)
                    if kind == "scf":
                        P("pe", lambda e, oc=oc, h=h, f_=f_, l_=l_: e.matmul(oc, lhsT=scf[b][:, h * 128:(h + 1) * 128],
                                                                             rhs=vget(h * 128, (h + 1) * 128), start=f_, stop=l_),
                          r=["scf%d" % b, vid], w=[obid])
                    elif kind == "scb":
                        P("pe", lambda e, oc=oc, h=h, f_=f_, l_=l_: e.matmul(oc, lhsT=scb[b][:, h * 128:(h + 1) * 128],
                                                                             rhs=vget(h * 128, (h + 1) * 128), start=f_, stop=l_),
                          r=["scb%d" % b, vid], w=[obid])
                    elif kind == "sf":
                        P("pe", lambda e, oc=oc, pr=pr, hh=hh, f_=f_, l_=l_: e.matmul(
                            oc, lhsT=qT[b][:, 2 * pr + hh, :],
                            rhs=Sf[sfc][:, pr * 256 + hh * 128:pr * 256 + (hh + 1) * 128], start=f_, stop=l_),
                          r=["qT%d" % b, "Sf%d" % sfc], w=[obid])
                    else:
                        P("pe", lambda e, oc=oc, pr=pr, hh=hh, f_=f_, l_=l_: e.matmul(
                            oc, lhsT=qT[b][:, 4 + 2 * pr + hh, :],
                            rhs=Sb[:, n, pr * 128:(pr + 1) * 128], start=f_, stop=l_),
                          r=["qT%d" % b, "Sb%d_%d0" % (n, pr), "Sb%d_%d1" % (n, pr)], w=[obid])
            if n == 0:
                P("dve", lambda e: e.tensor_copy(out=Tst[:], in_=kv[:, :]), r=[kvid], w=["Tst0", "Tst1"])
            else:
                for pr in range(2):
                    P("dve", lambda e, pr=pr: e.scalar_tensor_tensor(out=Tst[:, pr * 256:(pr + 1) * 256], in0=Tst[:, pr * 256:(pr + 1) * 256],
                                                                     scalar=dec[dprev][:, pr:pr + 1], in1=kv[:, pr * 256:(pr + 1) * 256],
                                                                     op0=ALU.mult, op1=ALU.add),
                      r=[kvid, "Tst%d" % pr, "dec%d" % dprev], w=["Tst%d" % pr])
            if n + 1 < nown:
                P("dve", lambda e: e.tensor_tensor(out=Sf[1 - sfc][:].rearrange("p (r c) -> p r c", r=2),
                                                   in0=Tst[:].rearrange("p (r c) -> p r c", r=2),
                                                   in1=dec[dcur][:, 0:2].unsqueeze(2).to_broadcast([128, 2, 256]), op=ALU.mult),
                  r=["Tst0", "Tst1", "dec%d" % dcur], w=["Sf%d" % (1 - sfc)])
            P("act", lambda e: e.activation(out=scr[:], in_=ob[:, :], func=AF.Square), r=[obid], w=["scr"])
            P("dve", lambda e: e.reduce_sum(out=sm[ss][:, 20:24], in_=scr[:].rearrange("p (h c) -> p h c", h=4), axis=AX.X),
              r=["scr"], w=["sm%dh" % ss])
            P("act", lambda e: e.activation(out=sm[ss][:, 24:28], in_=sm[ss][:, 20:24], func=AF.Ln, scale=1.0 / 128, bias=EPS),
              r=["sm%dh" % ss], w=["sm%di" % ss])
            P("act", lambda e: e.activation(out=sm[ss][:, 28:32], in_=sm[ss][:, 24:28], func=AF.Exp, scale=-0.5),
              r=["sm%di" % ss], w=["sm%dj" % ss])
            P("dve", lambda e: e.tensor_tensor(out=ON[:].rearrange("p (h c) -> p h c", h=4), in0=ob[:, :].rearrange("p (h c) -> p h c", h=4),
                                               in1=sm[ss][:, 28:32].unsqueeze(2).to_broadcast([128, 4, 128]), op=ALU.mult),
              r=[obid, "sm%dj" % ss], w=["ON"])
            P("dve", lambda e: e.tensor_tensor(out=cat[:, 512:1024], in0=ON[:], in1=ZB[b][:], op=ALU.mult),
              r=["ON", "ZB%d" % b], w=["catb"])
            yield
            yield
            ph, pid = ptbank()
            for k in range(8):
                P("pe", lambda e, ph=ph, k=k: e.transpose(out=ph[:, k * 128:(k + 1) * 128], in_=cat[:, k * 128:(k + 1) * 128],
                                                          identity=ident[:]), r=["cata", "catb", "ident"], w=[pid])
            P("dve", lambda e, ph=ph: e.tensor_copy(out=catT[:], in_=ph[:, :].rearrange("p (k t) -> p k t", k=8)), r=[pid], w=["catT"])
            yield
            ybk = []
            for c in range(2):
                bk, bid = pb[4 + c], "pb%d" % (4 + c)
                for k in range(8):
                    P("pe", lambda e, k=k, bk=bk, c=c: e.matmul(bk[:, :], lhsT=catT[:, k, :], rhs=Wo[:, k, c * 512:(c + 1) * 512],
                                                                start=(k == 0), stop=(k == 7)),
                      r=["catT", "Wo"], w=[bid])
                ybk.append((bk, bid))
            pend[g] = (ybk, ss, s3, or0 + n * 128)
            yield

        pend = {}

        def p2_C(g):
            ybk, ss, s3, orow = pend.pop(g)
            for c, (bk, bid) in enumerate(ybk):
                P("act", lambda e, bk=bk, c=c: e.activation(out=junky[:, c * 512:(c + 1) * 512], in_=bk[:, :], func=AF.Square,
                                                            accum_out=sm[ss][:, 3 + c:4 + c]), r=[bid], w=["sm%dk%d" % (ss, c), "junky%d" % c])
            yield
            P("dve", lambda e: e.tensor_tensor(out=sm[ss][:, 5:6], in0=sm[ss][:, 3:4], in1=sm[ss][:, 4:5], op=ALU.add),
              r=["sm%dk0" % ss, "sm%dk1" % ss], w=["sm%dl" % ss])
            P("act", lambda e: e.activation(out=sm[ss][:, 6:7], in_=sm[ss][:, 5:6], func=AF.Ln, scale=1.0 / D, bias=EPS),
              r=["sm%dl" % ss], w=["sm%dm" % ss])
            P("act", lambda e: e.activation(out=sm[ss][:, 7:8], in_=sm[ss][:, 6:7], func=AF.Exp, scale=-0.5),
              r=["sm%dm" % ss], w=["sm%dn" % ss])
            yield
            for c, (bk, bid) in enumerate(ybk):
                P("dve", lambda e, bk=bk, c=c: e.scalar_tensor_tensor(out=yt[:, c * 512:(c + 1) * 512], in0=bk[:, :], scalar=sm[ss][:, 7:8],
                                                                      in1=gpost[:, c * 512:(c + 1) * 512], op0=ALU.mult, op1=ALU.mult),
                  r=[bid, "sm%dn" % ss, "gpost"], w=["yt%d" % c])
            S.op("pool", lambda e: e.dma_start(out=y_d[orow:orow + 128, :], in_=yt[:], accum_op=ALU.add),
                 reads=["yt0", "yt1", "yrow%d" % orow], writes=["yrow%d" % orow], dma_key="yst")
            yield

        tiles = []
        for job in jobs:
            ntot, nown, xr0, _ = job
            for i, n in enumerate(range(ntot - 1, 0, -1)):
                tiles.append(("p1", job, n, i == 0))
            for n in range(nown):
                tiles.append(("p2", job, n, False))
        first_p2 = next(i for i, t in enumerate(tiles) if t[0] == "p2")
        AHEAD = 2

        def xrow(t):
            return t[1][2] + t[2] * 128

        def genA(i):
            kind, job, n, first = tiles[i]
            return (p1_A if kind == "p1" else p2_A)(job, n, first, i)

        def genB(i):
            kind, job, n, first = tiles[i]
            return (p1_B if kind == "p1" else p2_B)(job, n, first, i)

        def run_interleaved(gens):
            gens = [g_ for g_ in gens if g_ is not None]
            while gens:
                for g_ in list(gens):
                    try:
                        next(g_)
                    except StopIteration:
                        gens.remove(g_)

        def orow_of(t):
            return (t[1][3] + t[2] * 128) if t[0] == "p2" else None

        for i in range(min(AHEAD, len(tiles))):
            load_x(xrow(tiles[i]), i, orow_of(tiles[i]))
        for i in range(min(2, len(tiles))):
            front1(i)
            front2(i)
        run_interleaved([genA(0)])
        def genC(i):
            return p2_C(i) if (i >= 0 and i in pend) else None

        for i in range(len(tiles) + 1):
            if i + AHEAD < len(tiles):
                load_x(xrow(tiles[i + AHEAD]), i + AHEAD, orow_of(tiles[i + AHEAD]))
            if i + 1 == first_p2:
                while deferred:
                    deferred.pop(0)()
            elif deferred and i < len(tiles) and tiles[i][0] == "p1":
                deferred.pop(0)()
            run_interleaved([genC(i - 1),
                             genB(i) if i < len(tiles) else None,
                             genA(i + 1) if i + 1 < len(tiles) else None,
                             genF(i + 2) if i + 2 < len(tiles) else None])
        S.emit(st)
        nc._mk_total_ops = S.total
    return nc


def _layout(flip, xsamp, xpr, norm_pre, w_in, w_sp, b_sp, g_v_a, w_gk_fwd, b_gk_fwd, w_gk_bwd, b_gk_bwd,
            g_norm_b, w_out, norm_post):
    w_in_c = w_in[0]
    lr_f, lr_b = w_in_c[:, 3072:3088], w_in_c[:, 3088:3104]
    wsp, bsp_ = w_sp[0], b_sp[0]
    gf, bf, gb, bb = w_gk_fwd[0], b_gk_fwd[0], w_gk_bwd[0], b_gk_bwd[0]
    if flip:
        xsamp = xsamp[::-1]
        xpr = xpr[::-1]
        lr_f, lr_b = lr_b, lr_f
        wsp = wsp[:, ::-1, ::-1]
        bsp_ = bsp_[:, ::-1]
        gf, bf, gb, bb = gb, bb, gf, bf
    w_in_c = np.concatenate([w_in_c[:, :2048], lr_f, lr_b, w_in_c[:, 2048:3072]], axis=1)
    wgk = np.zeros((33, 512), np.float32)
    wgk[0:16, 0:256] = gf
    wgk[16:32, 256:512] = gb
    wgk[32, 0:256] = bf
    wgk[32, 256:512] = bb
    return {
        "x": np.ascontiguousarray(np.concatenate([xsamp, xpr], axis=0), dtype=np.float32),
        "w_in": np.ascontiguousarray(w_in_c, dtype=np.float32),
        "w_out": np.ascontiguousarray(w_out[0], dtype=np.float32),
        "gpre": np.ascontiguousarray(norm_pre[0].reshape(8, 128).T, dtype=np.float32),
        "wspT": np.ascontiguousarray(np.transpose(wsp, (2, 0, 1)).reshape(128, 512), dtype=np.float32),
        "bsp": np.ascontiguousarray(bsp_.T, dtype=np.float32),
        "gv": np.ascontiguousarray(g_v_a[0], dtype=np.float32),
        "wgk": wgk,
        "gnb": np.ascontiguousarray(np.tile(g_norm_b[0], 4), dtype=np.float32),
        "gpost": np.ascontiguousarray(norm_post[0], dtype=np.float32),
    }


def kernel(**inputs):
    inputs = {k: np.asarray(v) for k, v in inputs.items()}
    nc = build_nc(FULL_JOBS)
    xp, xsm = inputs.pop("x_prompt"), inputs.pop("x_sample")
    in_maps = [_layout(c % 2 == 1, xsm[c // 2], xp[c], **inputs) for c in range(8)]
    res = run_bass_kernel_spmd(nc, in_maps, core_ids=list(range(8)))
    y_prompt = np.empty((8, 2048, D), np.float32)
    y_sample = np.empty((4, 8192, D), np.float32)
    for c in range(8):
        y = np.asarray(res.results[c]["y"])
        ys, yp = y[:4096], y[4096:6144]
        if c % 2 == 1:
            y_sample[c // 2, 4096:] = ys[::-1]
            y_prompt[c] = yp[::-1]
        else:
            y_sample[c // 2, :4096] = ys
            y_prompt[c] = yp
    return (y_prompt, y_sample)
```
